# Optimizing a Trainium2 kernel written in Bass

```python
import jax, jax.numpy as jnp
from jax import lax
import numpy as np

D_MODEL = 2048
BATCH = 2
SEQ = 8192
DEPTH = 1
DEC_BATCH = 16
DEC_SEQ = 32
PAST_LEN = 1024

CHUNK = 64
EPS = 1e-6
VA = 128
HA = (D_MODEL // 2) // VA
NOPE = 128
ROPE = 64
QKA = NOPE + ROPE
Q_LORA = 512
KV_LORA = 256
ROPE_BASE = 10000.0
SCALE_A = QKA ** -0.5
DHB = 128
HB = (D_MODEL // 2) // DHB
BAND_CHUNKS = 8
BAND_PAST = BAND_CHUNKS * CHUNK
BAND_KEYS = (BAND_CHUNKS + 1) * CHUNK
MAX_REL = 128
N_REL = 2 * MAX_REL + 1
SCALE_B = DHB ** -0.5
MIX_WIDTH = HA * VA + HB * DHB
IN_COLS = Q_LORA + KV_LORA + ROPE + 3 * HB * DHB
SPLITS = [Q_LORA, Q_LORA + KV_LORA, Q_LORA + KV_LORA + ROPE,
          Q_LORA + KV_LORA + ROPE + HB * DHB, Q_LORA + KV_LORA + ROPE + 2 * HB * DHB]
D_FF = 4 * D_MODEL
Q_BLOCK = 128
NEG = -1e30

kernel_name = 'hybrid_mla_chunkband_stream_step'


def rms_norm(x, g):
    xf = x.astype(jnp.float32)
    y = xf * lax.rsqrt(jnp.mean(xf * xf, axis=-1, keepdims=True) + EPS)
    return (y * g.astype(jnp.float32)).astype(x.dtype)


def rope_part(x, pos):
    inv = 1.0 / (ROPE_BASE ** (jnp.arange(0, ROPE, 2, dtype=jnp.float32) / ROPE))
    ang = pos.astype(jnp.float32)[:, None] * inv[None, :]
    c = jnp.cos(ang)[:, None, :].astype(x.dtype)
    s = jnp.sin(ang)[:, None, :].astype(x.dtype)
    x1, x2 = jnp.split(x[..., NOPE:], 2, axis=-1)
    return jnp.concatenate([x[..., :NOPE], x1 * c - x2 * s, x1 * s + x2 * c], axis=-1)


def mixer_inputs(hn, pos, w_in, g_cq, w_uq, g_ckv, g_qa, g_qb, g_kb):
    B, T, _ = hn.shape
    c_q, c_kv, kpe, qb, kb, vb = jnp.split(hn @ w_in, SPLITS, axis=-1)
    qa = (rms_norm(c_q, g_cq) @ w_uq).reshape(B, T, HA, QKA)
    qa = rope_part(rms_norm(qa, g_qa), pos)
    ckv = rms_norm(c_kv, g_ckv)
    qb = rms_norm(qb.reshape(B, T, HB, DHB), g_qb)
    kb = rms_norm(kb.reshape(B, T, HB, DHB), g_kb)
    vb = vb.reshape(B, T, HB, DHB)
    return qa, ckv, kpe, qb, kb, vb


def mla_keys_values(ckv, kpe, pos, w_uk, w_uv, g_ka):
    B, T, _ = ckv.shape
    k_nope = (ckv @ w_uk).reshape(B, T, HA, NOPE)
    k = jnp.concatenate([k_nope, jnp.broadcast_to(kpe[:, :, None, :], (B, T, HA, ROPE))], axis=-1)
    k = rope_part(rms_norm(k, g_ka), pos)
    v = (ckv @ w_uv).reshape(B, T, HA, VA)
    return k, v


def dense_attention(q, k, v, scale, bias):
    B, Q = q.shape[:2]
    s = jnp.einsum('bqhd,bkhd->bhqk', q, k, preferred_element_type=jnp.float32) * scale
    if bias is not None:
        s = s + bias.astype(jnp.float32)[None]
    p = jax.nn.softmax(s, axis=-1).astype(v.dtype)
    return jnp.einsum('bhqk,bkhd->bqhd', p, v).reshape(B, Q, -1)


def mla_prompt(q, k, v):
    B, S = q.shape[:2]
    nblk = S // Q_BLOCK
    qblocks = q.reshape(B, nblk, Q_BLOCK, HA, QKA).transpose(1, 0, 2, 3, 4)
    key_chunk = jnp.arange(S) // CHUNK

    def one_block(args):
        qblk, i = args
        q_chunk = (i * Q_BLOCK + jnp.arange(Q_BLOCK)) // CHUNK
        s = jnp.einsum('bqhd,bkhd->bhqk', qblk, k, preferred_element_type=jnp.float32) * SCALE_A
        s = jnp.where((key_chunk[None, :] <= q_chunk[:, None])[None, None], s, NEG)
        p = jax.nn.softmax(s, axis=-1).astype(v.dtype)
        return jnp.einsum('bhqk,bkhd->bqhd', p, v)

    o = lax.map(one_block, (qblocks, jnp.arange(nblk)))
    return o.transpose(1, 0, 2, 3, 4).reshape(B, S, HA * VA)


def band_prompt(q, k, v, rel_bias):
    B, S = q.shape[:2]
    NC = S // CHUNK
    qc = q.reshape(B, NC, CHUNK, HB, DHB)
    pad = ((0, 0), (BAND_CHUNKS, 0), (0, 0), (0, 0), (0, 0))
    kc = jnp.pad(k.reshape(B, NC, CHUNK, HB, DHB), pad)
    vc = jnp.pad(v.reshape(B, NC, CHUNK, HB, DHB), pad)
    kband = jnp.concatenate([kc[:, j:j + NC] for j in range(BAND_CHUNKS + 1)], axis=2)
    vband = jnp.concatenate([vc[:, j:j + NC] for j in range(BAND_CHUNKS + 1)], axis=2)
    src_chunk = jnp.arange(NC)[:, None] - BAND_CHUNKS + jnp.arange(BAND_CHUNKS + 1)[None, :]
    valid = jnp.repeat(src_chunk >= 0, CHUNK, axis=1)
    rel = jnp.arange(CHUNK)[:, None] - (jnp.arange(BAND_KEYS) - BAND_PAST)[None, :]
    bias = rel_bias[:, jnp.clip(rel, -MAX_REL, MAX_REL) + MAX_REL].astype(jnp.float32)
    s = jnp.einsum('bcqhd,bckhd->bchqk', qc, kband, preferred_element_type=jnp.float32) * SCALE_B
    s = jnp.where(valid[None, :, None, None, :], s + bias[None, None], NEG)
    p = jax.nn.softmax(s, axis=-1).astype(v.dtype)
    o = jnp.einsum('bchqk,bckhd->bcqhd', p, vband)
    return o.reshape(B, S, HB * DHB)


def merge_and_ffn(h, oa, ob, w_o, norm_ffn, w_up, w_down):
    h = h + jnp.concatenate([oa, ob], axis=-1) @ w_o
    u = rms_norm(h, norm_ffn) @ w_up
    return h + jnp.square(jax.nn.relu(u)) @ w_down


def setup_inputs(seed: int = 0) -> dict:
    key = jax.random.key(seed)
    ks = jax.random.split(key, 24)
    f32 = jnp.float32
    L_band = min(BAND_PAST, PAST_LEN)

    def w(k, shape, fan_in):
        return jax.random.normal(k, (DEPTH,) + shape, f32) * (fan_in ** -0.5)

    def gain(k, n):
        return 1.0 + 0.01 * jax.random.normal(k, (DEPTH, n), f32)

    return {
        'x_prompt': jax.random.normal(ks[0], (BATCH, SEQ, D_MODEL), f32),
        'x_sample': jax.random.normal(ks[1], (DEC_BATCH, DEC_SEQ, D_MODEL), f32),
        'cache_mla_ckv': jax.random.normal(ks[2], (DEPTH, DEC_BATCH, PAST_LEN, KV_LORA), f32),
        'cache_mla_kpe': jax.random.normal(ks[3], (DEPTH, DEC_BATCH, PAST_LEN, ROPE), f32),
        'cache_band_k': jax.random.normal(ks[4], (DEPTH, DEC_BATCH, L_band, HB, DHB), f32),
        'cache_band_v': jax.random.normal(ks[5], (DEPTH, DEC_BATCH, L_band, HB, DHB), f32),
        'norm_mix': gain(ks[6], D_MODEL),
        'w_in': w(ks[7], (D_MODEL, IN_COLS), D_MODEL),
        'g_cq': gain(ks[8], Q_LORA),
        'w_uq': w(ks[9], (Q_LORA, HA * QKA), Q_LORA),
        'g_ckv': gain(ks[10], KV_LORA),
        'w_uk': w(ks[11], (KV_LORA, HA * NOPE), KV_LORA),
        'w_uv': w(ks[12], (KV_LORA, HA * VA), KV_LORA),
        'g_qa': gain(ks[13], QKA),
        'g_ka': gain(ks[14], QKA),
        'g_qb': gain(ks[15], DHB),
        'g_kb': gain(ks[16], DHB),
        'rel_bias': 0.5 * jax.random.normal(ks[17], (DEPTH, HB, N_REL), f32),
        'w_o': w(ks[18], (MIX_WIDTH, D_MODEL), MIX_WIDTH),
        'norm_ffn': gain(ks[19], D_MODEL),
        'w_up': w(ks[20], (D_MODEL, D_FF), D_MODEL),
        'w_down': w(ks[21], (D_FF, D_MODEL), D_FF),
    }


def reference(x_prompt, x_sample, cache_mla_ckv, cache_mla_kpe, cache_band_k, cache_band_v,
              norm_mix, w_in, g_cq, w_uq, g_ckv, w_uk, w_uv, g_qa, g_ka, g_qb, g_kb, rel_bias,
              w_o, norm_ffn, w_up, w_down):
    S = x_prompt.shape[1]
    T = x_sample.shape[1]
    P = cache_mla_ckv.shape[2]
    Lb = cache_band_k.shape[2]
    keep_p = min(BAND_PAST, S)
    pos_p = jnp.arange(S, dtype=jnp.int32)
    pos_s = P + jnp.arange(T, dtype=jnp.int32)
    pos_hist = jnp.arange(P + T, dtype=jnp.int32)
    band_kpos = jnp.concatenate([jnp.arange(P - Lb, P, dtype=jnp.int32), pos_s])
    band_rel = jnp.clip(pos_s[:, None] - band_kpos[None, :], -MAX_REL, MAX_REL) + MAX_REL

    hp, hs = x_prompt, x_sample
    ckv_p_l, kpe_p_l, bk_p_l, bv_p_l = [], [], [], []
    ckv_s_l, kpe_s_l, bk_s_l, bv_s_l = [], [], [], []
    for l in range(DEPTH):
        qa, ckv, kpe, qb, kb, vb = mixer_inputs(rms_norm(hp, norm_mix[l]), pos_p, w_in[l], g_cq[l],
                                                w_uq[l], g_ckv[l], g_qa[l], g_qb[l], g_kb[l])
        ka, va = mla_keys_values(ckv, kpe, pos_p, w_uk[l], w_uv[l], g_ka[l])
        oa = mla_prompt(qa, ka, va)
        ob = band_prompt(qb, kb, vb, rel_bias[l])
        hp = merge_and_ffn(hp, oa, ob, w_o[l], norm_ffn[l], w_up[l], w_down[l])
        ckv_p_l.append(ckv)
        kpe_p_l.append(kpe)
        bk_p_l.append(kb[:, S - keep_p:])
        bv_p_l.append(vb[:, S - keep_p:])

        qa_s, ckv_s, kpe_s, qb_s, kb_s, vb_s = mixer_inputs(rms_norm(hs, norm_mix[l]), pos_s, w_in[l], g_cq[l],
                                                            w_uq[l], g_ckv[l], g_qa[l], g_qb[l], g_kb[l])
        ckv_all = jnp.concatenate([cache_mla_ckv[l].astype(ckv_s.dtype), ckv_s], axis=1)
        kpe_all = jnp.concatenate([cache_mla_kpe[l].astype(kpe_s.dtype), kpe_s], axis=1)
        ka_s, va_s = mla_keys_values(ckv_all, kpe_all, pos_hist, w_uk[l], w_uv[l], g_ka[l])
        oa_s = dense_attention(qa_s, ka_s, va_s, SCALE_A, None)
        kb_all = jnp.concatenate([cache_band_k[l].astype(kb_s.dtype), kb_s], axis=1)
        vb_all = jnp.concatenate([cache_band_v[l].astype(vb_s.dtype), vb_s], axis=1)
        ob_s = dense_attention(qb_s, kb_all, vb_all, SCALE_B, rel_bias[l][:, band_rel])
        hs = merge_and_ffn(hs, oa_s, ob_s, w_o[l], norm_ffn[l], w_up[l], w_down[l])
        ckv_s_l.append(ckv_s)
        kpe_s_l.append(kpe_s)
        bk_s_l.append(kb_s)
        bv_s_l.append(vb_s)

    ckv_prompt = jnp.stack(ckv_p_l)
    kpe_prompt = jnp.stack(kpe_p_l)
    bandk_prompt = jnp.stack(bk_p_l)
    bandv_prompt = jnp.stack(bv_p_l)
    ckv_sample = jnp.stack(ckv_s_l)
    kpe_sample = jnp.stack(kpe_s_l)
    bandk_sample = jnp.stack(bk_s_l)
    bandv_sample = jnp.stack(bv_s_l)
    return (hp, hs, ckv_prompt, kpe_prompt, bandk_prompt, bandv_prompt,
            ckv_sample, kpe_sample, bandk_sample, bandv_sample)
```

```python
import numpy as np
from collections import deque
from contextlib import ExitStack
import concourse.bass as bass
import concourse.mybir as mybir
from concourse.bass_utils import run_bass_kernel_spmd

F32 = mybir.dt.float32
BF16 = mybir.dt.bfloat16
AF = mybir.ActivationFunctionType
ALU = mybir.AluOpType
AX = mybir.AxisListType

D = 2048
H = 8
SEG = 1024
NHIST = 7 * SEG
NOWN = 2 * SEG + 128
NHALO = 1024
NKM = NHIST + NOWN
NKB = 512 + SEG + 512 + SEG + 128
KB_HA, KB_A, KB_HB, KB_B, KB_S = 0, 512, 1536, 2048, 3072
PAST = 1024
LB = 512
EPS = 1e-6
SCALE_A = 192 ** -0.5
SCALE_B = 128 ** -0.5
NEGM = -30000.0
DFF = 8192
IN_COLS = 3904

PHASES = {"p1", "p2", "p3", "p4", "p5"}
SMALL = False


class Buf:
    __slots__ = ("name", "w", "r", "sem", "semval", "semname", "kind", "alt")

    def __init__(self, name):
        self.name = name
        self.w = None
        self.r = {}
        self.sem = None
        self.semval = 0
        self.semname = None
        self.kind = None
        self.alt = None


class Tn:
    def __init__(self, t, name):
        self.t = t
        self.b = Buf(name)


ENGS = ("pe", "act", "dve", "pool", "sp")


class Sched:
    def __init__(self, nc, es):
        self.nc = nc
        self.es = es
        self.q = {e: [] for e in ENGS}
        self.esem = {e: es.enter_context(nc.semaphore("s_" + e)) for e in ENGS}
        self.ecnt = {e: 0 for e in ENGS}
        self.seen = {e: {} for e in ENGS}
        self.dbufs = []
        self.nbuf = 0
        self.rec = None
        self.semfinal = {}
        self.freesems = []
        self.nsem = 0

    def _waits(self, eng, reads, writes):
        need = {}

        def add(tok):
            if tok is None:
                return
            k, h, v = tok
            if k not in need or need[k][1] < v:
                need[k] = (h, v)

        for b in reads:
            add(b.w)
        for b in writes:
            add(b.w)
            for tok in b.r.values():
                add(tok)
        for k, (h, v) in need.items():
            if self.seen[eng].get(k, 0) >= v:
                continue
            self.seen[eng][k] = v
            self.q[eng].append(("w", h, v))

    def _mark(self, tok, reads, writes):
        for b in reads:
            b.r[tok[0]] = tok
        for b in writes:
            b.w = tok
            b.r = {}

    def start_rec(self):
        self.rec = []

    def stop_rec(self):
        r = self.rec
        self.rec = None
        return r

    def interleave(self, lists):
        idx = [0] * len(lists)
        live = True
        while live:
            live = False
            for li, l in enumerate(lists):
                if idx[li] < len(l):
                    it = l[idx[li]]
                    idx[li] += 1
                    live = True
                    if it[0] == "op":
                        self.op(*it[1:])
                    else:
                        self.dma(*it[1:])

    def op(self, eng, fn, reads=(), writes=()):
        if self.rec is not None:
            self.rec.append(("op", eng, fn, tuple(reads), tuple(writes)))
            return None
        self._waits(eng, reads, writes)
        self.ecnt[eng] += 1
        tok = ("e_" + eng, self.esem[eng], self.ecnt[eng])
        self.q[eng].append(("o", fn, self.esem[eng], 1))
        if eng == "pe":
            self.seen[eng]["e_pe"] = self.ecnt[eng]
        self._mark(tok, reads, writes)
        return tok

    def dma(self, queue, out_ap, in_ap, semb, reads=(), writes=(), slow=False):
        if self.rec is not None:
            self.rec.append(("dma", queue, out_ap, in_ap, semb, tuple(reads), tuple(writes), slow))
            return None
        self._waits(queue, reads, writes)
        kind = "sw" if queue == "pool" else "hw"
        if semb.sem is not None and semb.kind != kind:
            if semb.alt is None:
                semb.alt = Buf(semb.name + "_alt")
            semb = semb.alt
        if semb.sem is None:
            fl = [x for x in self.freesems if x[0] == kind]
            semb.kind = kind
            if fl:
                self.freesems.remove(fl[-1])
                _, semb.semname, semb.sem, semb.semval = fl[-1]
            else:
                semb.semname = "d%d" % self.nsem
                self.nsem += 1
                semb.sem = self.es.enter_context(self.nc.semaphore(semb.semname))
            self.dbufs.append(semb)
        semb.semval += 16
        self.semfinal[semb.semname] = (semb.sem, semb.semval)
        tok = (semb.semname, semb.sem, semb.semval)

        def fn(e, o=out_ap, i=in_ap):
            if slow:
                return e.dma_start(out=o, in_=i, allow_slow_non_contiguous=True)
            return e.dma_start(out=o, in_=i)

        self.q[queue].append(("o", fn, semb.sem, 16))
        self._mark(tok, reads, writes)
        return tok

    def barrier(self):
        toks = []
        for e in ENGS:
            if self.ecnt[e] > 0:
                toks.append(("e_" + e, self.esem[e], self.ecnt[e]))
        for k, (h, v) in self.semfinal.items():
            toks.append((k, h, v))
        for e in ENGS:
            for (k, h, v) in toks:
                if self.seen[e].get(k, 0) >= v:
                    continue
                self.seen[e][k] = v
                self.q[e].append(("w", h, v))
        for b in self.dbufs:
            self.freesems.append((b.kind, b.semname, b.sem, b.semval))
            b.sem = None
        self.dbufs = []

    def emit(self, eng, e):
        for it in self.q[eng]:
            if it[0] == "w":
                e.wait_ge(it[1], it[2])
            elif it[0] == "o":
                ins = it[1](e)
                ins.then_inc(it[2], it[3])
            else:
                e.nop().then_inc(it[1], 1)


class SBAlloc:
    def __init__(self, nc):
        self.nc = nc
        self.base = (nc.sbuf_base + 63) // 64 * 64
        self.top = nc.sbuf_top
        self.off = self.base
        self.n = 0

    def alloc(self, shape, dt, name=None):
        per = 1
        for x in shape[1:]:
            per *= x
        per *= 2 if dt == BF16 else 4
        per = (per + 63) // 64 * 64
        assert self.off + per <= self.top, ("SBUF overflow", name, self.off, per, self.top)
        self.n += 1
        nm = "%s_%d" % (name or "t", self.n)
        t = self.nc.alloc_sbuf_tensor_at(nm, list(shape), dt, offset=self.off)
        self.off += per
        return Tn(t, nm)

    def mark(self):
        return self.off

    def release(self, m):
        self.off = m


def f_mm(groups):
    def fn(e):
        ins = None
        for (o, l, r, st, sp) in groups:
            ins = e.matmul(o, l, r, start=st, stop=sp)
        return ins
    return fn


def f_tr(items, ident):
    def fn(e):
        ins = None
        for (o, i) in items:
            ins = e.transpose(o, i, ident)
        return ins
    return fn


def f_tt(out, in0, in1, op):
    return lambda e: e.tensor_tensor(out=out, in0=in0, in1=in1, op=op)


def f_ts(out, in0, s1, s2, op0, op1=None):
    if op1 is None:
        return lambda e: e.tensor_scalar(out=out, in0=in0, scalar1=s1, scalar2=None, op0=op0)
    return lambda e: e.tensor_scalar(out=out, in0=in0, scalar1=s1, scalar2=s2, op0=op0, op1=op1)


def f_stt(out, in0, scalar, in1, op0, op1, accum=None):
    if accum is None:
        return lambda e: e.scalar_tensor_tensor(out=out, in0=in0, scalar=scalar, in1=in1, op0=op0, op1=op1)
    return lambda e: e.scalar_tensor_tensor(out=out, in0=in0, scalar=scalar, in1=in1, op0=op0, op1=op1,
                                            accum_out=accum)


def f_act(out, in_, func, bias=None, scale=None):
    kw = {}
    if bias is not None:
        kw["bias"] = bias
    if scale is not None:
        kw["scale"] = scale
    return lambda e: e.activation(out=out, in_=in_, func=func, **kw)


def f_red(out, in_):
    return lambda e: e.tensor_reduce(out=out, in_=in_, axis=AX.X, op=ALU.add)


def f_copy(out, in_):
    def fn(e):
        if hasattr(e, "tensor_copy"):
            return e.tensor_copy(out, in_)
        return e.activation(out=out, in_=in_, func=AF.Copy)
    return fn


def f_memset(ap, v):
    return lambda e: e.memset(ap, v)


def f_recip(out, in_):
    return lambda e: e.reciprocal(out, in_)


def build_nc(debug=False):
    nc = bass.Bass("TRN2", target_bir_lowering=False)
    es = ExitStack()

    def din(name, shape, dt=F32):
        return nc.dram_tensor(name, list(shape), dt, kind="ExternalInput").ap()

    def dout(name, shape, dt=F32):
        return nc.dram_tensor(name, list(shape), dt, kind="ExternalOutput").ap()

    def dscr(name, shape, dt=BF16):
        kind = "ExternalOutput" if debug else "Internal"
        return nc.dram_tensor(name, list(shape), dt, kind=kind).ap()

    xh = din("xh", [NHIST, D])
    xo = din("xo", [NOWN, D])
    xl = din("xl", [NHALO, D])
    c_ckv = din("c_ckv", [2 * PAST, 256])
    c_kpe = din("c_kpe", [2 * PAST, 64])
    c_bk = din("c_bk", [2 * LB, 1024])
    c_bv = din("c_bv", [2 * LB, 1024])
    rope_h = din("rope_h", [NHIST, 64])
    rope_o = din("rope_o", [NOWN, 64])
    rope_c = din("rope_c", [PAST, 64])
    flg = din("flg", [128, 16])
    bbias = din("bbias", [H, 640, 128])
    bmask = din("bmask", [640, 128])
    sbias = din("sbias", [H, 640, 32])
    ident_d = din("ident", [128, 128])
    norm_mix = din("norm_mix", [D])
    norm_ffn = din("norm_ffn", [D])
    w_in = din("w_in", [D, IN_COLS])
    g_cq = din("g_cq", [512])
    w_uq = din("w_uq", [512, 1536])
    g_ckv = din("g_ckv", [256])
    w_uk = din("w_uk", [256, 1024])
    w_uv = din("w_uv", [256, 1024])
    g_qa = din("g_qa", [192])
    g_ka = din("g_ka", [192])
    g_qb = din("g_qb", [128])
    g_kb = din("g_kb", [128])
    w_o = din("w_o", [D, D])
    w_up = din("w_up", [D, DFF])
    w_down = din("w_down", [DFF, D])

    y_o = dout("y_o", [NOWN, D])
    ckv_o = dout("ckv_o", [NOWN, 256])
    kpe_o = dout("kpe_o", [NOWN, 64])
    bk_o = dout("bk_o", [NOWN, 1024])
    bv_o = dout("bv_o", [NOWN, 1024])

    KTn = dscr("KTn", [H, 128, NKM])
    KTp = dscr("KTp", [H, 64, NKM])
    Vs = dscr("Vs", [NKM, 1024])
    KTn_c = dscr("KTn_c", [H, 128, 2 * PAST])
    KTp_c = dscr("KTp_c", [H, 64, 2 * PAST])
    V_c = dscr("V_c", [2 * PAST, 1024])
    QTn = dscr("QTn", [H, 128, NOWN])
    QTp = dscr("QTp", [H, 64, NOWN])
    QBT = dscr("QBT", [H, 128, NOWN])
    KBT = dscr("KBT", [H, 128, NKB])
    VB = dscr("VB", [NKB, 1024])
    KBT_c = dscr("KBT_c", [H, 128, 2 * LB])
    OT = dscr("OT", [2 * H, 128, NOWN])
    Hs = dscr("Hs", [NOWN, D], F32)

    WU = dscr("WU", [DFF // 256, 128, 16, 256])
    WD = dscr("WD", [D // 256, 128, 64, 256])
    S = Sched(nc, es)
    sb = SBAlloc(nc)
    psum = nc.alloc_psum_tensor("ps", [128, 8, 512], F32)
    PB = [Tn(None, "pb%d" % i) for i in range(8)]

    def pbank(i):
        return psum[:, i, :]

    def pbank_bf(i):
        return psum[:, i, :].bitcast(BF16)

    ident = sb.alloc([128, 128], BF16, "ident")
    ones = sb.alloc([128, 128], BF16, "ones")
    epsc = sb.alloc([128, 1], F32, "eps")
    zero_c = sb.alloc([128, 1], F32, "zero")
    flags = sb.alloc([128, 16], F32, "flags")
    nmix = sb.alloc([128, 16], F32, "nmix")
    nffn = sb.alloc([128, 16], F32, "nffn")
    gqa_n = sb.alloc([128, 1], F32, "gqa_n")
    gka_n = sb.alloc([128, 1], F32, "gka_n")
    gqb_p = sb.alloc([128, 1], F32, "gqb_p")
    gqa_pe = sb.alloc([128, 64], F32, "gqa_pe")
    gka_pe = sb.alloc([128, 64], F32, "gka_pe")
    gkb_b = sb.alloc([128, 128], F32, "gkb_b")
    gcq_b = sb.alloc([128, 512], F32, "gcq_b")
    gckv_b = sb.alloc([128, 256], F32, "gckv_b")

    S.dma("pool", ident.t[:], ident_d, ident.b, writes=[ident.b])
    S.op("pool", f_memset(ones.t[:], 1.0), writes=[ones.b])
    S.op("pool", f_memset(epsc.t[:], EPS), writes=[epsc.b])
    S.op("pool", f_memset(zero_c.t[:], 0.0), writes=[zero_c.b])
    S.dma("sp", flags.t[:], flg, flags.b, writes=[flags.b])
    S.dma("sp", nmix.t[:], norm_mix.rearrange("(k p) -> p k", p=128), nmix.b, writes=[nmix.b], slow=True)
    S.dma("sp", nffn.t[:], norm_ffn.rearrange("(k p) -> p k", p=128), nffn.b, writes=[nffn.b], slow=True)
    S.dma("sp", gqa_n.t[:], g_qa[0:128].rearrange("(p o) -> p o", o=1), gqa_n.b, writes=[gqa_n.b])
    S.dma("sp", gka_n.t[:], g_ka[0:128].rearrange("(p o) -> p o", o=1), gka_n.b, writes=[gka_n.b])
    S.dma("sp", gqb_p.t[:], g_qb.rearrange("(p o) -> p o", o=1), gqb_p.b, writes=[gqb_p.b])
    S.dma("sp", gqa_pe.t[:], g_qa[128:192].partition_broadcast(128), gqa_pe.b, writes=[gqa_pe.b])
    S.dma("sp", gka_pe.t[:], g_ka[128:192].partition_broadcast(128), gka_pe.b, writes=[gka_pe.b])
    S.dma("sp", gkb_b.t[:], g_kb.partition_broadcast(128), gkb_b.b, writes=[gkb_b.b])
    S.dma("sp", gcq_b.t[:], g_cq.partition_broadcast(128), gcq_b.b, writes=[gcq_b.b])
    S.dma("sp", gckv_b.t[:], g_ckv.partition_broadcast(128), gckv_b.b, writes=[gckv_b.b])
    bg = deque()
    cvb = Buf("cv")
    w_up_v = w_up.rearrange("(k p) c -> p k c", p=128)
    w_dn_v = w_down.rearrange("(k p) c -> p k c", p=128)
    if "p5" in PHASES:
        for g_ in range(DFF // 256):
            bg.append(lambda g_=g_: S.dma("pool", WU[g_], w_up_v[:, :, g_ * 256:(g_ + 1) * 256], cvb))
        for g_ in range(D // 256):
            for pc_ in range(4):
                bg.append(lambda g_=g_, pc_=pc_: S.dma("pool", WD[g_][:, pc_ * 16:(pc_ + 1) * 16, :],
                                                       w_dn_v[:, pc_ * 16:(pc_ + 1) * 16, g_ * 256:(g_ + 1) * 256], cvb))

    def bg_step(n=1):
        for _ in range(n):
            if bg:
                bg.popleft()()

    consts = [ident.b, ones.b, epsc.b, zero_c.b, flags.b, nmix.b, nffn.b, gqa_n.b, gka_n.b, gqb_p.b,
              gqa_pe.b, gka_pe.b, gkb_b.b, gcq_b.b, gckv_b.b]
    S.barrier()
    pm0 = sb.mark()

    def load_w(dst, src_ap, kchunks, step=4):
        v = src_ap.rearrange("(k p) c -> p k c", p=128)
        for k0 in range(0, kchunks, step):
            k1 = min(kchunks, k0 + step)
            S.dma("pool", dst.t[:, k0:k1, :], v[:, k0:k1, :], dst.b, writes=[])
        dst.b.w = (dst.b.semname, dst.b.sem, dst.b.semval)
        dst.b.r = {}

    def dma_split(queue, pieces, buf, reads=()):
        tok = None
        for n_, (o_, i_) in enumerate(pieces):
            tok = S.dma(queue, o_, i_, buf, reads=reads, writes=[buf] if n_ == 0 else [])
        buf.w = tok
        buf.r = {}

    def rstd_from_ss(ss_ap, ss_buf, n, dim, stat):
        S.op("act", f_act(ss_ap, ss_ap, AF.Ln, bias=epsc.t[:, 0:1], scale=1.0 / dim), reads=[ss_buf], writes=[ss_buf])
        S.op("act", f_act(ss_ap, ss_ap, AF.Exp, scale=-0.5), reads=[ss_buf], writes=[ss_buf])

    if "p1" in PHASES:
        cnt = {"s": 0}

        def alloc_sets(specs, nch=2):
            sets = [{nm: sb.alloc(shape, dt, "%s_%d" % (nm, p)) for (nm, shape, dt) in specs} for p in range(nch)]
            for p, z in enumerate(sets):
                z["bk"] = [4 * p + n for n in range(4)] if nch == 2 else [2 * p, 2 * p + 1, 2 * p, 2 * p + 1]
            return sets

        def front(x_rows, nm, Z, par):
            xb, st, xn, hnT = Z["xt"], Z["st"], Z["xn"], Z["hnT"]
            S.dma("sp", xb.t[:], x_rows, xb.b, writes=[xb.b])
            S.op("pool", f_memset(st.t[:, 0:1], 0.0), writes=[st.b])
            S.op("dve", f_stt(junk.t[:], xb.t[:], 1.0, xb.t[:], ALU.mult, ALU.mult, accum=st.t[:, 0:1]),
                 reads=[xb.b], writes=[junk.b, st.b])
            rstd_from_ss(st.t[:, 0:1], st.b, 1, D, st)
            S.op("act", f_act(xn.t[:], xb.t[:], AF.Copy, scale=st.t[:, 0:1]), reads=[xb.b, st.b], writes=[xn.b])
            for half in range(2):
                bk = Z["bk"][half]
                pv = pbank_bf(bk).rearrange("p (k t) -> p k t", k=8)
                S.op("pe", f_tr([(pv[:, k, :], xn.t[:, (half * 8 + k) * 128:(half * 8 + k + 1) * 128])
                                 for k in range(8)], ident.t[:]), reads=[xn.b, ident.b], writes=[PB[bk].b])
                S.op("dve", f_tt(hnT.t[:, half * 8:half * 8 + 8, :], pv,
                                 nm.t[:, half * 8:half * 8 + 8].unsqueeze(2).to_broadcast([128, 8, 128]), ALU.mult),
                     reads=[PB[bk].b, nm.b], writes=[hnT.b])

        def head_rstd(src3, srcbuf, nh, dh, ss_ap, stb, sqb, extra=None, div=None):
            S.op("pool", f_tt(sqb.t[:, 0:nh, 0:dh], src3, src3, ALU.mult), reads=[srcbuf], writes=[sqb.b])
            S.op("dve", f_red(ss_ap, sqb.t[:, 0:nh, 0:dh]), reads=[sqb.b], writes=[stb.b])
            if extra is not None:
                S.op("dve", f_ts(ss_ap, ss_ap, extra, None, ALU.add), reads=[stb.b], writes=[stb.b])
            rstd_from_ss(ss_ap, stb.b, nh, div or dh, stb)

        def rope(dst3, src3, srcbuf, dstbuf, csb, nh, rp):
            c = csb.t[:, 0:32].unsqueeze(1).to_broadcast([128, nh, 32])
            sn = csb.t[:, 32:64].unsqueeze(1).to_broadcast([128, nh, 32])
            x1 = src3[:, :, 0:32]
            x2 = src3[:, :, 32:64]
            t = [rp.t[:, i, 0:nh, :] for i in range(4)]
            S.op("pool", f_tt(t[0], x1, c, ALU.mult), reads=[srcbuf, csb.b], writes=[rp.b])
            S.op("pool", f_tt(t[1], x2, sn, ALU.mult), reads=[srcbuf, csb.b], writes=[rp.b])
            S.op("pool", f_tt(t[2], x1, sn, ALU.mult), reads=[srcbuf, csb.b], writes=[rp.b])
            S.op("pool", f_tt(t[3], x2, c, ALU.mult), reads=[srcbuf, csb.b], writes=[rp.b])
            S.op("pool", f_tt(dst3[:, :, 0:32], t[0], t[1], ALU.subtract), reads=[rp.b], writes=[dstbuf])
            S.op("pool", f_tt(dst3[:, :, 32:64], t[2], t[3], ALU.add), reads=[rp.b], writes=[dstbuf])

        win_m = sb.alloc([128, 16, 832], BF16, "win_m")
        wuk = sb.alloc([128, 2, 1024], BF16, "wuk")
        wuv = sb.alloc([128, 2, 1024], BF16, "wuv")
        load_w(win_m, w_in[:, 0:832], 16)
        load_w(wuk, w_uk, 2)
        load_w(wuv, w_uv, 2)
        junk = sb.alloc([128, D], BF16, "junk")
        stg_kn = sb.alloc([128, 8, 512], BF16, "skn")
        stg_kp = sb.alloc([64, 8, 512], BF16, "skp")
        stg_v = sb.alloc([128, 4, 1024], BF16, "sv")
        csl = [sb.alloc([128, 64], F32, "cs%d" % i) for i in range(8)]
        pm1 = sb.mark()
        KV_SPECS = [("xt", [128, D], F32), ("st", [128, 32], F32), ("xn", [128, D], BF16),
                    ("hnT", [128, 16, 128], BF16), ("ckvf", [128, 256], F32), ("kpef", [128, 64], F32),
                    ("ckvb", [128, 256], BF16), ("smT", [128, 6, 128], BF16), ("kf", [128, 8, 128], F32),
                    ("kn", [128, 8, 192], BF16), ("kpg", [128, 64], F32), ("kpr", [128, 64], F32)]
        ZM4 = alloc_sets(KV_SPECS + [("sqk", [128, 8, 128], F32), ("rpk", [128, 4, 1, 32], F32)], nch=4)

        def kv_from_latent(Z, par, sub):
            ckv_f, kpe_f, csb, st = Z["ckvf"], Z["kpef"], Z["csb"], Z["st"]
            ckvb, smT, kf, kn, kpg, kpr = Z["ckvb"], Z["smT"], Z["kf"], Z["kn"], Z["kpg"], Z["kpr"]
            S.op("dve", f_copy(ckvb.t[:], ckv_f.t[:]), reads=[ckv_f.b], writes=[ckvb.b])
            b0, b1, b2, b3 = Z["bk"]
            b0, b1 = b2, b3
            pv = pbank_bf(b2).rearrange("p (k t) -> p k t", k=8)
            S.op("pe", f_tr([(pv[:, 4 + k, :], ckvb.t[:, k * 128:(k + 1) * 128]) for k in range(2)], ident.t[:]),
                 reads=[ckvb.b, ident.b], writes=[PB[b2].b])
            S.op("act", f_copy(smT.t[:, 4:6, :], pv[:, 4:6, :]), reads=[PB[b2].b], writes=[smT.b])
            for g in range(2):
                S.op("pe", f_mm([(pbank(b2 + g), smT.t[:, 4 + k, :], wuk.t[:, k, g * 512:(g + 1) * 512], k == 0, k == 1)
                                 for k in range(2)]), reads=[smT.b, wuk.b], writes=[PB[b2 + g].b])
            for g in range(2):
                S.op("act", f_copy(kf.t[:, g * 4:(g + 1) * 4, :], pbank(b2 + g).rearrange("p (h d) -> p h d", h=4)),
                     reads=[PB[b2 + g].b], writes=[kf.b])
            for g in range(2):
                S.op("pe", f_mm([(pbank(b0 + g), smT.t[:, 4 + k, :], wuv.t[:, k, g * 512:(g + 1) * 512], k == 0, k == 1)
                                 for k in range(2)]), reads=[smT.b, wuv.b], writes=[PB[b0 + g].b])
            for g in range(2):
                S.op("act", f_copy(stg_v.t[:, sub, g * 512:(g + 1) * 512], pbank(b0 + g)), reads=[PB[b0 + g].b],
                     writes=[stg_v.b])
            S.op("pool", f_memset(st.t[:, 1:2], 0.0), writes=[st.b])
            S.op("dve", f_stt(kpg.t[:], kpe_f.t[:], 1.0, kpe_f.t[:], ALU.mult, ALU.mult, accum=st.t[:, 1:2]),
                 reads=[kpe_f.b], writes=[kpg.b, st.b])
            head_rstd(kf.t[:, :, :], kf.b, 8, 128, st.t[:, 8:16], st, Z["sqk"], extra=st.t[:, 1:2], div=192)
            S.op("dve", f_tt(kn.t[:, :, 0:128], kf.t[:, :, :], st.t[:, 8:16].unsqueeze(2).to_broadcast([128, 8, 128]),
                             ALU.mult), reads=[kf.b, st.b], writes=[kn.b])
            S.op("pool", f_tt(kpg.t[:], kpe_f.t[:], gka_pe.t[:], ALU.mult), reads=[kpe_f.b, gka_pe.b], writes=[kpg.b])
            rope(kpr.t[:].rearrange("p (h d) -> p h d", h=1), kpg.t[:].rearrange("p (h d) -> p h d", h=1),
                 kpg.b, kpr.b, csb, 1, Z["rpk"])
            S.op("dve", f_tt(kn.t[:, :, 128:192], kpr.t[:].unsqueeze(1).to_broadcast([128, 8, 64]),
                             st.t[:, 8:16].unsqueeze(2).to_broadcast([128, 8, 64]), ALU.mult),
                 reads=[kpr.b, st.b], writes=[kn.b])
            pn = pbank_bf(b2).rearrange("p (h t) -> p h t", h=8)
            pp = pbank_bf(b3).rearrange("p (h t) -> p h t", h=8)
            S.op("pe", f_tr([(pn[:, h, :], kn.t[:, h, 0:128]) for h in range(8)], ident.t[:]),
                 reads=[kn.b, ident.b], writes=[PB[b2].b])
            S.op("pe", f_tr([(pp[0:64, h, :], kn.t[:, h, 128:192]) for h in range(8)], ident.t[:]),
                 reads=[kn.b, ident.b], writes=[PB[b3].b])
            S.op("act", f_act(stg_kn.t[:, :, sub * 128:(sub + 1) * 128], pn, AF.Copy, scale=gka_n.t[:, 0:1]),
                 reads=[PB[b2].b, gka_n.b], writes=[stg_kn.b])
            S.op("dve", f_copy(stg_kp.t[:, :, sub * 128:(sub + 1) * 128], pp[0:64, :, :]), reads=[PB[b3].b],
                 writes=[stg_kp.b])

        def q_from_cq(Z, par, sub):
            st, cqf, cqn, smT, qf, qn, csb = Z["st"], Z["cqf"], Z["cqn"], Z["smT"], Z["qf"], Z["qn"], Z["csb"]
            b0, b1, b2, b3 = Z["bk"]
            S.op("pool", f_memset(st.t[:, 2:3], 0.0), writes=[st.b])
            S.op("dve", f_stt(junk.t[:, 0:512], cqf.t[:], 1.0, cqf.t[:], ALU.mult, ALU.mult, accum=st.t[:, 2:3]),
                 reads=[cqf.b], writes=[junk.b, st.b])
            rstd_from_ss(st.t[:, 2:3], st.b, 1, 512, st)
            S.op("dve", f_stt(cqn.t[:], cqf.t[:], st.t[:, 2:3], gcq_b.t[:], ALU.mult, ALU.mult),
                 reads=[cqf.b, st.b, gcq_b.b], writes=[cqn.b])
            pv = pbank_bf(b0).rearrange("p (k t) -> p k t", k=8)
            S.op("pe", f_tr([(pv[:, k, :], cqn.t[:, k * 128:(k + 1) * 128]) for k in range(4)], ident.t[:]),
                 reads=[cqn.b, ident.b], writes=[PB[b0].b])
            S.op("act", f_copy(smT.t[:, 0:4, :], pv[:, 0:4, :]), reads=[PB[b0].b], writes=[smT.b])
            qf2 = qf.t[:].rearrange("p h d -> p (h d)")
            for g in range(3):
                bq = (b1, b0, b1)[g]
                S.op("pe", f_mm([(pbank(bq), smT.t[:, k, :], wuq.t[:, k, g * 512:(g + 1) * 512], k == 0, k == 3)
                                 for k in range(4)]), reads=[smT.b, wuq.b], writes=[PB[bq].b])
                S.op("act", f_copy(qf2[:, g * 512:(g + 1) * 512], pbank(bq)), reads=[PB[bq].b], writes=[qf.b])
            head_rstd(qf.t[:, :, :], qf.b, 8, 192, st.t[:, 16:24], st, Z["sq"])
            S.op("dve", f_tt(qn.t[:, :, 0:128], qf.t[:, :, 0:128], st.t[:, 16:24].unsqueeze(2).to_broadcast([128, 8, 128]),
                             ALU.mult), reads=[qf.b, st.b], writes=[qn.b])
            S.op("dve", f_tt(qf.t[:, :, 128:192], qf.t[:, :, 128:192],
                             st.t[:, 16:24].unsqueeze(2).to_broadcast([128, 8, 64]), ALU.mult),
                 reads=[qf.b, st.b], writes=[qf.b])
            S.op("pool", f_tt(qf.t[:, :, 128:192], qf.t[:, :, 128:192],
                              gqa_pe.t[:].unsqueeze(1).to_broadcast([128, 8, 64]), ALU.mult),
                 reads=[qf.b, gqa_pe.b], writes=[qf.b])
            rope(qn.t[:, :, 128:192], qf.t[:, :, 128:192], qf.b, qn.b, csb, 8, Z["rp"])
            pn = pbank_bf(b0).rearrange("p (h t) -> p h t", h=8)
            pp = pbank_bf(b1).rearrange("p (h t) -> p h t", h=8)
            S.op("pe", f_tr([(pn[:, h, :], qn.t[:, h, 0:128]) for h in range(8)], ident.t[:]),
                 reads=[qn.b, ident.b], writes=[PB[b0].b])
            S.op("pe", f_tr([(pp[0:64, h, :], qn.t[:, h, 128:192]) for h in range(8)], ident.t[:]),
                 reads=[qn.b, ident.b], writes=[PB[b1].b])
            S.op("act", f_act(stg_qn.t[:, :, sub * 128:(sub + 1) * 128], pn, AF.Copy, scale=gqa_n.t[:, 0:1]),
                 reads=[PB[b0].b, gqa_n.b], writes=[stg_qn.b])
            S.op("dve", f_copy(stg_qp.t[:, :, sub * 128:(sub + 1) * 128], pp[0:64, :, :]), reads=[PB[b1].b],
                 writes=[stg_qp.b])

        def mla_tile(xsrc, row0, nsub, ropesrc, rope0, kslot0, own_row0=None, from_cache=None, kdst=None, sets=None):
            KTn_d, KTp_d, V_d = kdst
            recs = []
            recs2 = []
            for sub in range(nsub):
                S.start_rec()
                nch = len(sets)
                par = cnt["s"] % nch
                Z = dict(sets[par])
                Z["csb"] = csl[cnt["s"] % 8]
                cnt["s"] += 1
                b2, b3 = Z["bk"][2], Z["bk"][3]
                r0 = row0 + sub * 128
                csb, cf, kp, hnT = Z["csb"], Z["ckvf"], Z["kpef"], Z["hnT"]
                if from_cache is not None:
                    S.dma("sp", csb.t[:], ropesrc[rope0 + sub * 128:rope0 + (sub + 1) * 128, :], csb.b, writes=[csb.b])
                if from_cache is None:
                    front(xsrc[r0:r0 + 128, :], nmix, Z, par)
                    S.dma("sp", csb.t[:], ropesrc[rope0 + sub * 128:rope0 + (sub + 1) * 128, :], csb.b, writes=[csb.b])
                    if own_row0 is not None:
                        S.op("pe", f_mm([(pbank(b2), hnT.t[:, k, :], win_m.t[:, k, 0:512], k == 0, k == 15)
                                         for k in range(16)]), reads=[hnT.b, win_m.b], writes=[PB[b2].b])
                    S.op("pe", f_mm([(pbank(b3)[:, 0:320], hnT.t[:, k, :], win_m.t[:, k, 512:832], k == 0, k == 15)
                                     for k in range(16)]), reads=[hnT.b, win_m.b], writes=[PB[b3].b])
                    if own_row0 is not None:
                        S.op("act", f_copy(Z["cqf"].t[:], pbank(b2)), reads=[PB[b2].b], writes=[Z["cqf"].b])
                    S.op("act", f_copy(cf.t[:], pbank(b3)[:, 0:256]), reads=[PB[b3].b], writes=[cf.b])
                    S.op("act", f_copy(kp.t[:], pbank(b3)[:, 256:320]), reads=[PB[b3].b], writes=[kp.b])
                    st = Z["st"]
                    S.op("pool", f_memset(st.t[:, 3:4], 0.0), writes=[st.b])
                    S.op("dve", f_stt(junk.t[:, 0:256], cf.t[:], 1.0, cf.t[:], ALU.mult, ALU.mult,
                                      accum=st.t[:, 3:4]), reads=[cf.b], writes=[junk.b, st.b])
                    rstd_from_ss(st.t[:, 3:4], st.b, 1, 256, st)
                    S.op("dve", f_stt(cf.t[:], cf.t[:], st.t[:, 3:4], gckv_b.t[:], ALU.mult, ALU.mult),
                         reads=[cf.b, st.b, gckv_b.b], writes=[cf.b])
                    if own_row0 is not None:
                        o0 = own_row0 + sub * 128
                        S.dma("pool", ckv_o[o0:o0 + 128, :], cf.t[:], cf.b, reads=[cf.b])
                        S.dma("pool", kpe_o[o0:o0 + 128, :], kp.t[:], kp.b, reads=[kp.b])
                        recs.append(S.stop_rec())
                        S.start_rec()
                        q_from_cq(Z, par, sub)
                        recs2.append(S.stop_rec())
                        S.start_rec()
                else:
                    ca, ka = from_cache
                    S.dma("sp", cf.t[:], ca[r0:r0 + 128, :], cf.b, writes=[cf.b])
                    S.dma("sp", kp.t[:], ka[r0:r0 + 128, :], kp.b, writes=[kp.b])
                kv_from_latent(Z, par, sub)
                (recs2 if (own_row0 is not None and from_cache is None) else recs).append(S.stop_rec())
            npl = 2 if (own_row0 is not None and from_cache is None) else 0
            for i0 in range(0, nsub, nch):
                S.interleave(recs[i0:i0 + nch])
                if npl:
                    S.interleave(recs2[npl * i0:npl * (i0 + nch)])
            n = nsub * 128
            S.dma("act", KTn_d[:, :, kslot0:kslot0 + n].rearrange("h d t -> d h t"), stg_kn.t[:, :, 0:n], stg_kn.b,
                  reads=[stg_kn.b])
            S.dma("act", KTp_d[:, :, kslot0:kslot0 + n].rearrange("h d t -> d h t"), stg_kp.t[:, :, 0:n], stg_kp.b,
                  reads=[stg_kp.b])
            S.dma("act", V_d[kslot0:kslot0 + n, :].rearrange("(s p) c -> p s c", p=128), stg_v.t[:, 0:nsub, :],
                  stg_v.b, reads=[stg_v.b])
            if own_row0 is not None:
                S.dma("act", QTn[:, :, own_row0:own_row0 + n].rearrange("h d t -> d h t"), stg_qn.t[:, :, 0:n],
                      stg_qn.b, reads=[stg_qn.b])
                S.dma("act", QTp[:, :, own_row0:own_row0 + n].rearrange("h d t -> d h t"), stg_qp.t[:, :, 0:n],
                      stg_qp.b, reads=[stg_qp.b])

        kd = (KTn, KTp, Vs)
        for t in range(1 if SMALL else NHIST // 512):
            mla_tile(xh, t * 512, 4, rope_h, t * 512, t * 512, kdst=kd, sets=ZM4)
        if "p4" in PHASES:
            kdc = (KTn_c, KTp_c, V_c)
            for sbi in range(1 if SMALL else 2):
                for t in range(1 if SMALL else 2):
                    mla_tile(None, sbi * PAST + t * 512, 4, rope_c, t * 512, sbi * PAST + t * 512,
                             from_cache=(c_ckv, c_kpe), kdst=kdc, sets=ZM4)
        S.barrier()
        sb.release(pm1)
        wuq = sb.alloc([128, 4, 1536], BF16, "wuq")
        load_w(wuq, w_uq, 4)
        ZM = alloc_sets(KV_SPECS + [("cqn", [128, 512], BF16), ("cqf", [128, 512], F32), ("qf", [128, 8, 192], F32),
                                    ("sq", [128, 8, 192], F32), ("qn", [128, 8, 192], BF16),
                                    ("rp", [128, 4, 8, 32], F32), ("sqk", [128, 8, 128], F32),
                                    ("rpk", [128, 4, 1, 32], F32)], nch=2)
        stg_qn = sb.alloc([128, 8, 512], BF16, "sqn")
        stg_qp = sb.alloc([64, 8, 512], BF16, "sqp")
        for t in range(1 if SMALL else 4):
            mla_tile(xo, t * 512, 4, rope_o, t * 512, NHIST + t * 512, own_row0=t * 512, kdst=kd, sets=ZM)
        mla_tile(xo, 2048, 1, rope_o, 2048, NHIST + 2048, own_row0=2048, kdst=kd, sets=ZM)
        S.barrier()
        sb.release(pm0)

        win_b = sb.alloc([128, 16, 3072], BF16, "win_b")
        load_w(win_b, w_in[:, 832:3904], 16, step=2)
        junk = sb.alloc([128, D], BF16, "junkb")
        ZB = alloc_sets([("xt", [128, D], F32), ("st", [128, 32], F32), ("xn", [128, D], BF16),
                         ("hnT", [128, 16, 128], BF16), ("bf", [128, 8, 128], F32), ("sq", [128, 8, 128], F32),
                         ("bn", [128, 8, 128], BF16), ("kbo", [128, 8, 128], F32), ("vbo", [128, 1024], F32)])
        stg_qb = sb.alloc([128, 8, 512], BF16, "sqb")
        stg_kb = sb.alloc([128, 8, 512], BF16, "skb")
        stg_vb = sb.alloc([128, 4, 1024], BF16, "svb")

        def band_tile(xsrc, row0, nsub, kslot0, own_row0=None):
            recs = []
            for sub in range(nsub):
                S.start_rec()
                par = cnt["s"] % 2
                cnt["s"] += 1
                Z = ZB[par]
                b0, b1, b2, b3 = Z["bk"]
                hnT, st, bf, bn, ko, vo = Z["hnT"], Z["st"], Z["bf"], Z["bn"], Z["kbo"], Z["vbo"]
                r0 = row0 + sub * 128
                front(xsrc[r0:r0 + 128, :], nmix, Z, par)
                if own_row0 is not None:
                    for g in range(2):
                        S.op("pe", f_mm([(pbank(b2 + g), hnT.t[:, k, :], win_b.t[:, k, g * 512:(g + 1) * 512], k == 0, k == 15)
                                         for k in range(16)]), reads=[hnT.b, win_b.b], writes=[PB[b2 + g].b])
                    for g in range(2):
                        S.op("act", f_copy(bf.t[:, g * 4:(g + 1) * 4, :], pbank(b2 + g).rearrange("p (h d) -> p h d", h=4)),
                             reads=[PB[b2 + g].b], writes=[bf.b])
                    head_rstd(bf.t[:, :, :], bf.b, 8, 128, st.t[:, 8:16], st, Z["sq"])
                    S.op("dve", f_tt(bn.t[:, :, :], bf.t[:, :, :], st.t[:, 8:16].unsqueeze(2).to_broadcast([128, 8, 128]),
                                     ALU.mult), reads=[bf.b, st.b], writes=[bn.b])
                    pn = pbank_bf(b0).rearrange("p (h t) -> p h t", h=8)
                    S.op("pe", f_tr([(pn[:, h, :], bn.t[:, h, :]) for h in range(8)], ident.t[:]),
                         reads=[bn.b, ident.b], writes=[PB[b0].b])
                    S.op("act", f_act(stg_qb.t[:, :, sub * 128:(sub + 1) * 128], pn, AF.Copy, scale=gqb_p.t[:, 0:1]),
                         reads=[PB[b0].b, gqb_p.b], writes=[stg_qb.b])
                for g in range(2):
                    S.op("pe", f_mm([(pbank(b2 + g), hnT.t[:, k, :], win_b.t[:, k, 1024 + g * 512:1024 + (g + 1) * 512],
                                      k == 0, k == 15) for k in range(16)]), reads=[hnT.b, win_b.b], writes=[PB[b2 + g].b])
                for g in range(2):
                    S.op("act", f_copy(bf.t[:, g * 4:(g + 1) * 4, :], pbank(b2 + g).rearrange("p (h d) -> p h d", h=4)),
                         reads=[PB[b2 + g].b], writes=[bf.b])
                for g in range(2):
                    S.op("pe", f_mm([(pbank(b0 + g), hnT.t[:, k, :], win_b.t[:, k, 2048 + g * 512:2048 + (g + 1) * 512],
                                      k == 0, k == 15) for k in range(16)]), reads=[hnT.b, win_b.b], writes=[PB[b0 + g].b])
                head_rstd(bf.t[:, :, :], bf.b, 8, 128, st.t[:, 16:24], st, Z["sq"])
                S.op("dve", f_tt(bf.t[:, :, :], bf.t[:, :, :], st.t[:, 16:24].unsqueeze(2).to_broadcast([128, 8, 128]),
                                 ALU.mult), reads=[bf.b, st.b], writes=[bf.b])
                S.op("pool", f_tt(ko.t[:, :, :], bf.t[:, :, :], gkb_b.t[:].unsqueeze(1).to_broadcast([128, 8, 128]),
                                  ALU.mult), reads=[bf.b, gkb_b.b], writes=[ko.b])
                S.op("dve", f_copy(bn.t[:, :, :], ko.t[:, :, :]), reads=[ko.b], writes=[bn.b])
                for g in range(2):
                    S.op("act", f_copy(vo.t[:, g * 512:(g + 1) * 512], pbank(b0 + g)), reads=[PB[b0 + g].b], writes=[vo.b])
                pn = pbank_bf(b2).rearrange("p (h t) -> p h t", h=8)
                S.op("pe", f_tr([(pn[:, h, :], bn.t[:, h, :]) for h in range(8)], ident.t[:]),
                     reads=[bn.b, ident.b], writes=[PB[b2].b])
                S.op("act", f_copy(stg_kb.t[:, :, sub * 128:(sub + 1) * 128], pn), reads=[PB[b2].b],
                     writes=[stg_kb.b])
                S.op("pool", f_copy(stg_vb.t[:, sub, :], vo.t[:]), reads=[vo.b], writes=[stg_vb.b])
                if own_row0 is not None:
                    o0 = own_row0 + sub * 128
                    S.dma("pool", bk_o[o0:o0 + 128, :], ko.t[:].rearrange("p h d -> p (h d)"), ko.b, reads=[ko.b])
                    S.dma("pool", bv_o[o0:o0 + 128, :], vo.t[:], vo.b, reads=[vo.b])
                recs.append(S.stop_rec())
            for i0 in range(0, nsub, 2):
                S.interleave(recs[i0:i0 + 2])
            n = nsub * 128
            S.dma("act", KBT[:, :, kslot0:kslot0 + n].rearrange("h d t -> d h t"), stg_kb.t[:, :, 0:n], stg_kb.b,
                  reads=[stg_kb.b])
            S.dma("act", VB[kslot0:kslot0 + n, :].rearrange("(s p) c -> p s c", p=128), stg_vb.t[:, 0:nsub, :],
                  stg_vb.b, reads=[stg_vb.b])
            if own_row0 is not None:
                S.dma("act", QBT[:, :, own_row0:own_row0 + n].rearrange("h d t -> d h t"), stg_qb.t[:, :, 0:n],
                      stg_qb.b, reads=[stg_qb.b])

        band_tile(xl, 0, 4, KB_HA)
        if not SMALL:
            band_tile(xl, 512, 4, KB_HB)
        for t in range(1 if SMALL else 2):
            band_tile(xo, t * 512, 4, KB_A + t * 512, own_row0=t * 512)
        for t in range(0 if SMALL else 2):
            band_tile(xo, 1024 + t * 512, 4, KB_B + t * 512, own_row0=1024 + t * 512)
        band_tile(xo, 2048, 1, KB_S, own_row0=2048)
        S.barrier()
        sb.release(pm0)

    att = {"i": 0, "u": 0, "pend": deque()}
    SKEW = 3

    def att_flush():
        while att["pend"]:
            att["pend"].popleft()()

    def attn_unit(qparts, ktiles, ncols, ot_dst, pT, obufs, rec):
        nt = len(ktiles)
        u = att["u"]
        att["u"] += 1
        bo, bl = (4, 5) if u % 2 == 0 else (6, 7)
        obuf = obufs[u % len(obufs)]
        rc = rec["rc"][u % len(rec["rc"])]
        for ti, kt in enumerate(ktiles):
            nk, c0 = kt["nk"], kt["c0"]
            w = ncols - c0
            bi = att["i"] % 4
            p = pT[att["i"] % len(pT)]
            tmp = rec["tmp"][att["i"] % len(rec["tmp"])] if "tmp" in rec else None
            att["i"] += 1
            sbk = PB[bi]
            groups = []
            np_ = len(qparts)
            multi = kt.get("multi")
            if multi is None:
                for pi, ((qa, qb_), (ka, kb_)) in enumerate(zip(qparts, kt["kparts"])):
                    groups.append((pbank(bi)[0:nk, 0:w], ka, qa[:, c0:ncols], pi == 0, pi == np_ - 1))
                S.op("pe", f_mm(groups), reads=[q[1] for q in qparts] + [k[1] for k in kt["kparts"]], writes=[sbk.b])
            else:
                kbufs = []
                for m_, (kps, _) in enumerate(multi):
                    for pi, ((qa, qb_), (ka, kb_)) in enumerate(zip(qparts, kps)):
                        groups.append((pbank(bi)[0:nk, m_ * ncols:(m_ + 1) * ncols], ka, qa[:, 0:ncols], pi == 0, pi == np_ - 1))
                        kbufs.append(kb_)
                S.op("pe", f_mm(groups), reads=[q[1] for q in qparts] + kbufs, writes=[sbk.b])
                w = len(multi) * ncols
            src = pbank(bi)[0:nk, 0:w]
            if kt.get("btab") is not None:
                ba, bb = kt["btab"]
                S.op("dve", f_stt(tmp.t[0:nk, 0:w], src, kt["scale"], ba, ALU.mult, ALU.add),
                     reads=[sbk.b, bb], writes=[tmp.b])
                S.op("act", f_act(p.t[0:nk, 0:w], tmp.t[0:nk, 0:w], AF.Exp, bias=kt["bias"]),
                     reads=[tmp.b, flags.b], writes=[p.b])
            else:
                S.op("act", f_act(p.t[0:nk, 0:w], src, AF.Exp, bias=kt["bias"], scale=kt["scale"]),
                     reads=[sbk.b, flags.b], writes=[p.b])
            if kt.get("zero") is not None:
                r0, r1, cc0, cc1 = kt["zero"]
                S.op("pool", f_memset(p.t[r0:r1, cc0:cc1], 0.0), writes=[p.b])

            def stage_c(kt=kt, p=p, ti=ti, nk=nk, c0=c0, w=w, multi=multi):
                if multi is None:
                    va, vb_ = kt["v"]
                    S.op("pe", f_mm([(pbank(bo)[:, c0:ncols], va, p.t[0:nk, 0:w], ti == 0, ti == nt - 1),
                                     (pbank(bl)[:, c0:ncols], ones.t[0:nk, :], p.t[0:nk, 0:w], ti == 0, ti == nt - 1)]),
                         reads=[p.b, vb_, ones.b], writes=[PB[bo].b, PB[bl].b])
                else:
                    mms, vbufs = [], []
                    nm_ = len(multi)
                    for m_, (_, (va, vb_)) in enumerate(multi):
                        pm = p.t[0:nk, m_ * ncols:(m_ + 1) * ncols]
                        first = (ti == 0 and m_ == 0)
                        last = (ti == nt - 1 and m_ == nm_ - 1)
                        mms.append((pbank(bo)[:, 0:ncols], va, pm, first, last))
                        mms.append((pbank(bl)[:, 0:ncols], ones.t[0:nk, :], pm, first, last))
                        vbufs.append(vb_)
                    S.op("pe", f_mm(mms), reads=[p.b, ones.b] + vbufs, writes=[PB[bo].b, PB[bl].b])
                if ti == nt - 1:
                    S.op("dve", f_recip(rc.t[:, 0:ncols], pbank(bl)[:, 0:ncols]), reads=[PB[bl].b], writes=[rc.b])
                    S.op("dve", f_tt(obuf.t[:, 0:ncols], pbank(bo)[:, 0:ncols], rc.t[:, 0:ncols], ALU.mult),
                         reads=[PB[bo].b, rc.b], writes=[obuf.b])
                    S.dma("act", ot_dst, obuf.t[:, 0:ncols], obuf.b, reads=[obuf.b])

            att["pend"].append(stage_c)
            while len(att["pend"]) > SKEW:
                att["pend"].popleft()()

    if "p2" in PHASES:
        NK2 = NHIST + 2 * SEG
        ktn = [sb.alloc([128, NK2], BF16, "ktn%d" % i) for i in range(2)]
        ktp = [sb.alloc([64, NK2], BF16, "ktp%d" % i) for i in range(2)]
        vv = [sb.alloc([128, NK2 // 128, 128], BF16, "vv%d" % i) for i in range(2)]
        qtn = [sb.alloc([128, 2 * SEG], BF16, "qtn%d" % i) for i in range(2)]
        qtp = [sb.alloc([64, 2 * SEG], BF16, "qtp%d" % i) for i in range(2)]
        pT = [sb.alloc([128, 512], BF16, "pT%d" % i) for i in range(6)]
        ob = [sb.alloc([128, 512], BF16, "ob%d" % i) for i in range(2)]
        rec = {"rc": [sb.alloc([128, 512], F32, "rc%d" % i) for i in range(2)]}

        def load_head(h):
            i = h % 2
            for c in range(0, NK2, 2304):
                pass
            S.dma("sp", ktn[i].t[:], KTn[h, :, 0:NK2], ktn[i].b, writes=[ktn[i].b])
            S.dma("sp", ktp[i].t[:], KTp[h, :, 0:NK2], ktp[i].b, writes=[ktp[i].b])
            vsrc = Vs[0:NK2, h * 128:(h + 1) * 128].rearrange("(t p) d -> p t d", p=128)
            dma_split("sp", [(vv[i].t[:, t0:t0 + 8, :], vsrc[:, t0:t0 + 8, :]) for t0 in range(0, NK2 // 128, 8)], vv[i].b)
            S.dma("sp", qtn[i].t[:], QTn[h, :, 0:2 * SEG], qtn[i].b, writes=[qtn[i].b])
            S.dma("sp", qtp[i].t[:], QTp[h, :, 0:2 * SEG], qtp[i].b, writes=[qtp[i].b])

        DO3 = "p3" in PHASES
        NKB2 = KB_S
        if DO3:
            kbt = [sb.alloc([128, NKB2], BF16, "kbt%d" % i) for i in range(2)]
            vbt = [sb.alloc([128, NKB2 // 128, 128], BF16, "vbt%d" % i) for i in range(2)]
            qbt = [sb.alloc([128, 2 * SEG], BF16, "qbt%d" % i) for i in range(2)]
            btab = [sb.alloc([128, 5, 128], F32, "btab%d" % i) for i in range(2)]
            bmk = sb.alloc([128, 5, 128], F32, "bmk")
            pTb = [sb.alloc([128, 512], BF16, "pTb%d" % i) for i in range(6)]
            obb = [sb.alloc([128, 128], BF16, "obb%d" % i) for i in range(2)]
            recb = {"rc": [sb.alloc([128, 128], F32, "rcb%d" % i) for i in range(2)],
                    "tmp": [sb.alloc([128, 512], F32, "tmpb%d" % i) for i in range(4)]}
            S.dma("sp", bmk.t[:], bmask.rearrange("(t p) q -> p t q", p=128), bmk.b, writes=[bmk.b])

        def load_bhead(h):
            i = h % 2
            S.dma("sp", kbt[i].t[:], KBT[h, :, 0:NKB2], kbt[i].b, writes=[kbt[i].b])
            vsrc = VB[0:NKB2, h * 128:(h + 1) * 128].rearrange("(t p) d -> p t d", p=128)
            dma_split("sp", [(vbt[i].t[:, t0:t0 + 8, :], vsrc[:, t0:t0 + 8, :]) for t0 in range(0, NKB2 // 128, 8)], vbt[i].b)
            S.dma("sp", qbt[i].t[:], QBT[h, :, 0:2 * SEG], qbt[i].b, writes=[qbt[i].b])
            S.dma("sp", btab[i].t[:], bbias[h].rearrange("(t p) q -> p t q", p=128), btab[i].b, writes=[btab[i].b])
            S.op("pool", f_tt(btab[i].t[:], btab[i].t[:], bmk.t[:], ALU.add), reads=[btab[i].b, bmk.b], writes=[btab[i].b])

        def band_units(h, lo, hi):
            i = h % 2
            for un in range(lo, hi):
                seg, pr = un // 8, un % 8
                kbase = KB_HA if seg == 0 else KB_HB
                q0 = seg * SEG + pr * 128
                qparts = [(qbt[i].t[:, q0:q0 + 128], qbt[i].b)]
                tiles = []

                def one(t):
                    k0 = kbase + pr * 128 + t * 128
                    is_halo = (pr * 128 + t * 128) < 512
                    fcol = 10 if (seg == 0 and is_halo) else 11
                    return dict(kparts=[(kbt[i].t[:, k0:k0 + 128], kbt[i].b)],
                                v=(vbt[i].t[:, k0 // 128, :], vbt[i].b), nk=128, c0=0,
                                bias=flags.t[:, fcol:fcol + 1], scale=SCALE_B,
                                btab=(btab[i].t[:, t, :], btab[i].b)), fcol

                singles = [one(t) for t in range(5)]
                if len(set(fc for (_, fc) in singles[0:4])) == 1:
                    fc = singles[0][1]
                    tiles.append(dict(multi=[(d_["kparts"], d_["v"]) for (d_, _) in singles[0:4]], nk=128, c0=0,
                                      bias=flags.t[:, fc:fc + 1], scale=SCALE_B,
                                      btab=(btab[i].t[:, 0:4, :].rearrange("p t q -> p (t q)"), btab[i].b)))
                    tiles.append(singles[4][0])
                else:
                    tiles = [d_ for (d_, _) in singles]
                attn_unit(qparts, tiles, 128, OT[H + h, :, q0:q0 + 128], pTb, obb, recb)

        load_head(0)
        if DO3:
            load_bhead(0)
        ui = 0
        H2 = 1 if SMALL else H
        for h in range(H2):
            att_flush()
            if h + 1 < H2:
                load_head(h + 1)
            i = h % 2
            for qb_ in range(4):
                seg = qb_ // 2
                half = qb_ % 2
                q0 = qb_ * 512
                qparts = [(qtn[i].t[:, q0:q0 + 512], qtn[i].b), (qtp[i].t[:, q0:q0 + 512], qtp[i].b)]
                tiles = []
                nslots = 3 if seg == 0 else 7
                for s_ in range(nslots):
                    fcol = s_ if seg == 0 else 3 + s_
                    for t in range(8):
                        k0 = s_ * SEG + t * 128
                        tiles.append(dict(kparts=[(ktn[i].t[:, k0:k0 + 128], ktn[i].b), (ktp[i].t[:, k0:k0 + 128], ktp[i].b)],
                                          v=(vv[i].t[:, k0 // 128, :], vv[i].b), nk=128, c0=0,
                                          bias=flags.t[:, fcol:fcol + 1], scale=SCALE_A))
                own0 = NHIST + seg * SEG
                for t in range(4 * half + 4):
                    k0 = own0 + t * 128
                    rel = t - 4 * half
                    c0 = max(0, rel) * 128
                    z = (64, 128, 0, 64) if rel >= 0 else None
                    tiles.append(dict(kparts=[(ktn[i].t[:, k0:k0 + 128], ktn[i].b), (ktp[i].t[:, k0:k0 + 128], ktp[i].b)],
                                      v=(vv[i].t[:, k0 // 128, :], vv[i].b), nk=128, c0=c0,
                                      bias=flags.t[:, 11:12], scale=SCALE_A, zero=z))
                attn_unit(qparts, tiles, 512, OT[h, :, q0:q0 + 512], pT, ob, rec)
                bg_step(2)
                ui += 1
        if DO3:
            for h in range(H2):
                att_flush()
                if h + 1 < H2:
                    load_bhead(h + 1)
                band_units(h, 0, 16)
        att_flush()
        S.barrier()
        sb.release(pm0)

    if "p4" in PHASES:
        cb = [sb.alloc([128, 1024], BF16, "cb%d" % i) for i in range(2)]
        stc = [sb.alloc([128, 8, 128], BF16, "stc%d" % i) for i in range(2)]
        for r in range(8):
            c_ = cb[r % 2]
            S.dma("pool", c_.t[:], c_bk[r * 128:(r + 1) * 128, :], c_.b, writes=[c_.b])
            pn = pbank_bf(r % 2).rearrange("p (h t) -> p h t", h=8)
            S.op("pe", f_tr([(pn[:, h, :], c_.t[:, h * 128:(h + 1) * 128]) for h in range(8)], ident.t[:]),
                 reads=[c_.b, ident.b], writes=[PB[r % 2].b])
            S.op("dve", f_copy(stc[r % 2].t[:], pn), reads=[PB[r % 2].b], writes=[stc[r % 2].b])
            S.dma("sp", KBT_c[:, :, r * 128:(r + 1) * 128].rearrange("h d t -> d h t"), stc[r % 2].t[:], stc[r % 2].b,
                  reads=[stc[r % 2].b])
        S.barrier()
        sb.release(pm0)
        wo_pref = sb.alloc([128, 16, D], BF16, "wo")
        load_w(wo_pref, w_o, 16, step=2)
        pm_p4 = sb.mark()
        NKS = PAST + 32
        ktn = [sb.alloc([128, NKS], BF16, "sktn%d" % i) for i in range(2)]
        ktp = [sb.alloc([64, NKS], BF16, "sktp%d" % i) for i in range(2)]
        vv = [sb.alloc([128, 9, 128], BF16, "svv%d" % i) for i in range(2)]
        qtn = [sb.alloc([128, 32], BF16, "sqtn%d" % i) for i in range(2)]
        qtp = [sb.alloc([64, 32], BF16, "sqtp%d" % i) for i in range(2)]
        kbt = [sb.alloc([128, LB + 32], BF16, "skbt%d" % i) for i in range(2)]
        vbt = [sb.alloc([128, 4, 128], BF16, "svbt%d" % i) for i in range(2)]
        vbn = [sb.alloc([32, 128], BF16, "svbn%d" % i) for i in range(2)]
        qbt = [sb.alloc([128, 32], BF16, "sqbt%d" % i) for i in range(2)]
        btab = [sb.alloc([128, 5, 32], F32, "sbtab%d" % i) for i in range(2)]
        pT = [sb.alloc([128, 32], BF16, "spT%d" % i) for i in range(6)]
        ob = [sb.alloc([128, 32], BF16, "sob%d" % i) for i in range(2)]
        rec = {"rc": [sb.alloc([128, 32], F32, "src%d" % i) for i in range(2)],
               "tmp": [sb.alloc([128, 32], F32, "stmp%d" % i) for i in range(4)]}
        units = [(sbi, h) for sbi in range(1 if SMALL else 2) for h in range(1 if SMALL else H)]

        def p4_load(n):
            sbi, h = units[n]
            i = n % 2
            qrow = 2048 + sbi * 32
            knew = NHIST + 2048 + sbi * 32
            kbnew = KB_S + sbi * 32
            dma_split("sp", [(ktn[i].t[:, 0:PAST], KTn_c[h, :, sbi * PAST:(sbi + 1) * PAST]),
                             (ktn[i].t[:, PAST:NKS], KTn[h, :, knew:knew + 32])], ktn[i].b)
            dma_split("sp", [(ktp[i].t[:, 0:PAST], KTp_c[h, :, sbi * PAST:(sbi + 1) * PAST]),
                             (ktp[i].t[:, PAST:NKS], KTp[h, :, knew:knew + 32])], ktp[i].b)
            dma_split("sp", [(vv[i].t[:, 0:8, :], V_c[sbi * PAST:(sbi + 1) * PAST, h * 128:(h + 1) * 128].rearrange(
                "(t p) d -> p t d", p=128)), (vv[i].t[0:32, 8, :], Vs[knew:knew + 32, h * 128:(h + 1) * 128])], vv[i].b)
            S.dma("sp", qtn[i].t[:], QTn[h, :, qrow:qrow + 32], qtn[i].b, writes=[qtn[i].b])
            S.dma("sp", qtp[i].t[:], QTp[h, :, qrow:qrow + 32], qtp[i].b, writes=[qtp[i].b])
            dma_split("sp", [(kbt[i].t[:, 0:LB], KBT_c[h, :, sbi * LB:(sbi + 1) * LB]),
                             (kbt[i].t[:, LB:LB + 32], KBT[h, :, kbnew:kbnew + 32])], kbt[i].b)
            S.dma("pool", vbt[i].t[:], c_bv[sbi * LB:(sbi + 1) * LB, h * 128:(h + 1) * 128].rearrange(
                "(t p) d -> p t d", p=128), vbt[i].b, writes=[vbt[i].b])
            S.dma("sp", vbn[i].t[:], VB[kbnew:kbnew + 32, h * 128:(h + 1) * 128], vbn[i].b, writes=[vbn[i].b])
            S.dma("sp", qbt[i].t[:], QBT[h, :, qrow:qrow + 32], qbt[i].b, writes=[qbt[i].b])
            S.dma("sp", btab[i].t[:], sbias[h].rearrange("(t p) q -> p t q", p=128), btab[i].b, writes=[btab[i].b])

        def p4_compute(n):
            sbi, h = units[n]
            i = n % 2
            qrow = 2048 + sbi * 32
            qparts = [(qtn[i].t[:, :], qtn[i].b), (qtp[i].t[:, :], qtp[i].b)]
            tiles = []
            for t in range(9):
                nk = 128 if t < 8 else 32
                k0 = t * 128
                tiles.append(dict(kparts=[(ktn[i].t[:, k0:k0 + nk], ktn[i].b), (ktp[i].t[:, k0:k0 + nk], ktp[i].b)],
                                  v=(vv[i].t[0:nk, t, :], vv[i].b), nk=nk, c0=0, bias=flags.t[0:nk, 11:12],
                                  scale=SCALE_A))
            attn_unit(qparts, tiles, 32, OT[h, :, qrow:qrow + 32], pT, ob, rec)
            qparts = [(qbt[i].t[:, :], qbt[i].b)]
            tiles = []
            for t in range(5):
                nk = 128 if t < 4 else 32
                k0 = t * 128
                vsrc = (vbt[i].t[:, t, :], vbt[i].b) if t < 4 else (vbn[i].t[:, :], vbn[i].b)
                tiles.append(dict(kparts=[(kbt[i].t[:, k0:k0 + nk], kbt[i].b)], v=vsrc,
                                  nk=nk, c0=0, bias=flags.t[0:nk, 11:12], scale=SCALE_B,
                                  btab=(btab[i].t[0:nk, t, :], btab[i].b)))
            attn_unit(qparts, tiles, 32, OT[H + h, :, qrow:qrow + 32], pT, ob, rec)

        p4_load(0)
        for n in range(len(units)):
            att_flush()
            if n + 1 < len(units):
                p4_load(n + 1)
            p4_compute(n)
        att_flush()
        S.barrier()
        sb.release(pm_p4)

    if "p5" in PHASES:
        bg_step(1000)
        if "p4" in PHASES:
            wo = wo_pref
        else:
            wo = sb.alloc([128, 16, D], BF16, "wo")
            load_w(wo, w_o, 16, step=2)
        otb = [sb.alloc([128, 16, 512], BF16, "otb%d" % i) for i in range(2)]
        xr = [sb.alloc([128, D], F32, "xr%d" % i) for i in range(2)]
        hb = [sb.alloc([128, D], F32, "hb%d" % i) for i in range(2)]
        t5a = [(1536, 4), (2048, 1)] if SMALL else [(0, 4), (512, 4), (1024, 4), (1536, 4), (2048, 1)]
        sc = 0
        for ti_, (tok0, nsub) in enumerate(t5a):
            ob_ = otb[ti_ % 2]
            n = nsub * 128
            osrc = OT[:, :, tok0:tok0 + n].rearrange("h d t -> d h t")
            dma_split("sp", [(ob_.t[:, h0:h0 + 8, 0:n], osrc[:, h0:h0 + 8, :]) for h0 in (0, 8)], ob_.b)
            for s_ in range(nsub):
                i = sc % 2
                sc += 1
                r0 = tok0 + s_ * 128
                S.dma("sp", xr[i].t[:], xo[r0:r0 + 128, :], xr[i].b, writes=[xr[i].b])
                for g in range(4):
                    bk = (sc * 4 + g) % 8
                    S.op("pe", f_mm([(pbank(bk), ob_.t[:, k, s_ * 128:(s_ + 1) * 128], wo.t[:, k, g * 512:(g + 1) * 512],
                                      k == 0, k == 15) for k in range(16)]), reads=[ob_.b, wo.b], writes=[PB[bk].b])
                    S.op("dve", f_tt(hb[i].t[:, g * 512:(g + 1) * 512], pbank(bk), xr[i].t[:, g * 512:(g + 1) * 512], ALU.add),
                         reads=[PB[bk].b, xr[i].b], writes=[hb[i].b])
                S.dma("act", Hs[r0:r0 + 128, :], hb[i].t[:], hb[i].b, reads=[hb[i].b])
        S.barrier()
        sb.release(pm0)

        TMAX = 640
        uT = sb.alloc([128, 64, TMAX], BF16, "uT")
        hn2T = sb.alloc([128, 16, TMAX], BF16, "hn2T")
        hx = [sb.alloc([128, D], F32, "hx%d" % i) for i in range(2)]
        junk = sb.alloc([128, D], BF16, "junk5")
        st = sb.alloc([128, 8], F32, "st5")
        xn = sb.alloc([128, D], BF16, "xn5")
        wup = [sb.alloc([128, 16, 256], BF16, "wup%d" % i) for i in range(3)]
        wdn = [sb.alloc([128, 16, 256], BF16, "wdn%d" % i) for i in range(3)]
        rl = [sb.alloc([128, 512], F32, "rl%d" % i) for i in range(2)]
        hres = [sb.alloc([128, 256], F32, "hres%d" % i) for i in range(2)]
        yo = [sb.alloc([128, 256], F32, "yo%d" % i) for i in range(2)]
        tiles5 = [(1536, 5)] if SMALL else [(0, 4), (512, 4), (1024, 4), (1536, 5)]
        cx = 0
        wi = 0
        di = 0
        ri = 0
        for (tok0, nsub) in tiles5:
            T = nsub * 128
            for s_ in range(nsub):
                xb = hx[cx % 2]
                cx += 1
                r0 = tok0 + s_ * 128
                S.dma("sp", xb.t[:], Hs[r0:r0 + 128, :], xb.b, writes=[xb.b])
                S.op("pool", f_memset(st.t[:, 0:1], 0.0), writes=[st.b])
                S.op("dve", f_stt(junk.t[:], xb.t[:], 1.0, xb.t[:], ALU.mult, ALU.mult, accum=st.t[:, 0:1]),
                     reads=[xb.b], writes=[junk.b, st.b])
                rstd_from_ss(st.t[:, 0:1], st.b, 1, D, st)
                S.op("act", f_act(xn.t[:], xb.t[:], AF.Copy, scale=st.t[:, 0:1]), reads=[xb.b, st.b], writes=[xn.b])
                for half in range(2):
                    pv = pbank_bf(half).rearrange("p (k t) -> p k t", k=8)
                    S.op("pe", f_tr([(pv[:, k, :], xn.t[:, (half * 8 + k) * 128:(half * 8 + k + 1) * 128])
                                     for k in range(8)], ident.t[:]), reads=[xn.b, ident.b], writes=[PB[half].b])
                    S.op("dve", f_tt(hn2T.t[:, half * 8:half * 8 + 8, s_ * 128:(s_ + 1) * 128], pv,
                                     nffn.t[:, half * 8:half * 8 + 8].unsqueeze(2).to_broadcast([128, 8, 128]), ALU.mult),
                         reads=[PB[half].b, nffn.b], writes=[hn2T.b])
            for gcol in range(DFF // 256):
                wb = wup[wi % 3]
                wi += 1
                S.dma("sp", wb.t[:], WU[gcol], wb.b, writes=[wb.b])
                for m2 in range(2):
                    m = gcol * 2 + m2
                    pieces = [(0, min(T, 512))] + ([(512, T)] if T > 512 else [])
                    for (a, b_) in pieces:
                        bk = 2 + (ri % 4)
                        r_ = rl[ri % 2]
                        ri += 1
                        S.op("pe", f_mm([(pbank(bk)[:, 0:b_ - a], wb.t[:, k, m2 * 128:(m2 + 1) * 128], hn2T.t[:, k, a:b_],
                                          k == 0, k == 15) for k in range(16)]), reads=[wb.b, hn2T.b], writes=[PB[bk].b])
                        S.op("act", f_act(r_.t[:, 0:b_ - a], pbank(bk)[:, 0:b_ - a], AF.Relu), reads=[PB[bk].b],
                             writes=[r_.b])
                        S.op("pool" if (ri % 2) else "dve", f_tt(uT.t[:, m, a:b_], r_.t[:, 0:b_ - a], r_.t[:, 0:b_ - a],
                                                                  ALU.mult), reads=[r_.b], writes=[uT.b])
            for gcol in range(D // 256):
                for pc in range(4):
                    wb = wdn[di % 3]
                    di += 1
                    S.dma("sp", wb.t[:], WD[gcol][:, pc * 16:(pc + 1) * 16, :], wb.b, writes=[wb.b])
                    for s_ in range(nsub):
                        bk = 2 + s_ if s_ < 4 else 0
                        S.op("pe", f_mm([(pbank(bk)[:, 0:256], uT.t[:, pc * 16 + k, s_ * 128:(s_ + 1) * 128], wb.t[:, k, :],
                                          pc == 0 and k == 0, pc == 3 and k == 15) for k in range(16)]),
                             reads=[wb.b, uT.b], writes=[PB[bk].b])
                for s_ in range(nsub):
                    bk = 2 + s_ if s_ < 4 else 0
                    r0 = tok0 + s_ * 128
                    hr = hres[(gcol * 5 + s_) % 2]
                    y_ = yo[(gcol * 5 + s_) % 2]
                    S.dma("act", hr.t[:], Hs[r0:r0 + 128, gcol * 256:(gcol + 1) * 256], hr.b, writes=[hr.b])
                    S.op("dve", f_tt(y_.t[:], pbank(bk)[:, 0:256], hr.t[:], ALU.add), reads=[PB[bk].b, hr.b], writes=[y_.b])
                    S.dma("act", y_o[r0:r0 + 128, gcol * 256:(gcol + 1) * 256], y_.t[:], y_.b, reads=[y_.b])
        S.barrier()
        sb.release(pm0)

    S.barrier()
    with nc.Block() as block:
        @block.tensor
        def _(e):
            S.emit("pe", e)

        @block.scalar
        def _(e):
            S.emit("act", e)

        @block.vector
        def _(e):
            S.emit("dve", e)

        @block.gpsimd
        def _(e):
            S.emit("pool", e)

        @block.sync
        def _(e):
            S.emit("sp", e)
    return nc


def _rope_table(pos):
    inv = 1.0 / (10000.0 ** (np.arange(0, 64, 2, dtype=np.float32) / 64.0))
    ang = pos.astype(np.float32)[:, None] * inv[None, :].astype(np.float32)
    return np.concatenate([np.cos(ang), np.sin(ang)], axis=1).astype(np.float32)


def _host_inputs(inp):
    f32 = np.float32
    xp = np.asarray(inp["x_prompt"], f32)
    xs = np.asarray(inp["x_sample"], f32)
    rb = np.asarray(inp["rel_bias"], f32)[0]
    kl = np.arange(640)[:, None]
    ql = np.arange(128)[None, :]
    idx = np.clip(ql - kl + 512, -128, 128) + 128
    bbias = np.ascontiguousarray(rb[:, idx])
    qc = ql // 64
    kc = kl // 64
    allowed = (kc <= qc + 8) & (kc >= qc)
    bmask = np.where(allowed, 0.0, NEGM).astype(f32)
    ks = np.arange(640)[:, None]
    ts = np.arange(32)[None, :]
    rel_s = np.where(ks < 512, 512 + ts - ks, ts - (ks - 512))
    sidx = np.clip(rel_s, -128, 128) + 128
    sbias = np.ascontiguousarray(rb[:, sidx])
    shared = {
        "bbias": bbias, "bmask": bmask, "sbias": sbias, "ident": np.eye(128, dtype=f32),
        "rope_h": _rope_table(np.arange(NHIST)), "rope_c": _rope_table(np.arange(PAST)),
    }
    for k in ("norm_mix", "norm_ffn", "w_in", "g_cq", "w_uq", "g_ckv", "w_uk", "w_uv", "g_qa", "g_ka", "g_qb", "g_kb",
              "w_o", "w_up", "w_down"):
        shared[k] = np.ascontiguousarray(np.asarray(inp[k], f32)[0])
    cck = np.asarray(inp["cache_mla_ckv"], f32)[0]
    ckp = np.asarray(inp["cache_mla_kpe"], f32)[0]
    cbk = np.asarray(inp["cache_band_k"], f32)[0].reshape(16, LB, 1024)
    cbv = np.asarray(inp["cache_band_v"], f32)[0].reshape(16, LB, 1024)
    maps = []
    for c in range(8):
        b, j = c // 4, c % 4
        a0, b0 = j * SEG, (7 - j) * SEG
        m = dict(shared)
        m["xh"] = np.ascontiguousarray(xp[b, 0:NHIST])
        xo = np.zeros((NOWN, D), f32)
        xo[0:SEG] = xp[b, a0:a0 + SEG]
        xo[SEG:2 * SEG] = xp[b, b0:b0 + SEG]
        xo[2048:2080] = xs[2 * c]
        xo[2080:2112] = xs[2 * c + 1]
        m["xo"] = xo
        xl = np.zeros((NHALO, D), f32)
        if a0 >= 512:
            xl[0:512] = xp[b, a0 - 512:a0]
        xl[512:1024] = xp[b, b0 - 512:b0]
        m["xl"] = xl
        m["c_ckv"] = np.ascontiguousarray(cck[2 * c:2 * c + 2].reshape(2 * PAST, 256))
        m["c_kpe"] = np.ascontiguousarray(ckp[2 * c:2 * c + 2].reshape(2 * PAST, 64))
        m["c_bk"] = np.ascontiguousarray(cbk[2 * c:2 * c + 2].reshape(2 * LB, 1024))
        m["c_bv"] = np.ascontiguousarray(cbv[2 * c:2 * c + 2].reshape(2 * LB, 1024))
        pos_o = np.concatenate([np.arange(a0, a0 + SEG), np.arange(b0, b0 + SEG), PAST + np.arange(32),
                                PAST + np.arange(32), np.zeros(64, np.int64)])
        m["rope_o"] = _rope_table(pos_o)
        fl = np.zeros((128, 16), f32)
        for s_ in range(3):
            fl[:, s_] = 0.0 if s_ < j else NEGM
        for s_ in range(7):
            fl[:, 3 + s_] = 0.0 if s_ < 7 - j else NEGM
        fl[:, 10] = NEGM if j == 0 else 0.0
        m["flg"] = fl
        maps.append(m)
    return maps


_NC_CACHE = {}


def kernel(**inputs):
    maps = _host_inputs(inputs)
    if "nc" not in _NC_CACHE:
        _NC_CACHE["nc"] = build_nc()
    nc = _NC_CACHE["nc"]
    res = run_bass_kernel_spmd(nc, maps, core_ids=list(range(8)))
    R = res.results
    f32 = np.float32
    S_ = 8192
    y_p = np.zeros((2, S_, D), f32)
    ckv_p = np.zeros((1, 2, S_, 256), f32)
    kpe_p = np.zeros((1, 2, S_, 64), f32)
    bk_p = np.zeros((1, 2, 512, 8, 128), f32)
    bv_p = np.zeros((1, 2, 512, 8, 128), f32)
    y_s = np.zeros((16, 32, D), f32)
    ckv_s = np.zeros((1, 16, 32, 256), f32)
    kpe_s = np.zeros((1, 16, 32, 64), f32)
    bk_s = np.zeros((1, 16, 32, 8, 128), f32)
    bv_s = np.zeros((1, 16, 32, 8, 128), f32)
    for c in range(8):
        b, j = c // 4, c % 4
        a0, b0 = j * SEG, (7 - j) * SEG
        r = R[c]
        for (dst, src, w) in ((y_p, "y_o", None), (ckv_p[0], "ckv_o", None), (kpe_p[0], "kpe_o", None)):
            dst[b, a0:a0 + SEG] = r[src][0:SEG]
            dst[b, b0:b0 + SEG] = r[src][SEG:2 * SEG]
        if j == 0:
            bk_p[0, b] = r["bk_o"][1536:2048].reshape(512, 8, 128)
            bv_p[0, b] = r["bv_o"][1536:2048].reshape(512, 8, 128)
        for sbi in range(2):
            rows = slice(2048 + 32 * sbi, 2048 + 32 * sbi + 32)
            y_s[2 * c + sbi] = r["y_o"][rows]
            ckv_s[0, 2 * c + sbi] = r["ckv_o"][rows]
            kpe_s[0, 2 * c + sbi] = r["kpe_o"][rows]
            bk_s[0, 2 * c + sbi] = r["bk_o"][rows].reshape(32, 8, 128)
            bv_s[0, 2 * c + sbi] = r["bv_o"][rows].reshape(32, 8, 128)
    return (y_p, y_s, ckv_p, kpe_p, bk_p, bv_p, ckv_s, kpe_s, bk_s, bv_s)
```

```python
import numpy as np
from collections import deque
from contextlib import ExitStack
import concourse.bass as bass
import concourse.mybir as mybir
from concourse.bass_utils import run_bass_kernel_spmd

F32 = mybir.dt.float32
BF16 = mybir.dt.bfloat16
AF = mybir.ActivationFunctionType
ALU = mybir.AluOpType
AX = mybir.AxisListType

D = 2048
H = 8
SEG = 1024
NHIST = 7 * SEG
NOWN = 2 * SEG + 128
NHALO = 1024
NKM = NHIST + NOWN
NKB = 512 + SEG + 512 + SEG + 128
KB_HA, KB_A, KB_HB, KB_B, KB_S = 0, 512, 1536, 2048, 3072
PAST = 1024
LB = 512
EPS = 1e-6
SCALE_A = 192 ** -0.5
SCALE_B = 128 ** -0.5
NEGM = -30000.0
DFF = 8192
IN_COLS = 3904

PHASES = {"p1", "p2", "p3", "p4", "p5"}
SMALL = False


class Buf:
    __slots__ = ("name", "w", "r", "sem", "semval", "semname", "kind", "alt")

    def __init__(self, name):
        self.name = name
        self.w = None
        self.r = {}
        self.sem = None
        self.semval = 0
        self.semname = None
        self.kind = None
        self.alt = None


class Tn:
    def __init__(self, t, name):
        self.t = t
        self.b = Buf(name)


ENGS = ("pe", "act", "dve", "pool", "sp")


class Sched:
    def __init__(self, nc, es):
        self.nc = nc
        self.es = es
        self.q = {e: [] for e in ENGS}
        self.esem = {e: es.enter_context(nc.semaphore("s_" + e)) for e in ENGS}
        self.ecnt = {e: 0 for e in ENGS}
        self.seen = {e: {} for e in ENGS}
        self.dbufs = []
        self.nbuf = 0
        self.rec = None
        self.semfinal = {}
        self.freesems = []
        self.nsem = 0

    def _waits(self, eng, reads, writes):
        need = {}

        def add(tok):
            if tok is None:
                return
            k, h, v = tok
            if k not in need or need[k][1] < v:
                need[k] = (h, v)

        for b in reads:
            add(b.w)
        for b in writes:
            add(b.w)
            for tok in b.r.values():
                add(tok)
        for k, (h, v) in need.items():
            if self.seen[eng].get(k, 0) >= v:
                continue
            self.seen[eng][k] = v
            self.q[eng].append(("w", h, v))

    def _mark(self, tok, reads, writes):
        for b in reads:
            b.r[tok[0]] = tok
        for b in writes:
            b.w = tok
            b.r = {}

    def start_rec(self):
        self.rec = []

    def stop_rec(self):
        r = self.rec
        self.rec = None
        return r

    def interleave(self, lists):
        idx = [0] * len(lists)
        live = True
        while live:
            live = False
            for li, l in enumerate(lists):
                if idx[li] < len(l):
                    it = l[idx[li]]
                    idx[li] += 1
                    live = True
                    if it[0] == "op":
                        self.op(*it[1:])
                    else:
                        self.dma(*it[1:])

    def op(self, eng, fn, reads=(), writes=()):
        if self.rec is not None:
            self.rec.append(("op", eng, fn, tuple(reads), tuple(writes)))
            return None
        self._waits(eng, reads, writes)
        self.ecnt[eng] += 1
        tok = ("e_" + eng, self.esem[eng], self.ecnt[eng])
        self.q[eng].append(("o", fn, self.esem[eng], 1))
        if eng == "pe":
            self.seen[eng]["e_pe"] = self.ecnt[eng]
        self._mark(tok, reads, writes)
        return tok

    def dma(self, queue, out_ap, in_ap, semb, reads=(), writes=(), slow=False):
        if self.rec is not None:
            self.rec.append(("dma", queue, out_ap, in_ap, semb, tuple(reads), tuple(writes), slow))
            return None
        self._waits(queue, reads, writes)
        kind = "sw" if queue == "pool" else "hw"
        if semb.sem is not None and semb.kind != kind:
            if semb.alt is None:
                semb.alt = Buf(semb.name + "_alt")
            semb = semb.alt
        if semb.sem is None:
            fl = [x for x in self.freesems if x[0] == kind]
            semb.kind = kind
            if fl:
                self.freesems.remove(fl[-1])
                _, semb.semname, semb.sem, semb.semval = fl[-1]
            else:
                semb.semname = "d%d" % self.nsem
                self.nsem += 1
                semb.sem = self.es.enter_context(self.nc.semaphore(semb.semname))
            self.dbufs.append(semb)
        semb.semval += 16
        self.semfinal[semb.semname] = (semb.sem, semb.semval)
        tok = (semb.semname, semb.sem, semb.semval)

        def fn(e, o=out_ap, i=in_ap):
            if slow:
                return e.dma_start(out=o, in_=i, allow_slow_non_contiguous=True)
            return e.dma_start(out=o, in_=i)

        self.q[queue].append(("o", fn, semb.sem, 16))
        self._mark(tok, reads, writes)
        return tok

    def barrier(self):
        toks = []
        for e in ENGS:
            if self.ecnt[e] > 0:
                toks.append(("e_" + e, self.esem[e], self.ecnt[e]))
        for k, (h, v) in self.semfinal.items():
            toks.append((k, h, v))
        for e in ENGS:
            for (k, h, v) in toks:
                if self.seen[e].get(k, 0) >= v:
                    continue
                self.seen[e][k] = v
                self.q[e].append(("w", h, v))
        for b in self.dbufs:
            self.freesems.append((b.kind, b.semname, b.sem, b.semval))
            b.sem = None
        self.dbufs = []

    def emit(self, eng, e):
        for it in self.q[eng]:
            if it[0] == "w":
                e.wait_ge(it[1], it[2])
            elif it[0] == "o":
                ins = it[1](e)
                ins.then_inc(it[2], it[3])
            else:
                e.nop().then_inc(it[1], 1)


class SBAlloc:
    def __init__(self, nc):
        self.nc = nc
        self.base = (nc.sbuf_base + 63) // 64 * 64
        self.top = nc.sbuf_top
        self.off = self.base
        self.n = 0

    def alloc(self, shape, dt, name=None):
        per = 1
        for x in shape[1:]:
            per *= x
        per *= 2 if dt == BF16 else 4
        per = (per + 63) // 64 * 64
        assert self.off + per <= self.top, ("SBUF overflow", name, self.off, per, self.top)
        self.n += 1
        nm = "%s_%d" % (name or "t", self.n)
        t = self.nc.alloc_sbuf_tensor_at(nm, list(shape), dt, offset=self.off)
        self.off += per
        return Tn(t, nm)

    def mark(self):
        return self.off

    def release(self, m):
        self.off = m


def f_mm(groups):
    def fn(e):
        ins = None
        for (o, l, r, st, sp) in groups:
            ins = e.matmul(o, l, r, start=st, stop=sp)
        return ins
    return fn


def f_tr(items, ident):
    def fn(e):
        ins = None
        for (o, i) in items:
            ins = e.transpose(o, i, ident)
        return ins
    return fn


def f_tt(out, in0, in1, op):
    return lambda e: e.tensor_tensor(out=out, in0=in0, in1=in1, op=op)


def f_ts(out, in0, s1, s2, op0, op1=None):
    if op1 is None:
        return lambda e: e.tensor_scalar(out=out, in0=in0, scalar1=s1, scalar2=None, op0=op0)
    return lambda e: e.tensor_scalar(out=out, in0=in0, scalar1=s1, scalar2=s2, op0=op0, op1=op1)


def f_stt(out, in0, scalar, in1, op0, op1, accum=None):
    if accum is None:
        return lambda e: e.scalar_tensor_tensor(out=out, in0=in0, scalar=scalar, in1=in1, op0=op0, op1=op1)
    return lambda e: e.scalar_tensor_tensor(out=out, in0=in0, scalar=scalar, in1=in1, op0=op0, op1=op1,
                                            accum_out=accum)


def f_act(out, in_, func, bias=None, scale=None):
    kw = {}
    if bias is not None:
        kw["bias"] = bias
    if scale is not None:
        kw["scale"] = scale
    return lambda e: e.activation(out=out, in_=in_, func=func, **kw)


def f_red(out, in_):
    return lambda e: e.tensor_reduce(out=out, in_=in_, axis=AX.X, op=ALU.add)


def f_copy(out, in_):
    def fn(e):
        if hasattr(e, "tensor_copy"):
            return e.tensor_copy(out, in_)
        return e.activation(out=out, in_=in_, func=AF.Copy)
    return fn


def f_memset(ap, v):
    return lambda e: e.memset(ap, v)


def f_recip(out, in_):
    return lambda e: e.reciprocal(out, in_)


def build_nc(debug=False):
    nc = bass.Bass("TRN2", target_bir_lowering=False)
    es = ExitStack()

    def din(name, shape, dt=F32):
        return nc.dram_tensor(name, list(shape), dt, kind="ExternalInput").ap()

    def dout(name, shape, dt=F32):
        return nc.dram_tensor(name, list(shape), dt, kind="ExternalOutput").ap()

    def dscr(name, shape, dt=BF16):
        kind = "ExternalOutput" if debug else "Internal"
        return nc.dram_tensor(name, list(shape), dt, kind=kind).ap()

    xh = din("xh", [NHIST, D])
    xo = din("xo", [NOWN, D])
    xl = din("xl", [NHALO, D])
    c_ckv = din("c_ckv", [2 * PAST, 256])
    c_kpe = din("c_kpe", [2 * PAST, 64])
    c_bk = din("c_bk", [2 * LB, 1024])
    c_bv = din("c_bv", [2 * LB, 1024])
    rope_h = din("rope_h", [NHIST, 64])
    rope_o = din("rope_o", [NOWN, 64])
    rope_c = din("rope_c", [PAST, 64])
    flg = din("flg", [128, 16])
    bbias = din("bbias", [H, 640, 128])
    bmask = din("bmask", [640, 128])
    sbias = din("sbias", [H, 640, 32])
    ident_d = din("ident", [128, 128])
    norm_mix = din("norm_mix", [D])
    norm_ffn = din("norm_ffn", [D])
    w_in = din("w_in", [D, IN_COLS])
    g_cq = din("g_cq", [512])
    w_uq = din("w_uq", [512, 1536])
    g_ckv = din("g_ckv", [256])
    w_uk = din("w_uk", [256, 1024])
    w_uv = din("w_uv", [256, 1024])
    g_qa = din("g_qa", [192])
    g_ka = din("g_ka", [192])
    g_qb = din("g_qb", [128])
    g_kb = din("g_kb", [128])
    w_o = din("w_o", [D, D])
    w_up = din("w_up", [D, DFF])
    w_down = din("w_down", [DFF, D])

    y_o = dout("y_o", [NOWN, D])
    ckv_o = dout("ckv_o", [NOWN, 256])
    kpe_o = dout("kpe_o", [NOWN, 64])
    bk_o = dout("bk_o", [NOWN, 1024])
    bv_o = dout("bv_o", [NOWN, 1024])

    KTn = dscr("KTn", [H, 128, NKM])
    KTp = dscr("KTp", [H, 64, NKM])
    Vs = dscr("Vs", [NKM, 1024])
    KTn_c = dscr("KTn_c", [H, 128, 2 * PAST])
    KTp_c = dscr("KTp_c", [H, 64, 2 * PAST])
    V_c = dscr("V_c", [2 * PAST, 1024])
    QTn = dscr("QTn", [H, 128, NOWN])
    QTp = dscr("QTp", [H, 64, NOWN])
    QBT = dscr("QBT", [H, 128, NOWN])
    KBT = dscr("KBT", [H, 128, NKB])
    VB = dscr("VB", [NKB, 1024])
    KBT_c = dscr("KBT_c", [H, 128, 2 * LB])
    OT = dscr("OT", [2 * H, 128, NOWN])
    Hs = dscr("Hs", [NOWN, D], F32)

    WU = dscr("WU", [DFF // 256, 128, 16, 256])
    WD = dscr("WD", [D // 256, 128, 64, 256])
    S = Sched(nc, es)
    sb = SBAlloc(nc)
    psum = nc.alloc_psum_tensor("ps", [128, 8, 512], F32)
    PB = [Tn(None, "pb%d" % i) for i in range(8)]

    def pbank(i):
        return psum[:, i, :]

    def pbank_bf(i):
        return psum[:, i, :].bitcast(BF16)

    ident = sb.alloc([128, 128], BF16, "ident")
    ones = sb.alloc([128, 128], BF16, "ones")
    epsc = sb.alloc([128, 1], F32, "eps")
    zero_c = sb.alloc([128, 1], F32, "zero")
    flags = sb.alloc([128, 16], F32, "flags")
    nmix = sb.alloc([128, 16], F32, "nmix")
    nffn = sb.alloc([128, 16], F32, "nffn")
    gqa_n = sb.alloc([128, 1], F32, "gqa_n")
    gka_n = sb.alloc([128, 1], F32, "gka_n")
    gqb_p = sb.alloc([128, 1], F32, "gqb_p")
    gqa_pe = sb.alloc([128, 64], F32, "gqa_pe")
    gka_pe = sb.alloc([128, 64], F32, "gka_pe")
    gkb_b = sb.alloc([128, 128], F32, "gkb_b")
    gcq_b = sb.alloc([128, 512], F32, "gcq_b")
    gckv_b = sb.alloc([128, 256], F32, "gckv_b")

    S.dma("pool", ident.t[:], ident_d, ident.b, writes=[ident.b])
    S.op("pool", f_memset(ones.t[:], 1.0), writes=[ones.b])
    S.op("pool", f_memset(epsc.t[:], EPS), writes=[epsc.b])
    S.op("pool", f_memset(zero_c.t[:], 0.0), writes=[zero_c.b])
    S.dma("sp", flags.t[:], flg, flags.b, writes=[flags.b])
    S.dma("sp", nmix.t[:], norm_mix.rearrange("(k p) -> p k", p=128), nmix.b, writes=[nmix.b], slow=True)
    S.dma("sp", nffn.t[:], norm_ffn.rearrange("(k p) -> p k", p=128), nffn.b, writes=[nffn.b], slow=True)
    S.dma("sp", gqa_n.t[:], g_qa[0:128].rearrange("(p o) -> p o", o=1), gqa_n.b, writes=[gqa_n.b])
    S.dma("sp", gka_n.t[:], g_ka[0:128].rearrange("(p o) -> p o", o=1), gka_n.b, writes=[gka_n.b])
    S.dma("sp", gqb_p.t[:], g_qb.rearrange("(p o) -> p o", o=1), gqb_p.b, writes=[gqb_p.b])
    S.dma("sp", gqa_pe.t[:], g_qa[128:192].partition_broadcast(128), gqa_pe.b, writes=[gqa_pe.b])
    S.dma("sp", gka_pe.t[:], g_ka[128:192].partition_broadcast(128), gka_pe.b, writes=[gka_pe.b])
    S.dma("sp", gkb_b.t[:], g_kb.partition_broadcast(128), gkb_b.b, writes=[gkb_b.b])
    S.dma("sp", gcq_b.t[:], g_cq.partition_broadcast(128), gcq_b.b, writes=[gcq_b.b])
    S.dma("sp", gckv_b.t[:], g_ckv.partition_broadcast(128), gckv_b.b, writes=[gckv_b.b])
    bg = deque()
    cvb = Buf("cv")
    w_up_v = w_up.rearrange("(k p) c -> p k c", p=128)
    w_dn_v = w_down.rearrange("(k p) c -> p k c", p=128)
    if "p5" in PHASES:
        for g_ in range(DFF // 256):
            bg.append(lambda g_=g_: S.dma("pool", WU[g_], w_up_v[:, :, g_ * 256:(g_ + 1) * 256], cvb))
        for g_ in range(D // 256):
            for pc_ in range(4):
                bg.append(lambda g_=g_, pc_=pc_: S.dma("pool", WD[g_][:, pc_ * 16:(pc_ + 1) * 16, :],
                                                       w_dn_v[:, pc_ * 16:(pc_ + 1) * 16, g_ * 256:(g_ + 1) * 256], cvb))

    def bg_step(n=1):
        for _ in range(n):
            if bg:
                bg.popleft()()

    consts = [ident.b, ones.b, epsc.b, zero_c.b, flags.b, nmix.b, nffn.b, gqa_n.b, gka_n.b, gqb_p.b,
              gqa_pe.b, gka_pe.b, gkb_b.b, gcq_b.b, gckv_b.b]
    S.barrier()
    pm0 = sb.mark()

    def load_w(dst, src_ap, kchunks, step=4):
        v = src_ap.rearrange("(k p) c -> p k c", p=128)
        for k0 in range(0, kchunks, step):
            k1 = min(kchunks, k0 + step)
            S.dma("pool", dst.t[:, k0:k1, :], v[:, k0:k1, :], dst.b, writes=[])
        dst.b.w = (dst.b.semname, dst.b.sem, dst.b.semval)
        dst.b.r = {}

    def dma_split(queue, pieces, buf, reads=()):
        tok = None
        for n_, (o_, i_) in enumerate(pieces):
            tok = S.dma(queue, o_, i_, buf, reads=reads, writes=[buf] if n_ == 0 else [])
        buf.w = tok
        buf.r = {}

    def rstd_from_ss(ss_ap, ss_buf, n, dim, stat):
        S.op("act", f_act(ss_ap, ss_ap, AF.Ln, bias=epsc.t[:, 0:1], scale=1.0 / dim), reads=[ss_buf], writes=[ss_buf])
        S.op("act", f_act(ss_ap, ss_ap, AF.Exp, scale=-0.5), reads=[ss_buf], writes=[ss_buf])

    if "p1" in PHASES:
        cnt = {"s": 0}

        def alloc_sets(specs, nch=2):
            sets = [{nm: sb.alloc(shape, dt, "%s_%d" % (nm, p)) for (nm, shape, dt) in specs} for p in range(nch)]
            for p, z in enumerate(sets):
                z["bk"] = [4 * p + n for n in range(4)] if nch == 2 else [2 * p, 2 * p + 1, 2 * p, 2 * p + 1]
            return sets

        def front(x_rows, nm, Z, par):
            xb, st, xn, hnT = Z["xt"], Z["st"], Z["xn"], Z["hnT"]
            S.dma("sp", xb.t[:], x_rows, xb.b, writes=[xb.b])
            S.op("pool", f_memset(st.t[:, 0:1], 0.0), writes=[st.b])
            S.op("dve", f_stt(junk.t[:], xb.t[:], 1.0, xb.t[:], ALU.mult, ALU.mult, accum=st.t[:, 0:1]),
                 reads=[xb.b], writes=[junk.b, st.b])
            rstd_from_ss(st.t[:, 0:1], st.b, 1, D, st)
            S.op("act", f_act(xn.t[:], xb.t[:], AF.Copy, scale=st.t[:, 0:1]), reads=[xb.b, st.b], writes=[xn.b])
            for half in range(2):
                bk = Z["bk"][half]
                pv = pbank_bf(bk).rearrange("p (k t) -> p k t", k=8)
                S.op("pe", f_tr([(pv[:, k, :], xn.t[:, (half * 8 + k) * 128:(half * 8 + k + 1) * 128])
                                 for k in range(8)], ident.t[:]), reads=[xn.b, ident.b], writes=[PB[bk].b])
                S.op("dve", f_tt(hnT.t[:, half * 8:half * 8 + 8, :], pv,
                                 nm.t[:, half * 8:half * 8 + 8].unsqueeze(2).to_broadcast([128, 8, 128]), ALU.mult),
                     reads=[PB[bk].b, nm.b], writes=[hnT.b])

        def head_rstd(src3, srcbuf, nh, dh, ss_ap, stb, sqb, extra=None, div=None):
            S.op("pool", f_tt(sqb.t[:, 0:nh, 0:dh], src3, src3, ALU.mult), reads=[srcbuf], writes=[sqb.b])
            S.op("dve", f_red(ss_ap, sqb.t[:, 0:nh, 0:dh]), reads=[sqb.b], writes=[stb.b])
            if extra is not None:
                S.op("dve", f_ts(ss_ap, ss_ap, extra, None, ALU.add), reads=[stb.b], writes=[stb.b])
            rstd_from_ss(ss_ap, stb.b, nh, div or dh, stb)

        def rope(dst3, src3, srcbuf, dstbuf, csb, nh, rp):
            c = csb.t[:, 0:32].unsqueeze(1).to_broadcast([128, nh, 32])
            sn = csb.t[:, 32:64].unsqueeze(1).to_broadcast([128, nh, 32])
            x1 = src3[:, :, 0:32]
            x2 = src3[:, :, 32:64]
            t = [rp.t[:, i, 0:nh, :] for i in range(4)]
            S.op("pool", f_tt(t[0], x1, c, ALU.mult), reads=[srcbuf, csb.b], writes=[rp.b])
            S.op("pool", f_tt(t[1], x2, sn, ALU.mult), reads=[srcbuf, csb.b], writes=[rp.b])
            S.op("pool", f_tt(t[2], x1, sn, ALU.mult), reads=[srcbuf, csb.b], writes=[rp.b])
            S.op("pool", f_tt(t[3], x2, c, ALU.mult), reads=[srcbuf, csb.b], writes=[rp.b])
            S.op("pool", f_tt(dst3[:, :, 0:32], t[0], t[1], ALU.subtract), reads=[rp.b], writes=[dstbuf])
            S.op("pool", f_tt(dst3[:, :, 32:64], t[2], t[3], ALU.add), reads=[rp.b], writes=[dstbuf])

        win_m = sb.alloc([128, 16, 832], BF16, "win_m")
        wuk = sb.alloc([128, 2, 1024], BF16, "wuk")
        wuv = sb.alloc([128, 2, 1024], BF16, "wuv")
        load_w(win_m, w_in[:, 0:832], 16)
        load_w(wuk, w_uk, 2)
        load_w(wuv, w_uv, 2)
        junk = sb.alloc([128, D], BF16, "junk")
        stg_kn = sb.alloc([128, 8, 512], BF16, "skn")
        stg_kp = sb.alloc([64, 8, 512], BF16, "skp")
        stg_v = sb.alloc([128, 4, 1024], BF16, "sv")
        csl = [sb.alloc([128, 64], F32, "cs%d" % i) for i in range(8)]
        pm1 = sb.mark()
        KV_SPECS = [("xt", [128, D], F32), ("st", [128, 32], F32), ("xn", [128, D], BF16),
                    ("hnT", [128, 16, 128], BF16), ("ckvf", [128, 256], F32), ("kpef", [128, 64], F32),
                    ("ckvb", [128, 256], BF16), ("smT", [128, 6, 128], BF16), ("kf", [128, 8, 128], F32),
                    ("kn", [128, 8, 192], BF16), ("kpg", [128, 64], F32), ("kpr", [128, 64], F32)]
        ZM4 = alloc_sets(KV_SPECS + [("sqk", [128, 8, 128], F32), ("rpk", [128, 4, 1, 32], F32)], nch=4)

        def kv_from_latent(Z, par, sub):
            ckv_f, kpe_f, csb, st = Z["ckvf"], Z["kpef"], Z["csb"], Z["st"]
            ckvb, smT, kf, kn, kpg, kpr = Z["ckvb"], Z["smT"], Z["kf"], Z["kn"], Z["kpg"], Z["kpr"]
            S.op("dve", f_copy(ckvb.t[:], ckv_f.t[:]), reads=[ckv_f.b], writes=[ckvb.b])
            b0, b1, b2, b3 = Z["bk"]
            b0, b1 = b2, b3
            pv = pbank_bf(b2).rearrange("p (k t) -> p k t", k=8)
            S.op("pe", f_tr([(pv[:, 4 + k, :], ckvb.t[:, k * 128:(k + 1) * 128]) for k in range(2)], ident.t[:]),
                 reads=[ckvb.b, ident.b], writes=[PB[b2].b])
            S.op("act", f_copy(smT.t[:, 4:6, :], pv[:, 4:6, :]), reads=[PB[b2].b], writes=[smT.b])
            for g in range(2):
                S.op("pe", f_mm([(pbank(b2 + g), smT.t[:, 4 + k, :], wuk.t[:, k, g * 512:(g + 1) * 512], k == 0, k == 1)
                                 for k in range(2)]), reads=[smT.b, wuk.b], writes=[PB[b2 + g].b])
            for g in range(2):
                S.op("act", f_copy(kf.t[:, g * 4:(g + 1) * 4, :], pbank(b2 + g).rearrange("p (h d) -> p h d", h=4)),
                     reads=[PB[b2 + g].b], writes=[kf.b])
            for g in range(2):
                S.op("pe", f_mm([(pbank(b0 + g), smT.t[:, 4 + k, :], wuv.t[:, k, g * 512:(g + 1) * 512], k == 0, k == 1)
                                 for k in range(2)]), reads=[smT.b, wuv.b], writes=[PB[b0 + g].b])
            for g in range(2):
                S.op("act", f_copy(stg_v.t[:, sub, g * 512:(g + 1) * 512], pbank(b0 + g)), reads=[PB[b0 + g].b],
                     writes=[stg_v.b])
            S.op("pool", f_memset(st.t[:, 1:2], 0.0), writes=[st.b])
            S.op("dve", f_stt(kpg.t[:], kpe_f.t[:], 1.0, kpe_f.t[:], ALU.mult, ALU.mult, accum=st.t[:, 1:2]),
                 reads=[kpe_f.b], writes=[kpg.b, st.b])
            head_rstd(kf.t[:, :, :], kf.b, 8, 128, st.t[:, 8:16], st, Z["sqk"], extra=st.t[:, 1:2], div=192)
            S.op("dve", f_tt(kn.t[:, :, 0:128], kf.t[:, :, :], st.t[:, 8:16].unsqueeze(2).to_broadcast([128, 8, 128]),
                             ALU.mult), reads=[kf.b, st.b], writes=[kn.b])
            S.op("pool", f_tt(kpg.t[:], kpe_f.t[:], gka_pe.t[:], ALU.mult), reads=[kpe_f.b, gka_pe.b], writes=[kpg.b])
            rope(kpr.t[:].rearrange("p (h d) -> p h d", h=1), kpg.t[:].rearrange("p (h d) -> p h d", h=1),
                 kpg.b, kpr.b, csb, 1, Z["rpk"])
            S.op("dve", f_tt(kn.t[:, :, 128:192], kpr.t[:].unsqueeze(1).to_broadcast([128, 8, 64]),
                             st.t[:, 8:16].unsqueeze(2).to_broadcast([128, 8, 64]), ALU.mult),
                 reads=[kpr.b, st.b], writes=[kn.b])
            pn = pbank_bf(b2).rearrange("p (h t) -> p h t", h=8)
            pp = pbank_bf(b3).rearrange("p (h t) -> p h t", h=8)
            S.op("pe", f_tr([(pn[:, h, :], kn.t[:, h, 0:128]) for h in range(8)], ident.t[:]),
                 reads=[kn.b, ident.b], writes=[PB[b2].b])
            S.op("pe", f_tr([(pp[0:64, h, :], kn.t[:, h, 128:192]) for h in range(8)], ident.t[:]),
                 reads=[kn.b, ident.b], writes=[PB[b3].b])
            S.op("act", f_act(stg_kn.t[:, :, sub * 128:(sub + 1) * 128], pn, AF.Copy, scale=gka_n.t[:, 0:1]),
                 reads=[PB[b2].b, gka_n.b], writes=[stg_kn.b])
            S.op("dve", f_copy(stg_kp.t[:, :, sub * 128:(sub + 1) * 128], pp[0:64, :, :]), reads=[PB[b3].b],
                 writes=[stg_kp.b])

        def q_from_cq(Z, par, sub):
            st, cqf, cqn, smT, qf, qn, csb = Z["st"], Z["cqf"], Z["cqn"], Z["smT"], Z["qf"], Z["qn"], Z["csb"]
            b0, b1, b2, b3 = Z["bk"]
            S.op("pool", f_memset(st.t[:, 2:3], 0.0), writes=[st.b])
            S.op("dve", f_stt(junk.t[:, 0:512], cqf.t[:], 1.0, cqf.t[:], ALU.mult, ALU.mult, accum=st.t[:, 2:3]),
                 reads=[cqf.b], writes=[junk.b, st.b])
            rstd_from_ss(st.t[:, 2:3], st.b, 1, 512, st)
            S.op("dve", f_stt(cqn.t[:], cqf.t[:], st.t[:, 2:3], gcq_b.t[:], ALU.mult, ALU.mult),
                 reads=[cqf.b, st.b, gcq_b.b], writes=[cqn.b])
            pv = pbank_bf(b0).rearrange("p (k t) -> p k t", k=8)
            S.op("pe", f_tr([(pv[:, k, :], cqn.t[:, k * 128:(k + 1) * 128]) for k in range(4)], ident.t[:]),
                 reads=[cqn.b, ident.b], writes=[PB[b0].b])
            S.op("act", f_copy(smT.t[:, 0:4, :], pv[:, 0:4, :]), reads=[PB[b0].b], writes=[smT.b])
            qf2 = qf.t[:].rearrange("p h d -> p (h d)")
            for g in range(3):
                bq = (b1, b0, b1)[g]
                S.op("pe", f_mm([(pbank(bq), smT.t[:, k, :], wuq.t[:, k, g * 512:(g + 1) * 512], k == 0, k == 3)
                                 for k in range(4)]), reads=[smT.b, wuq.b], writes=[PB[bq].b])
                S.op("act", f_copy(qf2[:, g * 512:(g + 1) * 512], pbank(bq)), reads=[PB[bq].b], writes=[qf.b])
            head_rstd(qf.t[:, :, :], qf.b, 8, 192, st.t[:, 16:24], st, Z["sq"])
            S.op("dve", f_tt(qn.t[:, :, 0:128], qf.t[:, :, 0:128], st.t[:, 16:24].unsqueeze(2).to_broadcast([128, 8, 128]),
                             ALU.mult), reads=[qf.b, st.b], writes=[qn.b])
            S.op("dve", f_tt(qf.t[:, :, 128:192], qf.t[:, :, 128:192],
                             st.t[:, 16:24].unsqueeze(2).to_broadcast([128, 8, 64]), ALU.mult),
                 reads=[qf.b, st.b], writes=[qf.b])
            S.op("pool", f_tt(qf.t[:, :, 128:192], qf.t[:, :, 128:192],
                              gqa_pe.t[:].unsqueeze(1).to_broadcast([128, 8, 64]), ALU.mult),
                 reads=[qf.b, gqa_pe.b], writes=[qf.b])
            rope(qn.t[:, :, 128:192], qf.t[:, :, 128:192], qf.b, qn.b, csb, 8, Z["rp"])
            pn = pbank_bf(b0).rearrange("p (h t) -> p h t", h=8)
            pp = pbank_bf(b1).rearrange("p (h t) -> p h t", h=8)
            S.op("pe", f_tr([(pn[:, h, :], qn.t[:, h, 0:128]) for h in range(8)], ident.t[:]),
                 reads=[qn.b, ident.b], writes=[PB[b0].b])
            S.op("pe", f_tr([(pp[0:64, h, :], qn.t[:, h, 128:192]) for h in range(8)], ident.t[:]),
                 reads=[qn.b, ident.b], writes=[PB[b1].b])
            S.op("act", f_act(stg_qn.t[:, :, sub * 128:(sub + 1) * 128], pn, AF.Copy, scale=gqa_n.t[:, 0:1]),
                 reads=[PB[b0].b, gqa_n.b], writes=[stg_qn.b])
            S.op("dve", f_copy(stg_qp.t[:, :, sub * 128:(sub + 1) * 128], pp[0:64, :, :]), reads=[PB[b1].b],
                 writes=[stg_qp.b])

        def mla_tile(xsrc, row0, nsub, ropesrc, rope0, kslot0, own_row0=None, from_cache=None, kdst=None, sets=None,
                     want_q=True):
            KTn_d, KTp_d, V_d = kdst
            recs = []
            recs2 = []
            for sub in range(nsub):
                S.start_rec()
                nch = len(sets)
                par = cnt["s"] % nch
                Z = dict(sets[par])
                Z["csb"] = csl[cnt["s"] % 8]
                cnt["s"] += 1
                b2, b3 = Z["bk"][2], Z["bk"][3]
                r0 = row0 + sub * 128
                csb, cf, kp, hnT = Z["csb"], Z["ckvf"], Z["kpef"], Z["hnT"]
                if from_cache is not None:
                    S.dma("sp", csb.t[:], ropesrc[rope0 + sub * 128:rope0 + (sub + 1) * 128, :], csb.b, writes=[csb.b])
                if from_cache is None:
                    front(xsrc[r0:r0 + 128, :], nmix, Z, par)
                    S.dma("sp", csb.t[:], ropesrc[rope0 + sub * 128:rope0 + (sub + 1) * 128, :], csb.b, writes=[csb.b])
                    if own_row0 is not None and want_q:
                        S.op("pe", f_mm([(pbank(b2), hnT.t[:, k, :], win_m.t[:, k, 0:512], k == 0, k == 15)
                                         for k in range(16)]), reads=[hnT.b, win_m.b], writes=[PB[b2].b])
                    S.op("pe", f_mm([(pbank(b3)[:, 0:320], hnT.t[:, k, :], win_m.t[:, k, 512:832], k == 0, k == 15)
                                     for k in range(16)]), reads=[hnT.b, win_m.b], writes=[PB[b3].b])
                    if own_row0 is not None and want_q:
                        S.op("act", f_copy(Z["cqf"].t[:], pbank(b2)), reads=[PB[b2].b], writes=[Z["cqf"].b])
                    S.op("act", f_copy(cf.t[:], pbank(b3)[:, 0:256]), reads=[PB[b3].b], writes=[cf.b])
                    S.op("act", f_copy(kp.t[:], pbank(b3)[:, 256:320]), reads=[PB[b3].b], writes=[kp.b])
                    st = Z["st"]
                    S.op("pool", f_memset(st.t[:, 3:4], 0.0), writes=[st.b])
                    S.op("dve", f_stt(junk.t[:, 0:256], cf.t[:], 1.0, cf.t[:], ALU.mult, ALU.mult,
                                      accum=st.t[:, 3:4]), reads=[cf.b], writes=[junk.b, st.b])
                    rstd_from_ss(st.t[:, 3:4], st.b, 1, 256, st)
                    S.op("dve", f_stt(cf.t[:], cf.t[:], st.t[:, 3:4], gckv_b.t[:], ALU.mult, ALU.mult),
                         reads=[cf.b, st.b, gckv_b.b], writes=[cf.b])
                    if own_row0 is not None:
                        o0 = own_row0 + sub * 128
                        S.dma("pool", ckv_o[o0:o0 + 128, :], cf.t[:], cf.b, reads=[cf.b])
                        S.dma("pool", kpe_o[o0:o0 + 128, :], kp.t[:], kp.b, reads=[kp.b])
                        if want_q:
                            recs.append(S.stop_rec())
                            S.start_rec()
                            q_from_cq(Z, par, sub)
                            recs2.append(S.stop_rec())
                            S.start_rec()
                else:
                    ca, ka = from_cache
                    S.dma("sp", cf.t[:], ca[r0:r0 + 128, :], cf.b, writes=[cf.b])
                    S.dma("sp", kp.t[:], ka[r0:r0 + 128, :], kp.b, writes=[kp.b])
                kv_from_latent(Z, par, sub)
                (recs2 if (own_row0 is not None and from_cache is None and want_q) else recs).append(S.stop_rec())
            npl = 2 if (own_row0 is not None and from_cache is None and want_q) else 0
            for i0 in range(0, nsub, nch):
                S.interleave(recs[i0:i0 + nch])
                if npl:
                    S.interleave(recs2[npl * i0:npl * (i0 + nch)])
            n = nsub * 128
            S.dma("act", KTn_d[:, :, kslot0:kslot0 + n].rearrange("h d t -> d h t"), stg_kn.t[:, :, 0:n], stg_kn.b,
                  reads=[stg_kn.b])
            S.dma("act", KTp_d[:, :, kslot0:kslot0 + n].rearrange("h d t -> d h t"), stg_kp.t[:, :, 0:n], stg_kp.b,
                  reads=[stg_kp.b])
            S.dma("act", V_d[kslot0:kslot0 + n, :].rearrange("(s p) c -> p s c", p=128), stg_v.t[:, 0:nsub, :],
                  stg_v.b, reads=[stg_v.b])
            if own_row0 is not None and want_q:
                S.dma("act", QTn[:, :, own_row0:own_row0 + n].rearrange("h d t -> d h t"), stg_qn.t[:, :, 0:n],
                      stg_qn.b, reads=[stg_qn.b])
                S.dma("act", QTp[:, :, own_row0:own_row0 + n].rearrange("h d t -> d h t"), stg_qp.t[:, :, 0:n],
                      stg_qp.b, reads=[stg_qp.b])

        kd = (KTn, KTp, Vs)
        for t in range(1 if SMALL else NHIST // 512):
            mla_tile(xh, t * 512, 4, rope_h, t * 512, t * 512, kdst=kd, sets=ZM4)
        if "p4" in PHASES:
            kdc = (KTn_c, KTp_c, V_c)
            for sbi in range(1 if SMALL else 2):
                for t in range(1 if SMALL else 2):
                    mla_tile(None, sbi * PAST + t * 512, 4, rope_c, t * 512, sbi * PAST + t * 512,
                             from_cache=(c_ckv, c_kpe), kdst=kdc, sets=ZM4)
        for t in range(1 if SMALL else 4):
            mla_tile(xo, t * 512, 4, rope_o, t * 512, NHIST + t * 512, own_row0=t * 512, kdst=kd, sets=ZM4, want_q=False)
        mla_tile(xo, 2048, 1, rope_o, 2048, NHIST + 2048, own_row0=2048, kdst=kd, sets=ZM4, want_q=False)
        S.barrier()
        sb.release(pm0)
        win_q = sb.alloc([128, 16, 512], BF16, "win_q")
        wuq = sb.alloc([128, 4, 1536], BF16, "wuq")
        load_w(win_q, w_in[:, 0:512], 16)
        load_w(wuq, w_uq, 4)
        junk = sb.alloc([128, D], BF16, "junkq")
        csl = [sb.alloc([128, 64], F32, "csq%d" % i) for i in range(8)]
        ZQ = alloc_sets([("xt", [128, D], F32), ("st", [128, 32], F32), ("xn", [128, D], BF16),
                         ("hnT", [128, 16, 128], BF16), ("cqn", [128, 512], BF16), ("cqf", [128, 512], F32),
                         ("smT", [128, 6, 128], BF16), ("qf", [128, 8, 192], F32), ("sq", [128, 8, 192], F32),
                         ("qn", [128, 8, 192], BF16), ("rp", [128, 4, 8, 32], F32)], nch=3)
        stg_qn = sb.alloc([128, 8, 384], BF16, "sqn")
        stg_qp = sb.alloc([64, 8, 384], BF16, "sqp")
        qgroups = [[0, 1, 2], [3], [16]] if SMALL else [[0, 1, 2], [3, 4, 5], [6, 7, 8], [9, 10, 11], [12, 13, 14], [15, 16]]
        for grp in qgroups:
            recs = []
            for k_, sidx in enumerate(grp):
                S.start_rec()
                Z = dict(ZQ[k_])
                Z["csb"] = csl[cnt["s"] % 8]
                cnt["s"] += 1
                r0 = sidx * 128
                front(xo[r0:r0 + 128, :], nmix, Z, k_)
                S.dma("sp", Z["csb"].t[:], rope_o[r0:r0 + 128, :], Z["csb"].b, writes=[Z["csb"].b])
                b2 = Z["bk"][2]
                S.op("pe", f_mm([(pbank(b2), Z["hnT"].t[:, k, :], win_q.t[:, k, :], k == 0, k == 15) for k in range(16)]),
                     reads=[Z["hnT"].b, win_q.b], writes=[PB[b2].b])
                S.op("act", f_copy(Z["cqf"].t[:], pbank(b2)), reads=[PB[b2].b], writes=[Z["cqf"].b])
                q_from_cq(Z, k_, k_)
                recs.append(S.stop_rec())
            S.interleave(recs)
            n = len(grp) * 128
            c0 = grp[0] * 128
            S.dma("act", QTn[:, :, c0:c0 + n].rearrange("h d t -> d h t"), stg_qn.t[:, :, 0:n], stg_qn.b, reads=[stg_qn.b])
            S.dma("act", QTp[:, :, c0:c0 + n].rearrange("h d t -> d h t"), stg_qp.t[:, :, 0:n], stg_qp.b, reads=[stg_qp.b])
        S.barrier()
        sb.release(pm0)

        win_b = sb.alloc([128, 16, 3072], BF16, "win_b")
        load_w(win_b, w_in[:, 832:3904], 16, step=2)
        junk = sb.alloc([128, D], BF16, "junkb")
        ZB = alloc_sets([("xt", [128, D], F32), ("st", [128, 32], F32), ("xn", [128, D], BF16),
                         ("hnT", [128, 16, 128], BF16), ("bf", [128, 8, 128], F32), ("sq", [128, 8, 128], F32),
                         ("bn", [128, 8, 128], BF16), ("kbo", [128, 8, 128], F32), ("vbo", [128, 1024], F32)])
        stg_qb = sb.alloc([128, 8, 512], BF16, "sqb")
        stg_kb = sb.alloc([128, 8, 512], BF16, "skb")
        stg_vb = sb.alloc([128, 4, 1024], BF16, "svb")

        def band_tile(xsrc, row0, nsub, kslot0, own_row0=None):
            recs = []
            for sub in range(nsub):
                S.start_rec()
                par = cnt["s"] % 2
                cnt["s"] += 1
                Z = ZB[par]
                b0, b1, b2, b3 = Z["bk"]
                hnT, st, bf, bn, ko, vo = Z["hnT"], Z["st"], Z["bf"], Z["bn"], Z["kbo"], Z["vbo"]
                r0 = row0 + sub * 128
                front(xsrc[r0:r0 + 128, :], nmix, Z, par)
                if own_row0 is not None:
                    for g in range(2):
                        S.op("pe", f_mm([(pbank(b2 + g), hnT.t[:, k, :], win_b.t[:, k, g * 512:(g + 1) * 512], k == 0, k == 15)
                                         for k in range(16)]), reads=[hnT.b, win_b.b], writes=[PB[b2 + g].b])
                    for g in range(2):
                        S.op("act", f_copy(bf.t[:, g * 4:(g + 1) * 4, :], pbank(b2 + g).rearrange("p (h d) -> p h d", h=4)),
                             reads=[PB[b2 + g].b], writes=[bf.b])
                    head_rstd(bf.t[:, :, :], bf.b, 8, 128, st.t[:, 8:16], st, Z["sq"])
                    S.op("dve", f_tt(bn.t[:, :, :], bf.t[:, :, :], st.t[:, 8:16].unsqueeze(2).to_broadcast([128, 8, 128]),
                                     ALU.mult), reads=[bf.b, st.b], writes=[bn.b])
                    pn = pbank_bf(b0).rearrange("p (h t) -> p h t", h=8)
                    S.op("pe", f_tr([(pn[:, h, :], bn.t[:, h, :]) for h in range(8)], ident.t[:]),
                         reads=[bn.b, ident.b], writes=[PB[b0].b])
                    S.op("act", f_act(stg_qb.t[:, :, sub * 128:(sub + 1) * 128], pn, AF.Copy, scale=gqb_p.t[:, 0:1]),
                         reads=[PB[b0].b, gqb_p.b], writes=[stg_qb.b])
                for g in range(2):
                    S.op("pe", f_mm([(pbank(b2 + g), hnT.t[:, k, :], win_b.t[:, k, 1024 + g * 512:1024 + (g + 1) * 512],
                                      k == 0, k == 15) for k in range(16)]), reads=[hnT.b, win_b.b], writes=[PB[b2 + g].b])
                for g in range(2):
                    S.op("act", f_copy(bf.t[:, g * 4:(g + 1) * 4, :], pbank(b2 + g).rearrange("p (h d) -> p h d", h=4)),
                         reads=[PB[b2 + g].b], writes=[bf.b])
                for g in range(2):
                    S.op("pe", f_mm([(pbank(b0 + g), hnT.t[:, k, :], win_b.t[:, k, 2048 + g * 512:2048 + (g + 1) * 512],
                                      k == 0, k == 15) for k in range(16)]), reads=[hnT.b, win_b.b], writes=[PB[b0 + g].b])
                head_rstd(bf.t[:, :, :], bf.b, 8, 128, st.t[:, 16:24], st, Z["sq"])
                S.op("dve", f_tt(bf.t[:, :, :], bf.t[:, :, :], st.t[:, 16:24].unsqueeze(2).to_broadcast([128, 8, 128]),
                                 ALU.mult), reads=[bf.b, st.b], writes=[bf.b])
                S.op("pool", f_tt(ko.t[:, :, :], bf.t[:, :, :], gkb_b.t[:].unsqueeze(1).to_broadcast([128, 8, 128]),
                                  ALU.mult), reads=[bf.b, gkb_b.b], writes=[ko.b])
                S.op("dve", f_copy(bn.t[:, :, :], ko.t[:, :, :]), reads=[ko.b], writes=[bn.b])
                for g in range(2):
                    S.op("act", f_copy(vo.t[:, g * 512:(g + 1) * 512], pbank(b0 + g)), reads=[PB[b0 + g].b], writes=[vo.b])
                pn = pbank_bf(b2).rearrange("p (h t) -> p h t", h=8)
                S.op("pe", f_tr([(pn[:, h, :], bn.t[:, h, :]) for h in range(8)], ident.t[:]),
                     reads=[bn.b, ident.b], writes=[PB[b2].b])
                S.op("act", f_copy(stg_kb.t[:, :, sub * 128:(sub + 1) * 128], pn), reads=[PB[b2].b],
                     writes=[stg_kb.b])
                S.op("pool", f_copy(stg_vb.t[:, sub, :], vo.t[:]), reads=[vo.b], writes=[stg_vb.b])
                if own_row0 is not None:
                    o0 = own_row0 + sub * 128
                    S.dma("pool", bk_o[o0:o0 + 128, :], ko.t[:].rearrange("p h d -> p (h d)"), ko.b, reads=[ko.b])
                    S.dma("pool", bv_o[o0:o0 + 128, :], vo.t[:], vo.b, reads=[vo.b])
                recs.append(S.stop_rec())
            for i0 in range(0, nsub, 2):
                S.interleave(recs[i0:i0 + 2])
            n = nsub * 128
            S.dma("act", KBT[:, :, kslot0:kslot0 + n].rearrange("h d t -> d h t"), stg_kb.t[:, :, 0:n], stg_kb.b,
                  reads=[stg_kb.b])
            S.dma("act", VB[kslot0:kslot0 + n, :].rearrange("(s p) c -> p s c", p=128), stg_vb.t[:, 0:nsub, :],
                  stg_vb.b, reads=[stg_vb.b])
            if own_row0 is not None:
                S.dma("act", QBT[:, :, own_row0:own_row0 + n].rearrange("h d t -> d h t"), stg_qb.t[:, :, 0:n],
                      stg_qb.b, reads=[stg_qb.b])

        band_tile(xl, 0, 4, KB_HA)
        if not SMALL:
            band_tile(xl, 512, 4, KB_HB)
        for t in range(1 if SMALL else 2):
            band_tile(xo, t * 512, 4, KB_A + t * 512, own_row0=t * 512)
        for t in range(0 if SMALL else 2):
            band_tile(xo, 1024 + t * 512, 4, KB_B + t * 512, own_row0=1024 + t * 512)
        band_tile(xo, 2048, 1, KB_S, own_row0=2048)
        S.barrier()
        sb.release(pm0)

    att = {"i": 0, "u": 0, "pend": deque()}
    SKEW = 3

    def att_flush():
        while att["pend"]:
            att["pend"].popleft()()

    def attn_unit(qparts, ktiles, ncols, ot_dst, pT, obufs, rec):
        nt = len(ktiles)
        u = att["u"]
        att["u"] += 1
        bo, bl = (4, 5) if u % 2 == 0 else (6, 7)
        obuf = obufs[u % len(obufs)]
        rc = rec["rc"][u % len(rec["rc"])]
        for ti, kt in enumerate(ktiles):
            nk, c0 = kt["nk"], kt["c0"]
            w = ncols - c0
            bi = att["i"] % 4
            p = pT[att["i"] % len(pT)]
            tmp = rec["tmp"][att["i"] % len(rec["tmp"])] if "tmp" in rec else None
            att["i"] += 1
            sbk = PB[bi]
            groups = []
            np_ = len(qparts)
            multi = kt.get("multi")
            if multi is None:
                for pi, ((qa, qb_), (ka, kb_)) in enumerate(zip(qparts, kt["kparts"])):
                    groups.append((pbank(bi)[0:nk, 0:w], ka, qa[:, c0:ncols], pi == 0, pi == np_ - 1))
                S.op("pe", f_mm(groups), reads=[q[1] for q in qparts] + [k[1] for k in kt["kparts"]], writes=[sbk.b])
            else:
                kbufs = []
                for m_, (kps, _) in enumerate(multi):
                    for pi, ((qa, qb_), (ka, kb_)) in enumerate(zip(qparts, kps)):
                        groups.append((pbank(bi)[0:nk, m_ * ncols:(m_ + 1) * ncols], ka, qa[:, 0:ncols], pi == 0, pi == np_ - 1))
                        kbufs.append(kb_)
                S.op("pe", f_mm(groups), reads=[q[1] for q in qparts] + kbufs, writes=[sbk.b])
                w = len(multi) * ncols
            src = pbank(bi)[0:nk, 0:w]
            if kt.get("btab") is not None:
                ba, bb = kt["btab"]
                S.op("dve", f_stt(tmp.t[0:nk, 0:w], src, kt["scale"], ba, ALU.mult, ALU.add),
                     reads=[sbk.b, bb], writes=[tmp.b])
                S.op("act", f_act(p.t[0:nk, 0:w], tmp.t[0:nk, 0:w], AF.Exp, bias=kt["bias"]),
                     reads=[tmp.b, flags.b], writes=[p.b])
            else:
                S.op("act", f_act(p.t[0:nk, 0:w], src, AF.Exp, bias=kt["bias"], scale=kt["scale"]),
                     reads=[sbk.b, flags.b], writes=[p.b])
            if kt.get("zero") is not None:
                r0, r1, cc0, cc1 = kt["zero"]
                S.op("pool", f_memset(p.t[r0:r1, cc0:cc1], 0.0), writes=[p.b])

            def stage_c(kt=kt, p=p, ti=ti, nk=nk, c0=c0, w=w, multi=multi):
                if multi is None:
                    va, vb_ = kt["v"]
                    S.op("pe", f_mm([(pbank(bo)[:, c0:ncols], va, p.t[0:nk, 0:w], ti == 0, ti == nt - 1),
                                     (pbank(bl)[:, c0:ncols], ones.t[0:nk, :], p.t[0:nk, 0:w], ti == 0, ti == nt - 1)]),
                         reads=[p.b, vb_, ones.b], writes=[PB[bo].b, PB[bl].b])
                else:
                    mms, vbufs = [], []
                    nm_ = len(multi)
                    for m_, (_, (va, vb_)) in enumerate(multi):
                        pm = p.t[0:nk, m_ * ncols:(m_ + 1) * ncols]
                        first = (ti == 0 and m_ == 0)
                        last = (ti == nt - 1 and m_ == nm_ - 1)
                        mms.append((pbank(bo)[:, 0:ncols], va, pm, first, last))
                        mms.append((pbank(bl)[:, 0:ncols], ones.t[0:nk, :], pm, first, last))
                        vbufs.append(vb_)
                    S.op("pe", f_mm(mms), reads=[p.b, ones.b] + vbufs, writes=[PB[bo].b, PB[bl].b])
                if ti == nt - 1:
                    S.op("dve", f_recip(rc.t[:, 0:ncols], pbank(bl)[:, 0:ncols]), reads=[PB[bl].b], writes=[rc.b])
                    S.op("dve", f_tt(obuf.t[:, 0:ncols], pbank(bo)[:, 0:ncols], rc.t[:, 0:ncols], ALU.mult),
                         reads=[PB[bo].b, rc.b], writes=[obuf.b])
                    S.dma("act", ot_dst, obuf.t[:, 0:ncols], obuf.b, reads=[obuf.b])

            att["pend"].append(stage_c)
            while len(att["pend"]) > SKEW:
                att["pend"].popleft()()

    if "p2" in PHASES:
        NK2 = NHIST + 2 * SEG
        ktn = [sb.alloc([128, NK2], BF16, "ktn%d" % i) for i in range(2)]
        ktp = [sb.alloc([64, NK2], BF16, "ktp%d" % i) for i in range(2)]
        vv = [sb.alloc([128, NK2 // 128, 128], BF16, "vv%d" % i) for i in range(2)]
        qtn = [sb.alloc([128, 2 * SEG], BF16, "qtn%d" % i) for i in range(2)]
        qtp = [sb.alloc([64, 2 * SEG], BF16, "qtp%d" % i) for i in range(2)]
        pT = [sb.alloc([128, 512], BF16, "pT%d" % i) for i in range(6)]
        ob = [sb.alloc([128, 512], BF16, "ob%d" % i) for i in range(2)]
        rec = {"rc": [sb.alloc([128, 512], F32, "rc%d" % i) for i in range(2)]}

        def load_head(h):
            i = h % 2
            for c in range(0, NK2, 2304):
                pass
            S.dma("sp", ktn[i].t[:], KTn[h, :, 0:NK2], ktn[i].b, writes=[ktn[i].b])
            S.dma("sp", ktp[i].t[:], KTp[h, :, 0:NK2], ktp[i].b, writes=[ktp[i].b])
            vsrc = Vs[0:NK2, h * 128:(h + 1) * 128].rearrange("(t p) d -> p t d", p=128)
            dma_split("sp", [(vv[i].t[:, t0:t0 + 8, :], vsrc[:, t0:t0 + 8, :]) for t0 in range(0, NK2 // 128, 8)], vv[i].b)
            S.dma("sp", qtn[i].t[:], QTn[h, :, 0:2 * SEG], qtn[i].b, writes=[qtn[i].b])
            S.dma("sp", qtp[i].t[:], QTp[h, :, 0:2 * SEG], qtp[i].b, writes=[qtp[i].b])

        DO3 = "p3" in PHASES
        NKB2 = KB_S
        if DO3:
            kbt = [sb.alloc([128, NKB2], BF16, "kbt%d" % i) for i in range(2)]
            vbt = [sb.alloc([128, NKB2 // 128, 128], BF16, "vbt%d" % i) for i in range(2)]
            qbt = [sb.alloc([128, 2 * SEG], BF16, "qbt%d" % i) for i in range(2)]
            btab = [sb.alloc([128, 5, 128], F32, "btab%d" % i) for i in range(2)]
            bmk = sb.alloc([128, 5, 128], F32, "bmk")
            pTb = [sb.alloc([128, 512], BF16, "pTb%d" % i) for i in range(6)]
            obb = [sb.alloc([128, 128], BF16, "obb%d" % i) for i in range(2)]
            recb = {"rc": [sb.alloc([128, 128], F32, "rcb%d" % i) for i in range(2)],
                    "tmp": [sb.alloc([128, 512], F32, "tmpb%d" % i) for i in range(4)]}
            S.dma("sp", bmk.t[:], bmask.rearrange("(t p) q -> p t q", p=128), bmk.b, writes=[bmk.b])

        def load_bhead(h):
            i = h % 2
            S.dma("sp", kbt[i].t[:], KBT[h, :, 0:NKB2], kbt[i].b, writes=[kbt[i].b])
            vsrc = VB[0:NKB2, h * 128:(h + 1) * 128].rearrange("(t p) d -> p t d", p=128)
            dma_split("sp", [(vbt[i].t[:, t0:t0 + 8, :], vsrc[:, t0:t0 + 8, :]) for t0 in range(0, NKB2 // 128, 8)], vbt[i].b)
            S.dma("sp", qbt[i].t[:], QBT[h, :, 0:2 * SEG], qbt[i].b, writes=[qbt[i].b])
            S.dma("sp", btab[i].t[:], bbias[h].rearrange("(t p) q -> p t q", p=128), btab[i].b, writes=[btab[i].b])
            S.op("pool", f_tt(btab[i].t[:], btab[i].t[:], bmk.t[:], ALU.add), reads=[btab[i].b, bmk.b], writes=[btab[i].b])

        def band_units(h, lo, hi):
            i = h % 2
            for un in range(lo, hi):
                seg, pr = un // 8, un % 8
                kbase = KB_HA if seg == 0 else KB_HB
                q0 = seg * SEG + pr * 128
                qparts = [(qbt[i].t[:, q0:q0 + 128], qbt[i].b)]
                tiles = []

                def one(t):
                    k0 = kbase + pr * 128 + t * 128
                    is_halo = (pr * 128 + t * 128) < 512
                    fcol = 10 if (seg == 0 and is_halo) else 11
                    return dict(kparts=[(kbt[i].t[:, k0:k0 + 128], kbt[i].b)],
                                v=(vbt[i].t[:, k0 // 128, :], vbt[i].b), nk=128, c0=0,
                                bias=flags.t[:, fcol:fcol + 1], scale=SCALE_B,
                                btab=(btab[i].t[:, t, :], btab[i].b)), fcol

                singles = [one(t) for t in range(5)]
                if len(set(fc for (_, fc) in singles[0:4])) == 1:
                    fc = singles[0][1]
                    tiles.append(dict(multi=[(d_["kparts"], d_["v"]) for (d_, _) in singles[0:4]], nk=128, c0=0,
                                      bias=flags.t[:, fc:fc + 1], scale=SCALE_B,
                                      btab=(btab[i].t[:, 0:4, :].rearrange("p t q -> p (t q)"), btab[i].b)))
                    tiles.append(singles[4][0])
                else:
                    tiles = [d_ for (d_, _) in singles]
                attn_unit(qparts, tiles, 128, OT[H + h, :, q0:q0 + 128], pTb, obb, recb)

        load_head(0)
        if DO3:
            load_bhead(0)
        ui = 0
        H2 = 1 if SMALL else H
        for h in range(H2):
            att_flush()
            if h + 1 < H2:
                load_head(h + 1)
            i = h % 2
            for qb_ in range(4):
                seg = qb_ // 2
                half = qb_ % 2
                q0 = qb_ * 512
                qparts = [(qtn[i].t[:, q0:q0 + 512], qtn[i].b), (qtp[i].t[:, q0:q0 + 512], qtp[i].b)]
                tiles = []
                nslots = 3 if seg == 0 else 7
                for s_ in range(nslots):
                    fcol = s_ if seg == 0 else 3 + s_
                    for t in range(8):
                        k0 = s_ * SEG + t * 128
                        tiles.append(dict(kparts=[(ktn[i].t[:, k0:k0 + 128], ktn[i].b), (ktp[i].t[:, k0:k0 + 128], ktp[i].b)],
                                          v=(vv[i].t[:, k0 // 128, :], vv[i].b), nk=128, c0=0,
                                          bias=flags.t[:, fcol:fcol + 1], scale=SCALE_A))
                own0 = NHIST + seg * SEG
                for t in range(4 * half + 4):
                    k0 = own0 + t * 128
                    rel = t - 4 * half
                    c0 = max(0, rel) * 128
                    z = (64, 128, 0, 64) if rel >= 0 else None
                    tiles.append(dict(kparts=[(ktn[i].t[:, k0:k0 + 128], ktn[i].b), (ktp[i].t[:, k0:k0 + 128], ktp[i].b)],
                                      v=(vv[i].t[:, k0 // 128, :], vv[i].b), nk=128, c0=c0,
                                      bias=flags.t[:, 11:12], scale=SCALE_A, zero=z))
                attn_unit(qparts, tiles, 512, OT[h, :, q0:q0 + 512], pT, ob, rec)
                bg_step(2)
                ui += 1
        if DO3:
            for h in range(H2):
                att_flush()
                if h + 1 < H2:
                    load_bhead(h + 1)
                band_units(h, 0, 16)
        att_flush()
        S.barrier()
        sb.release(pm0)

    if "p4" in PHASES:
        cb = [sb.alloc([128, 1024], BF16, "cb%d" % i) for i in range(2)]
        stc = [sb.alloc([128, 8, 128], BF16, "stc%d" % i) for i in range(2)]
        for r in range(8):
            c_ = cb[r % 2]
            S.dma("pool", c_.t[:], c_bk[r * 128:(r + 1) * 128, :], c_.b, writes=[c_.b])
            pn = pbank_bf(r % 2).rearrange("p (h t) -> p h t", h=8)
            S.op("pe", f_tr([(pn[:, h, :], c_.t[:, h * 128:(h + 1) * 128]) for h in range(8)], ident.t[:]),
                 reads=[c_.b, ident.b], writes=[PB[r % 2].b])
            S.op("dve", f_copy(stc[r % 2].t[:], pn), reads=[PB[r % 2].b], writes=[stc[r % 2].b])
            S.dma("sp", KBT_c[:, :, r * 128:(r + 1) * 128].rearrange("h d t -> d h t"), stc[r % 2].t[:], stc[r % 2].b,
                  reads=[stc[r % 2].b])
        S.barrier()
        sb.release(pm0)
        wo_pref = sb.alloc([128, 16, D], BF16, "wo")
        load_w(wo_pref, w_o, 16, step=2)
        pm_p4 = sb.mark()
        NKS = PAST + 32
        ktn = [sb.alloc([128, NKS], BF16, "sktn%d" % i) for i in range(2)]
        ktp = [sb.alloc([64, NKS], BF16, "sktp%d" % i) for i in range(2)]
        vv = [sb.alloc([128, 9, 128], BF16, "svv%d" % i) for i in range(2)]
        qtn = [sb.alloc([128, 32], BF16, "sqtn%d" % i) for i in range(2)]
        qtp = [sb.alloc([64, 32], BF16, "sqtp%d" % i) for i in range(2)]
        kbt = [sb.alloc([128, LB + 32], BF16, "skbt%d" % i) for i in range(2)]
        vbt = [sb.alloc([128, 4, 128], BF16, "svbt%d" % i) for i in range(2)]
        vbn = [sb.alloc([32, 128], BF16, "svbn%d" % i) for i in range(2)]
        qbt = [sb.alloc([128, 32], BF16, "sqbt%d" % i) for i in range(2)]
        btab = [sb.alloc([128, 5, 32], F32, "sbtab%d" % i) for i in range(2)]
        pT = [sb.alloc([128, 32], BF16, "spT%d" % i) for i in range(6)]
        ob = [sb.alloc([128, 32], BF16, "sob%d" % i) for i in range(2)]
        rec = {"rc": [sb.alloc([128, 32], F32, "src%d" % i) for i in range(2)],
               "tmp": [sb.alloc([128, 32], F32, "stmp%d" % i) for i in range(4)]}
        units = [(sbi, h) for sbi in range(1 if SMALL else 2) for h in range(1 if SMALL else H)]

        def p4_load(n):
            sbi, h = units[n]
            i = n % 2
            qrow = 2048 + sbi * 32
            knew = NHIST + 2048 + sbi * 32
            kbnew = KB_S + sbi * 32
            dma_split("sp", [(ktn[i].t[:, 0:PAST], KTn_c[h, :, sbi * PAST:(sbi + 1) * PAST]),
                             (ktn[i].t[:, PAST:NKS], KTn[h, :, knew:knew + 32])], ktn[i].b)
            dma_split("sp", [(ktp[i].t[:, 0:PAST], KTp_c[h, :, sbi * PAST:(sbi + 1) * PAST]),
                             (ktp[i].t[:, PAST:NKS], KTp[h, :, knew:knew + 32])], ktp[i].b)
            dma_split("sp", [(vv[i].t[:, 0:8, :], V_c[sbi * PAST:(sbi + 1) * PAST, h * 128:(h + 1) * 128].rearrange(
                "(t p) d -> p t d", p=128)), (vv[i].t[0:32, 8, :], Vs[knew:knew + 32, h * 128:(h + 1) * 128])], vv[i].b)
            S.dma("sp", qtn[i].t[:], QTn[h, :, qrow:qrow + 32], qtn[i].b, writes=[qtn[i].b])
            S.dma("sp", qtp[i].t[:], QTp[h, :, qrow:qrow + 32], qtp[i].b, writes=[qtp[i].b])
            dma_split("sp", [(kbt[i].t[:, 0:LB], KBT_c[h, :, sbi * LB:(sbi + 1) * LB]),
                             (kbt[i].t[:, LB:LB + 32], KBT[h, :, kbnew:kbnew + 32])], kbt[i].b)
            S.dma("pool", vbt[i].t[:], c_bv[sbi * LB:(sbi + 1) * LB, h * 128:(h + 1) * 128].rearrange(
                "(t p) d -> p t d", p=128), vbt[i].b, writes=[vbt[i].b])
            S.dma("sp", vbn[i].t[:], VB[kbnew:kbnew + 32, h * 128:(h + 1) * 128], vbn[i].b, writes=[vbn[i].b])
            S.dma("sp", qbt[i].t[:], QBT[h, :, qrow:qrow + 32], qbt[i].b, writes=[qbt[i].b])
            S.dma("sp", btab[i].t[:], sbias[h].rearrange("(t p) q -> p t q", p=128), btab[i].b, writes=[btab[i].b])

        def p4_compute(n):
            sbi, h = units[n]
            i = n % 2
            qrow = 2048 + sbi * 32
            qparts = [(qtn[i].t[:, :], qtn[i].b), (qtp[i].t[:, :], qtp[i].b)]
            tiles = []
            for t in range(9):
                nk = 128 if t < 8 else 32
                k0 = t * 128
                tiles.append(dict(kparts=[(ktn[i].t[:, k0:k0 + nk], ktn[i].b), (ktp[i].t[:, k0:k0 + nk], ktp[i].b)],
                                  v=(vv[i].t[0:nk, t, :], vv[i].b), nk=nk, c0=0, bias=flags.t[0:nk, 11:12],
                                  scale=SCALE_A))
            attn_unit(qparts, tiles, 32, OT[h, :, qrow:qrow + 32], pT, ob, rec)
            qparts = [(qbt[i].t[:, :], qbt[i].b)]
            tiles = []
            for t in range(5):
                nk = 128 if t < 4 else 32
                k0 = t * 128
                vsrc = (vbt[i].t[:, t, :], vbt[i].b) if t < 4 else (vbn[i].t[:, :], vbn[i].b)
                tiles.append(dict(kparts=[(kbt[i].t[:, k0:k0 + nk], kbt[i].b)], v=vsrc,
                                  nk=nk, c0=0, bias=flags.t[0:nk, 11:12], scale=SCALE_B,
                                  btab=(btab[i].t[0:nk, t, :], btab[i].b)))
            attn_unit(qparts, tiles, 32, OT[H + h, :, qrow:qrow + 32], pT, ob, rec)

        p4_load(0)
        for n in range(len(units)):
            att_flush()
            if n + 1 < len(units):
                p4_load(n + 1)
            p4_compute(n)
        att_flush()
        S.barrier()
        sb.release(pm_p4)

    if "p5" in PHASES:
        bg_step(1000)
        if "p4" in PHASES:
            wo = wo_pref
        else:
            wo = sb.alloc([128, 16, D], BF16, "wo")
            load_w(wo, w_o, 16, step=2)
        otb = [sb.alloc([128, 16, 512], BF16, "otb%d" % i) for i in range(2)]
        xr = [sb.alloc([128, D], F32, "xr%d" % i) for i in range(2)]
        hb = [sb.alloc([128, D], F32, "hb%d" % i) for i in range(2)]
        t5a = [(1536, 4), (2048, 1)] if SMALL else [(0, 4), (512, 4), (1024, 4), (1536, 4), (2048, 1)]
        sc = 0
        for ti_, (tok0, nsub) in enumerate(t5a):
            ob_ = otb[ti_ % 2]
            n = nsub * 128
            osrc = OT[:, :, tok0:tok0 + n].rearrange("h d t -> d h t")
            dma_split("sp", [(ob_.t[:, h0:h0 + 8, 0:n], osrc[:, h0:h0 + 8, :]) for h0 in (0, 8)], ob_.b)
            for s_ in range(nsub):
                i = sc % 2
                sc += 1
                r0 = tok0 + s_ * 128
                S.dma("sp", xr[i].t[:], xo[r0:r0 + 128, :], xr[i].b, writes=[xr[i].b])
                for g in range(4):
                    bk = (sc * 4 + g) % 8
                    S.op("pe", f_mm([(pbank(bk), ob_.t[:, k, s_ * 128:(s_ + 1) * 128], wo.t[:, k, g * 512:(g + 1) * 512],
                                      k == 0, k == 15) for k in range(16)]), reads=[ob_.b, wo.b], writes=[PB[bk].b])
                    S.op("dve", f_tt(hb[i].t[:, g * 512:(g + 1) * 512], pbank(bk), xr[i].t[:, g * 512:(g + 1) * 512], ALU.add),
                         reads=[PB[bk].b, xr[i].b], writes=[hb[i].b])
                S.dma("act", Hs[r0:r0 + 128, :], hb[i].t[:], hb[i].b, reads=[hb[i].b])
        S.barrier()
        sb.release(pm0)

        TMAX = 640
        uT = sb.alloc([128, 64, TMAX], BF16, "uT")
        hn2T = sb.alloc([128, 16, TMAX], BF16, "hn2T")
        hx = [sb.alloc([128, D], F32, "hx%d" % i) for i in range(2)]
        junk = sb.alloc([128, D], BF16, "junk5")
        st = sb.alloc([128, 8], F32, "st5")
        xn = sb.alloc([128, D], BF16, "xn5")
        wup = [sb.alloc([128, 16, 256], BF16, "wup%d" % i) for i in range(3)]
        wdn = [sb.alloc([128, 16, 256], BF16, "wdn%d" % i) for i in range(3)]
        rl = [sb.alloc([128, 512], F32, "rl%d" % i) for i in range(2)]
        hres = [sb.alloc([128, 256], F32, "hres%d" % i) for i in range(2)]
        yo = [sb.alloc([128, 256], F32, "yo%d" % i) for i in range(2)]
        tiles5 = [(1536, 5)] if SMALL else [(0, 4), (512, 4), (1024, 4), (1536, 5)]
        cx = 0
        wi = 0
        di = 0
        ri = 0
        for (tok0, nsub) in tiles5:
            T = nsub * 128
            for s_ in range(nsub):
                xb = hx[cx % 2]
                cx += 1
                r0 = tok0 + s_ * 128
                S.dma("sp", xb.t[:], Hs[r0:r0 + 128, :], xb.b, writes=[xb.b])
                S.op("pool", f_memset(st.t[:, 0:1], 0.0), writes=[st.b])
                S.op("dve", f_stt(junk.t[:], xb.t[:], 1.0, xb.t[:], ALU.mult, ALU.mult, accum=st.t[:, 0:1]),
                     reads=[xb.b], writes=[junk.b, st.b])
                rstd_from_ss(st.t[:, 0:1], st.b, 1, D, st)
                S.op("act", f_act(xn.t[:], xb.t[:], AF.Copy, scale=st.t[:, 0:1]), reads=[xb.b, st.b], writes=[xn.b])
                for half in range(2):
                    pv = pbank_bf(half).rearrange("p (k t) -> p k t", k=8)
                    S.op("pe", f_tr([(pv[:, k, :], xn.t[:, (half * 8 + k) * 128:(half * 8 + k + 1) * 128])
                                     for k in range(8)], ident.t[:]), reads=[xn.b, ident.b], writes=[PB[half].b])
                    S.op("dve", f_tt(hn2T.t[:, half * 8:half * 8 + 8, s_ * 128:(s_ + 1) * 128], pv,
                                     nffn.t[:, half * 8:half * 8 + 8].unsqueeze(2).to_broadcast([128, 8, 128]), ALU.mult),
                         reads=[PB[half].b, nffn.b], writes=[hn2T.b])
            for gcol in range(DFF // 256):
                wb = wup[wi % 3]
                wi += 1
                S.dma("sp", wb.t[:], WU[gcol], wb.b, writes=[wb.b])
                for m2 in range(2):
                    m = gcol * 2 + m2
                    pieces = [(0, min(T, 512))] + ([(512, T)] if T > 512 else [])
                    for (a, b_) in pieces:
                        bk = 2 + (ri % 4)
                        r_ = rl[ri % 2]
                        ri += 1
                        S.op("pe", f_mm([(pbank(bk)[:, 0:b_ - a], wb.t[:, k, m2 * 128:(m2 + 1) * 128], hn2T.t[:, k, a:b_],
                                          k == 0, k == 15) for k in range(16)]), reads=[wb.b, hn2T.b], writes=[PB[bk].b])
                        S.op("act", f_act(r_.t[:, 0:b_ - a], pbank(bk)[:, 0:b_ - a], AF.Relu), reads=[PB[bk].b],
                             writes=[r_.b])
                        S.op("pool" if (ri % 2) else "dve", f_tt(uT.t[:, m, a:b_], r_.t[:, 0:b_ - a], r_.t[:, 0:b_ - a],
                                                                  ALU.mult), reads=[r_.b], writes=[uT.b])
            for gcol in range(D // 256):
                for pc in range(4):
                    wb = wdn[di % 3]
                    di += 1
                    S.dma("sp", wb.t[:], WD[gcol][:, pc * 16:(pc + 1) * 16, :], wb.b, writes=[wb.b])
                    for s_ in range(nsub):
                        bk = 2 + s_ if s_ < 4 else 0
                        S.op("pe", f_mm([(pbank(bk)[:, 0:256], uT.t[:, pc * 16 + k, s_ * 128:(s_ + 1) * 128], wb.t[:, k, :],
                                          pc == 0 and k == 0, pc == 3 and k == 15) for k in range(16)]),
                             reads=[wb.b, uT.b], writes=[PB[bk].b])
                for s_ in range(nsub):
                    bk = 2 + s_ if s_ < 4 else 0
                    r0 = tok0 + s_ * 128
                    hr = hres[(gcol * 5 + s_) % 2]
                    y_ = yo[(gcol * 5 + s_) % 2]
                    S.dma("act", hr.t[:], Hs[r0:r0 + 128, gcol * 256:(gcol + 1) * 256], hr.b, writes=[hr.b])
                    S.op("dve", f_tt(y_.t[:], pbank(bk)[:, 0:256], hr.t[:], ALU.add), reads=[PB[bk].b, hr.b], writes=[y_.b])
                    S.dma("act", y_o[r0:r0 + 128, gcol * 256:(gcol + 1) * 256], y_.t[:], y_.b, reads=[y_.b])
        S.barrier()
        sb.release(pm0)

    S.barrier()
    with nc.Block() as block:
        @block.tensor
        def _(e):
            S.emit("pe", e)

        @block.scalar
        def _(e):
            S.emit("act", e)

        @block.vector
        def _(e):
            S.emit("dve", e)

        @block.gpsimd
        def _(e):
            S.emit("pool", e)

        @block.sync
        def _(e):
            S.emit("sp", e)
    return nc


def _rope_table(pos):
    inv = 1.0 / (10000.0 ** (np.arange(0, 64, 2, dtype=np.float32) / 64.0))
    ang = pos.astype(np.float32)[:, None] * inv[None, :].astype(np.float32)
    return np.concatenate([np.cos(ang), np.sin(ang)], axis=1).astype(np.float32)


def _host_inputs(inp):
    f32 = np.float32
    xp = np.asarray(inp["x_prompt"], f32)
    xs = np.asarray(inp["x_sample"], f32)
    rb = np.asarray(inp["rel_bias"], f32)[0]
    kl = np.arange(640)[:, None]
    ql = np.arange(128)[None, :]
    idx = np.clip(ql - kl + 512, -128, 128) + 128
    bbias = np.ascontiguousarray(rb[:, idx])
    qc = ql // 64
    kc = kl // 64
    allowed = (kc <= qc + 8) & (kc >= qc)
    bmask = np.where(allowed, 0.0, NEGM).astype(f32)
    ks = np.arange(640)[:, None]
    ts = np.arange(32)[None, :]
    rel_s = np.where(ks < 512, 512 + ts - ks, ts - (ks - 512))
    sidx = np.clip(rel_s, -128, 128) + 128
    sbias = np.ascontiguousarray(rb[:, sidx])
    shared = {
        "bbias": bbias, "bmask": bmask, "sbias": sbias, "ident": np.eye(128, dtype=f32),
        "rope_h": _rope_table(np.arange(NHIST)), "rope_c": _rope_table(np.arange(PAST)),
    }
    for k in ("norm_mix", "norm_ffn", "w_in", "g_cq", "w_uq", "g_ckv", "w_uk", "w_uv", "g_qa", "g_ka", "g_qb", "g_kb",
              "w_o", "w_up", "w_down"):
        shared[k] = np.ascontiguousarray(np.asarray(inp[k], f32)[0])
    cck = np.asarray(inp["cache_mla_ckv"], f32)[0]
    ckp = np.asarray(inp["cache_mla_kpe"], f32)[0]
    cbk = np.asarray(inp["cache_band_k"], f32)[0].reshape(16, LB, 1024)
    cbv = np.asarray(inp["cache_band_v"], f32)[0].reshape(16, LB, 1024)
    maps = []
    for c in range(8):
        b, j = c // 4, c % 4
        a0, b0 = j * SEG, (7 - j) * SEG
        m = dict(shared)
        m["xh"] = np.ascontiguousarray(xp[b, 0:NHIST])
        xo = np.zeros((NOWN, D), f32)
        xo[0:SEG] = xp[b, a0:a0 + SEG]
        xo[SEG:2 * SEG] = xp[b, b0:b0 + SEG]
        xo[2048:2080] = xs[2 * c]
        xo[2080:2112] = xs[2 * c + 1]
        m["xo"] = xo
        xl = np.zeros((NHALO, D), f32)
        if a0 >= 512:
            xl[0:512] = xp[b, a0 - 512:a0]
        xl[512:1024] = xp[b, b0 - 512:b0]
        m["xl"] = xl
        m["c_ckv"] = np.ascontiguousarray(cck[2 * c:2 * c + 2].reshape(2 * PAST, 256))
        m["c_kpe"] = np.ascontiguousarray(ckp[2 * c:2 * c + 2].reshape(2 * PAST, 64))
        m["c_bk"] = np.ascontiguousarray(cbk[2 * c:2 * c + 2].reshape(2 * LB, 1024))
        m["c_bv"] = np.ascontiguousarray(cbv[2 * c:2 * c + 2].reshape(2 * LB, 1024))
        pos_o = np.concatenate([np.arange(a0, a0 + SEG), np.arange(b0, b0 + SEG), PAST + np.arange(32),
                                PAST + np.arange(32), np.zeros(64, np.int64)])
        m["rope_o"] = _rope_table(pos_o)
        fl = np.zeros((128, 16), f32)
        for s_ in range(3):
            fl[:, s_] = 0.0 if s_ < j else NEGM
        for s_ in range(7):
            fl[:, 3 + s_] = 0.0 if s_ < 7 - j else NEGM
        fl[:, 10] = NEGM if j == 0 else 0.0
        m["flg"] = fl
        maps.append(m)
    return maps


_NC_CACHE = {}


def kernel(**inputs):
    maps = _host_inputs(inputs)
    if "nc" not in _NC_CACHE:
        _NC_CACHE["nc"] = build_nc()
    nc = _NC_CACHE["nc"]
    res = run_bass_kernel_spmd(nc, maps, core_ids=list(range(8)))
    R = res.results
    f32 = np.float32
    S_ = 8192
    y_p = np.zeros((2, S_, D), f32)
    ckv_p = np.zeros((1, 2, S_, 256), f32)
    kpe_p = np.zeros((1, 2, S_, 64), f32)
    bk_p = np.zeros((1, 2, 512, 8, 128), f32)
    bv_p = np.zeros((1, 2, 512, 8, 128), f32)
    y_s = np.zeros((16, 32, D), f32)
    ckv_s = np.zeros((1, 16, 32, 256), f32)
    kpe_s = np.zeros((1, 16, 32, 64), f32)
    bk_s = np.zeros((1, 16, 32, 8, 128), f32)
    bv_s = np.zeros((1, 16, 32, 8, 128), f32)
    for c in range(8):
        b, j = c // 4, c % 4
        a0, b0 = j * SEG, (7 - j) * SEG
        r = R[c]
        for (dst, src, w) in ((y_p, "y_o", None), (ckv_p[0], "ckv_o", None), (kpe_p[0], "kpe_o", None)):
            dst[b, a0:a0 + SEG] = r[src][0:SEG]
            dst[b, b0:b0 + SEG] = r[src][SEG:2 * SEG]
        if j == 0:
            bk_p[0, b] = r["bk_o"][1536:2048].reshape(512, 8, 128)
            bv_p[0, b] = r["bv_o"][1536:2048].reshape(512, 8, 128)
        for sbi in range(2):
            rows = slice(2048 + 32 * sbi, 2048 + 32 * sbi + 32)
            y_s[2 * c + sbi] = r["y_o"][rows]
            ckv_s[0, 2 * c + sbi] = r["ckv_o"][rows]
            kpe_s[0, 2 * c + sbi] = r["kpe_o"][rows]
            bk_s[0, 2 * c + sbi] = r["bk_o"][rows].reshape(32, 8, 128)
            bv_s[0, 2 * c + sbi] = r["bv_o"][rows].reshape(32, 8, 128)
    return (y_p, y_s, ckv_p, kpe_p, bk_p, bv_p, ckv_s, kpe_s, bk_s, bv_s)
```

```python
import numpy as np
from collections import deque
from contextlib import ExitStack
import concourse.bass as bass
import concourse.mybir as mybir
from concourse.bass_utils import run_bass_kernel_spmd

F32 = mybir.dt.float32
BF16 = mybir.dt.bfloat16
AF = mybir.ActivationFunctionType
ALU = mybir.AluOpType
AX = mybir.AxisListType

D = 2048
H = 8
SEG = 1024
NHIST = 7 * SEG
NOWN = 2 * SEG + 128
NHALO = 1024
NKM = NHIST + NOWN
NKB = 512 + SEG + 512 + SEG + 128
KB_HA, KB_A, KB_HB, KB_B, KB_S = 0, 512, 1536, 2048, 3072
PAST = 1024
LB = 512
EPS = 1e-6
SCALE_A = 192 ** -0.5
SCALE_B = 128 ** -0.5
NEGM = -30000.0
DFF = 8192
IN_COLS = 3904

PHASES = {"p1", "p2", "p3", "p4", "p5"}
SMALL = False


class Buf:
    __slots__ = ("name", "w", "r", "sem", "semval", "semname", "kind", "alt")

    def __init__(self, name):
        self.name = name
        self.w = None
        self.r = {}
        self.sem = None
        self.semval = 0
        self.semname = None
        self.kind = None
        self.alt = None


class Tn:
    def __init__(self, t, name):
        self.t = t
        self.b = Buf(name)


ENGS = ("pe", "act", "dve", "pool", "sp")


class Sched:
    def __init__(self, nc, es):
        self.nc = nc
        self.es = es
        self.q = {e: [] for e in ENGS}
        self.esem = {e: es.enter_context(nc.semaphore("s_" + e)) for e in ENGS}
        self.ecnt = {e: 0 for e in ENGS}
        self.seen = {e: {} for e in ENGS}
        self.dbufs = []
        self.nbuf = 0
        self.rec = None
        self.semfinal = {}
        self.freesems = []
        self.nsem = 0

    def _waits(self, eng, reads, writes):
        need = {}

        def add(tok):
            if tok is None:
                return
            k, h, v = tok
            if k not in need or need[k][1] < v:
                need[k] = (h, v)

        for b in reads:
            add(b.w)
        for b in writes:
            add(b.w)
            for tok in b.r.values():
                add(tok)
        for k, (h, v) in need.items():
            if self.seen[eng].get(k, 0) >= v:
                continue
            self.seen[eng][k] = v
            self.q[eng].append(("w", h, v))

    def _mark(self, tok, reads, writes):
        for b in reads:
            b.r[tok[0]] = tok
        for b in writes:
            b.w = tok
            b.r = {}

    def start_rec(self):
        self.rec = []

    def stop_rec(self):
        r = self.rec
        self.rec = None
        return r

    def interleave(self, lists):
        idx = [0] * len(lists)
        live = True
        while live:
            live = False
            for li, l in enumerate(lists):
                if idx[li] < len(l):
                    it = l[idx[li]]
                    idx[li] += 1
                    live = True
                    if it[0] == "op":
                        self.op(*it[1:])
                    else:
                        self.dma(*it[1:])

    def op(self, eng, fn, reads=(), writes=()):
        if self.rec is not None:
            self.rec.append(("op", eng, fn, tuple(reads), tuple(writes)))
            return None
        self._waits(eng, reads, writes)
        self.ecnt[eng] += 1
        tok = ("e_" + eng, self.esem[eng], self.ecnt[eng])
        self.q[eng].append(("o", fn, self.esem[eng], 1))
        if eng == "pe":
            self.seen[eng]["e_pe"] = self.ecnt[eng]
        self._mark(tok, reads, writes)
        return tok

    def dma(self, queue, out_ap, in_ap, semb, reads=(), writes=(), slow=False):
        if self.rec is not None:
            self.rec.append(("dma", queue, out_ap, in_ap, semb, tuple(reads), tuple(writes), slow))
            return None
        self._waits(queue, reads, writes)
        kind = "sw" if queue == "pool" else "hw"
        if semb.sem is not None and semb.kind != kind:
            if semb.alt is None:
                semb.alt = Buf(semb.name + "_alt")
            semb = semb.alt
        if semb.sem is None:
            fl = [x for x in self.freesems if x[0] == kind]
            semb.kind = kind
            if fl:
                self.freesems.remove(fl[-1])
                _, semb.semname, semb.sem, semb.semval = fl[-1]
            else:
                semb.semname = "d%d" % self.nsem
                self.nsem += 1
                semb.sem = self.es.enter_context(self.nc.semaphore(semb.semname))
            self.dbufs.append(semb)
        semb.semval += 16
        self.semfinal[semb.semname] = (semb.sem, semb.semval)
        tok = (semb.semname, semb.sem, semb.semval)

        def fn(e, o=out_ap, i=in_ap):
            if slow:
                return e.dma_start(out=o, in_=i, allow_slow_non_contiguous=True)
            return e.dma_start(out=o, in_=i)

        self.q[queue].append(("o", fn, semb.sem, 16))
        self._mark(tok, reads, writes)
        return tok

    def barrier(self):
        toks = []
        for e in ENGS:
            if self.ecnt[e] > 0:
                toks.append(("e_" + e, self.esem[e], self.ecnt[e]))
        for k, (h, v) in self.semfinal.items():
            toks.append((k, h, v))
        for e in ENGS:
            for (k, h, v) in toks:
                if self.seen[e].get(k, 0) >= v:
                    continue
                self.seen[e][k] = v
                self.q[e].append(("w", h, v))
        for b in self.dbufs:
            self.freesems.append((b.kind, b.semname, b.sem, b.semval))
            b.sem = None
        self.dbufs = []

    def emit(self, eng, e):
        for it in self.q[eng]:
            if it[0] == "w":
                e.wait_ge(it[1], it[2])
            elif it[0] == "o":
                ins = it[1](e)
                ins.then_inc(it[2], it[3])
            else:
                e.nop().then_inc(it[1], 1)


class SBAlloc:
    def __init__(self, nc):
        self.nc = nc
        self.base = (nc.sbuf_base + 63) // 64 * 64
        self.top = nc.sbuf_top
        self.off = self.base
        self.n = 0

    def alloc(self, shape, dt, name=None):
        per = 1
        for x in shape[1:]:
            per *= x
        per *= 2 if dt == BF16 else 4
        per = (per + 63) // 64 * 64
        assert self.off + per <= self.top, ("SBUF overflow", name, self.off, per, self.top)
        self.n += 1
        nm = "%s_%d" % (name or "t", self.n)
        t = self.nc.alloc_sbuf_tensor_at(nm, list(shape), dt, offset=self.off)
        self.off += per
        return Tn(t, nm)

    def mark(self):
        return self.off

    def release(self, m):
        self.off = m


def f_mm(groups):
    def fn(e):
        ins = None
        for (o, l, r, st, sp) in groups:
            ins = e.matmul(o, l, r, start=st, stop=sp)
        return ins
    return fn


def f_tr(items, ident):
    def fn(e):
        ins = None
        for (o, i) in items:
            ins = e.transpose(o, i, ident)
        return ins
    return fn


def f_tt(out, in0, in1, op):
    return lambda e: e.tensor_tensor(out=out, in0=in0, in1=in1, op=op)


def f_ts(out, in0, s1, s2, op0, op1=None):
    if op1 is None:
        return lambda e: e.tensor_scalar(out=out, in0=in0, scalar1=s1, scalar2=None, op0=op0)
    return lambda e: e.tensor_scalar(out=out, in0=in0, scalar1=s1, scalar2=s2, op0=op0, op1=op1)


def f_stt(out, in0, scalar, in1, op0, op1, accum=None):
    if accum is None:
        return lambda e: e.scalar_tensor_tensor(out=out, in0=in0, scalar=scalar, in1=in1, op0=op0, op1=op1)
    return lambda e: e.scalar_tensor_tensor(out=out, in0=in0, scalar=scalar, in1=in1, op0=op0, op1=op1,
                                            accum_out=accum)


def f_act(out, in_, func, bias=None, scale=None):
    kw = {}
    if bias is not None:
        kw["bias"] = bias
    if scale is not None:
        kw["scale"] = scale
    return lambda e: e.activation(out=out, in_=in_, func=func, **kw)


def f_red(out, in_):
    return lambda e: e.tensor_reduce(out=out, in_=in_, axis=AX.X, op=ALU.add)


def f_copy(out, in_):
    def fn(e):
        if hasattr(e, "tensor_copy"):
            return e.tensor_copy(out, in_)
        return e.activation(out=out, in_=in_, func=AF.Copy)
    return fn


def f_memset(ap, v):
    return lambda e: e.memset(ap, v)


def f_recip(out, in_):
    return lambda e: e.reciprocal(out, in_)


def build_nc(debug=False):
    nc = bass.Bass("TRN2", target_bir_lowering=False)
    es = ExitStack()

    def din(name, shape, dt=F32):
        return nc.dram_tensor(name, list(shape), dt, kind="ExternalInput").ap()

    def dout(name, shape, dt=F32):
        return nc.dram_tensor(name, list(shape), dt, kind="ExternalOutput").ap()

    def dscr(name, shape, dt=BF16):
        kind = "ExternalOutput" if debug else "Internal"
        return nc.dram_tensor(name, list(shape), dt, kind=kind).ap()

    xh = din("xh", [NHIST, D])
    xo = din("xo", [NOWN, D])
    xl = din("xl", [NHALO, D])
    c_ckv = din("c_ckv", [2 * PAST, 256])
    c_kpe = din("c_kpe", [2 * PAST, 64])
    c_bk = din("c_bk", [2 * LB, 1024])
    c_bv = din("c_bv", [2 * LB, 1024])
    rope_h = din("rope_h", [NHIST, 64])
    rope_o = din("rope_o", [NOWN, 64])
    rope_c = din("rope_c", [PAST, 64])
    flg = din("flg", [128, 16])
    bbias = din("bbias", [H, 640, 128])
    bmask = din("bmask", [640, 128])
    sbias = din("sbias", [H, 640, 32])
    ident_d = din("ident", [128, 128])
    norm_mix = din("norm_mix", [D])
    norm_ffn = din("norm_ffn", [D])
    w_in = din("w_in", [D, IN_COLS])
    g_cq = din("g_cq", [512])
    w_uq = din("w_uq", [512, 1536])
    g_ckv = din("g_ckv", [256])
    w_uk = din("w_uk", [256, 1024])
    w_uv = din("w_uv", [256, 1024])
    g_qa = din("g_qa", [192])
    g_ka = din("g_ka", [192])
    g_qb = din("g_qb", [128])
    g_kb = din("g_kb", [128])
    w_o = din("w_o", [D, D])
    w_up = din("w_up", [D, DFF])
    w_down = din("w_down", [DFF, D])

    y_o = dout("y_o", [NOWN, D])
    ckv_o = dout("ckv_o", [NOWN, 256])
    kpe_o = dout("kpe_o", [NOWN, 64])
    bk_o = dout("bk_o", [NOWN, 1024])
    bv_o = dout("bv_o", [NOWN, 1024])

    KTn = dscr("KTn", [H, 128, NKM])
    KTp = dscr("KTp", [H, 64, NKM])
    Vs = dscr("Vs", [NKM, 1024])
    KTn_c = dscr("KTn_c", [H, 128, 2 * PAST])
    KTp_c = dscr("KTp_c", [H, 64, 2 * PAST])
    V_c = dscr("V_c", [2 * PAST, 1024])
    QTn = dscr("QTn", [H, 128, NOWN])
    QTp = dscr("QTp", [H, 64, NOWN])
    QBT = dscr("QBT", [H, 128, NOWN])
    KBT = dscr("KBT", [H, 128, NKB])
    VB = dscr("VB", [NKB, 1024])
    KBT_c = dscr("KBT_c", [H, 128, 2 * LB])
    OT = dscr("OT", [2 * H, 128, NOWN])
    Hs = dscr("Hs", [NOWN, D], F32)

    WU = dscr("WU", [DFF // 256, 128, 16, 256])
    WD = dscr("WD", [D // 256, 128, 64, 256])
    S = Sched(nc, es)
    sb = SBAlloc(nc)
    psum = nc.alloc_psum_tensor("ps", [128, 8, 512], F32)
    PB = [Tn(None, "pb%d" % i) for i in range(8)]

    def pbank(i):
        return psum[:, i, :]

    def pbank_bf(i):
        return psum[:, i, :].bitcast(BF16)

    ident = sb.alloc([128, 128], BF16, "ident")
    ones = sb.alloc([128, 128], BF16, "ones")
    epsc = sb.alloc([128, 1], F32, "eps")
    zero_c = sb.alloc([128, 1], F32, "zero")
    flags = sb.alloc([128, 16], F32, "flags")
    nmix = sb.alloc([128, 16], F32, "nmix")
    nffn = sb.alloc([128, 16], F32, "nffn")
    gqa_n = sb.alloc([128, 1], F32, "gqa_n")
    gka_n = sb.alloc([128, 1], F32, "gka_n")
    gqb_p = sb.alloc([128, 1], F32, "gqb_p")
    gqa_pe = sb.alloc([128, 64], F32, "gqa_pe")
    gka_pe = sb.alloc([128, 64], F32, "gka_pe")
    gkb_b = sb.alloc([128, 128], F32, "gkb_b")
    gcq_b = sb.alloc([128, 512], F32, "gcq_b")
    gckv_b = sb.alloc([128, 256], F32, "gckv_b")

    S.dma("pool", ident.t[:], ident_d, ident.b, writes=[ident.b])
    S.op("pool", f_memset(ones.t[:], 1.0), writes=[ones.b])
    S.op("pool", f_memset(epsc.t[:], EPS), writes=[epsc.b])
    S.op("pool", f_memset(zero_c.t[:], 0.0), writes=[zero_c.b])
    S.dma("sp", flags.t[:], flg, flags.b, writes=[flags.b])
    S.dma("sp", nmix.t[:], norm_mix.rearrange("(k p) -> p k", p=128), nmix.b, writes=[nmix.b], slow=True)
    S.dma("sp", nffn.t[:], norm_ffn.rearrange("(k p) -> p k", p=128), nffn.b, writes=[nffn.b], slow=True)
    S.dma("sp", gqa_n.t[:], g_qa[0:128].rearrange("(p o) -> p o", o=1), gqa_n.b, writes=[gqa_n.b])
    S.dma("sp", gka_n.t[:], g_ka[0:128].rearrange("(p o) -> p o", o=1), gka_n.b, writes=[gka_n.b])
    S.dma("sp", gqb_p.t[:], g_qb.rearrange("(p o) -> p o", o=1), gqb_p.b, writes=[gqb_p.b])
    S.dma("sp", gqa_pe.t[:], g_qa[128:192].partition_broadcast(128), gqa_pe.b, writes=[gqa_pe.b])
    S.dma("sp", gka_pe.t[:], g_ka[128:192].partition_broadcast(128), gka_pe.b, writes=[gka_pe.b])
    S.dma("sp", gkb_b.t[:], g_kb.partition_broadcast(128), gkb_b.b, writes=[gkb_b.b])
    S.dma("sp", gcq_b.t[:], g_cq.partition_broadcast(128), gcq_b.b, writes=[gcq_b.b])
    S.dma("sp", gckv_b.t[:], g_ckv.partition_broadcast(128), gckv_b.b, writes=[gckv_b.b])
    bg = deque()
    cvb = Buf("cv")
    w_up_v = w_up.rearrange("(k p) c -> p k c", p=128)
    w_dn_v = w_down.rearrange("(k p) c -> p k c", p=128)
    if "p5" in PHASES:
        for g_ in range(DFF // 256):
            bg.append(lambda g_=g_: S.dma("pool", WU[g_], w_up_v[:, :, g_ * 256:(g_ + 1) * 256], cvb))
        for g_ in range(D // 256):
            for pc_ in range(4):
                bg.append(lambda g_=g_, pc_=pc_: S.dma("pool", WD[g_][:, pc_ * 16:(pc_ + 1) * 16, :],
                                                       w_dn_v[:, pc_ * 16:(pc_ + 1) * 16, g_ * 256:(g_ + 1) * 256], cvb))

    def bg_step(n=1):
        for _ in range(n):
            if bg:
                bg.popleft()()

    consts = [ident.b, ones.b, epsc.b, zero_c.b, flags.b, nmix.b, nffn.b, gqa_n.b, gka_n.b, gqb_p.b,
              gqa_pe.b, gka_pe.b, gkb_b.b, gcq_b.b, gckv_b.b]
    S.barrier()
    pm0 = sb.mark()

    def load_w(dst, src_ap, kchunks, step=4):
        v = src_ap.rearrange("(k p) c -> p k c", p=128)
        for k0 in range(0, kchunks, step):
            k1 = min(kchunks, k0 + step)
            S.dma("pool", dst.t[:, k0:k1, :], v[:, k0:k1, :], dst.b, writes=[])
        dst.b.w = (dst.b.semname, dst.b.sem, dst.b.semval)
        dst.b.r = {}

    def dma_split(queue, pieces, buf, reads=()):
        tok = None
        for n_, (o_, i_) in enumerate(pieces):
            tok = S.dma(queue, o_, i_, buf, reads=reads, writes=[buf] if n_ == 0 else [])
        buf.w = tok
        buf.r = {}

    def rstd_from_ss(ss_ap, ss_buf, n, dim, stat):
        S.op("act", f_act(ss_ap, ss_ap, AF.Ln, bias=epsc.t[:, 0:1], scale=1.0 / dim), reads=[ss_buf], writes=[ss_buf])
        S.op("act", f_act(ss_ap, ss_ap, AF.Exp, scale=-0.5), reads=[ss_buf], writes=[ss_buf])

    if "p1" in PHASES:
        cnt = {"s": 0}

        def alloc_sets(specs, nch=2):
            sets = [{nm: sb.alloc(shape, dt, "%s_%d" % (nm, p)) for (nm, shape, dt) in specs} for p in range(nch)]
            for p, z in enumerate(sets):
                z["bk"] = [4 * p + n for n in range(4)] if nch == 2 else [2 * p, 2 * p + 1, 2 * p, 2 * p + 1]
            return sets

        def front(x_rows, nm, Z, par):
            xb, st, xn, hnT = Z["xt"], Z["st"], Z["xn"], Z["hnT"]
            S.dma("sp", xb.t[:], x_rows, xb.b, writes=[xb.b])
            S.op("pool", f_memset(st.t[:, 0:1], 0.0), writes=[st.b])
            S.op("dve", f_stt(junk.t[:], xb.t[:], 1.0, xb.t[:], ALU.mult, ALU.mult, accum=st.t[:, 0:1]),
                 reads=[xb.b], writes=[junk.b, st.b])
            rstd_from_ss(st.t[:, 0:1], st.b, 1, D, st)
            S.op("act", f_act(xn.t[:], xb.t[:], AF.Copy, scale=st.t[:, 0:1]), reads=[xb.b, st.b], writes=[xn.b])
            for half in range(2):
                bk = Z["bk"][half]
                pv = pbank_bf(bk).rearrange("p (k t) -> p k t", k=8)
                S.op("pe", f_tr([(pv[:, k, :], xn.t[:, (half * 8 + k) * 128:(half * 8 + k + 1) * 128])
                                 for k in range(8)], ident.t[:]), reads=[xn.b, ident.b], writes=[PB[bk].b])
                S.op("dve", f_tt(hnT.t[:, half * 8:half * 8 + 8, :], pv,
                                 nm.t[:, half * 8:half * 8 + 8].unsqueeze(2).to_broadcast([128, 8, 128]), ALU.mult),
                     reads=[PB[bk].b, nm.b], writes=[hnT.b])

        def head_rstd(src3, srcbuf, nh, dh, ss_ap, stb, sqb, extra=None, div=None):
            S.op("pool", f_tt(sqb.t[:, 0:nh, 0:dh], src3, src3, ALU.mult), reads=[srcbuf], writes=[sqb.b])
            S.op("dve", f_red(ss_ap, sqb.t[:, 0:nh, 0:dh]), reads=[sqb.b], writes=[stb.b])
            if extra is not None:
                S.op("dve", f_ts(ss_ap, ss_ap, extra, None, ALU.add), reads=[stb.b], writes=[stb.b])
            rstd_from_ss(ss_ap, stb.b, nh, div or dh, stb)

        def rope(dst3, src3, srcbuf, dstbuf, csb, nh, rp):
            c = csb.t[:, 0:32].unsqueeze(1).to_broadcast([128, nh, 32])
            sn = csb.t[:, 32:64].unsqueeze(1).to_broadcast([128, nh, 32])
            x1 = src3[:, :, 0:32]
            x2 = src3[:, :, 32:64]
            t = [rp.t[:, i, 0:nh, :] for i in range(4)]
            S.op("pool", f_tt(t[0], x1, c, ALU.mult), reads=[srcbuf, csb.b], writes=[rp.b])
            S.op("pool", f_tt(t[1], x2, sn, ALU.mult), reads=[srcbuf, csb.b], writes=[rp.b])
            S.op("pool", f_tt(t[2], x1, sn, ALU.mult), reads=[srcbuf, csb.b], writes=[rp.b])
            S.op("pool", f_tt(t[3], x2, c, ALU.mult), reads=[srcbuf, csb.b], writes=[rp.b])
            S.op("pool", f_tt(dst3[:, :, 0:32], t[0], t[1], ALU.subtract), reads=[rp.b], writes=[dstbuf])
            S.op("pool", f_tt(dst3[:, :, 32:64], t[2], t[3], ALU.add), reads=[rp.b], writes=[dstbuf])

        win_m = sb.alloc([128, 16, 832], BF16, "win_m")
        wuk = sb.alloc([128, 2, 1024], BF16, "wuk")
        wuv = sb.alloc([128, 2, 1024], BF16, "wuv")
        load_w(win_m, w_in[:, 0:832], 16)
        load_w(wuk, w_uk, 2)
        load_w(wuv, w_uv, 2)
        junk = sb.alloc([128, D], BF16, "junk")
        stg_kn = sb.alloc([128, 8, 512], BF16, "skn")
        stg_kp = sb.alloc([64, 8, 512], BF16, "skp")
        stg_v = sb.alloc([128, 4, 1024], BF16, "sv")
        csl = [sb.alloc([128, 64], F32, "cs%d" % i) for i in range(8)]
        pm1 = sb.mark()
        KV_SPECS = [("xt", [128, D], F32), ("st", [128, 32], F32), ("xn", [128, D], BF16),
                    ("hnT", [128, 16, 128], BF16), ("ckvf", [128, 256], F32), ("kpef", [128, 64], F32),
                    ("ckvb", [128, 256], BF16), ("smT", [128, 6, 128], BF16), ("kf", [128, 8, 128], F32),
                    ("kn", [128, 8, 192], BF16), ("kpg", [128, 64], F32), ("kpr", [128, 64], F32)]
        ZM4 = alloc_sets(KV_SPECS + [("sqk", [128, 8, 128], F32), ("rpk", [128, 4, 1, 32], F32)], nch=4)

        def kv_from_latent(Z, par, sub):
            ckv_f, kpe_f, csb, st = Z["ckvf"], Z["kpef"], Z["csb"], Z["st"]
            ckvb, smT, kf, kn, kpg, kpr = Z["ckvb"], Z["smT"], Z["kf"], Z["kn"], Z["kpg"], Z["kpr"]
            S.op("dve", f_copy(ckvb.t[:], ckv_f.t[:]), reads=[ckv_f.b], writes=[ckvb.b])
            b0, b1, b2, b3 = Z["bk"]
            b0, b1 = b2, b3
            pv = pbank_bf(b2).rearrange("p (k t) -> p k t", k=8)
            S.op("pe", f_tr([(pv[:, 4 + k, :], ckvb.t[:, k * 128:(k + 1) * 128]) for k in range(2)], ident.t[:]),
                 reads=[ckvb.b, ident.b], writes=[PB[b2].b])
            S.op("act", f_copy(smT.t[:, 4:6, :], pv[:, 4:6, :]), reads=[PB[b2].b], writes=[smT.b])
            for g in range(2):
                S.op("pe", f_mm([(pbank(b2 + g), smT.t[:, 4 + k, :], wuk.t[:, k, g * 512:(g + 1) * 512], k == 0, k == 1)
                                 for k in range(2)]), reads=[smT.b, wuk.b], writes=[PB[b2 + g].b])
            for g in range(2):
                S.op("act", f_copy(kf.t[:, g * 4:(g + 1) * 4, :], pbank(b2 + g).rearrange("p (h d) -> p h d", h=4)),
                     reads=[PB[b2 + g].b], writes=[kf.b])
            for g in range(2):
                S.op("pe", f_mm([(pbank(b0 + g), smT.t[:, 4 + k, :], wuv.t[:, k, g * 512:(g + 1) * 512], k == 0, k == 1)
                                 for k in range(2)]), reads=[smT.b, wuv.b], writes=[PB[b0 + g].b])
            for g in range(2):
                S.op("act", f_copy(stg_v.t[:, sub, g * 512:(g + 1) * 512], pbank(b0 + g)), reads=[PB[b0 + g].b],
                     writes=[stg_v.b])
            S.op("pool", f_memset(st.t[:, 1:2], 0.0), writes=[st.b])
            S.op("dve", f_stt(kpg.t[:], kpe_f.t[:], 1.0, kpe_f.t[:], ALU.mult, ALU.mult, accum=st.t[:, 1:2]),
                 reads=[kpe_f.b], writes=[kpg.b, st.b])
            head_rstd(kf.t[:, :, :], kf.b, 8, 128, st.t[:, 8:16], st, Z["sqk"], extra=st.t[:, 1:2], div=192)
            S.op("dve", f_tt(kn.t[:, :, 0:128], kf.t[:, :, :], st.t[:, 8:16].unsqueeze(2).to_broadcast([128, 8, 128]),
                             ALU.mult), reads=[kf.b, st.b], writes=[kn.b])
            S.op("pool", f_tt(kpg.t[:], kpe_f.t[:], gka_pe.t[:], ALU.mult), reads=[kpe_f.b, gka_pe.b], writes=[kpg.b])
            rope(kpr.t[:].rearrange("p (h d) -> p h d", h=1), kpg.t[:].rearrange("p (h d) -> p h d", h=1),
                 kpg.b, kpr.b, csb, 1, Z["rpk"])
            S.op("dve", f_tt(kn.t[:, :, 128:192], kpr.t[:].unsqueeze(1).to_broadcast([128, 8, 64]),
                             st.t[:, 8:16].unsqueeze(2).to_broadcast([128, 8, 64]), ALU.mult),
                 reads=[kpr.b, st.b], writes=[kn.b])
            pn = pbank_bf(b2).rearrange("p (h t) -> p h t", h=8)
            pp = pbank_bf(b3).rearrange("p (h t) -> p h t", h=8)
            S.op("pe", f_tr([(pn[:, h, :], kn.t[:, h, 0:128]) for h in range(8)], ident.t[:]),
                 reads=[kn.b, ident.b], writes=[PB[b2].b])
            S.op("pe", f_tr([(pp[0:64, h, :], kn.t[:, h, 128:192]) for h in range(8)], ident.t[:]),
                 reads=[kn.b, ident.b], writes=[PB[b3].b])
            S.op("act", f_act(stg_kn.t[:, :, sub * 128:(sub + 1) * 128], pn, AF.Copy, scale=gka_n.t[:, 0:1]),
                 reads=[PB[b2].b, gka_n.b], writes=[stg_kn.b])
            S.op("dve", f_copy(stg_kp.t[:, :, sub * 128:(sub + 1) * 128], pp[0:64, :, :]), reads=[PB[b3].b],
                 writes=[stg_kp.b])

        def q_from_cq(Z, par, sub):
            st, cqf, cqn, smT, qf, qn, csb = Z["st"], Z["cqf"], Z["cqn"], Z["smT"], Z["qf"], Z["qn"], Z["csb"]
            b0, b1, b2, b3 = Z["bk"]
            S.op("pool", f_memset(st.t[:, 2:3], 0.0), writes=[st.b])
            S.op("dve", f_stt(junk.t[:, 0:512], cqf.t[:], 1.0, cqf.t[:], ALU.mult, ALU.mult, accum=st.t[:, 2:3]),
                 reads=[cqf.b], writes=[junk.b, st.b])
            rstd_from_ss(st.t[:, 2:3], st.b, 1, 512, st)
            S.op("dve", f_stt(cqn.t[:], cqf.t[:], st.t[:, 2:3], gcq_b.t[:], ALU.mult, ALU.mult),
                 reads=[cqf.b, st.b, gcq_b.b], writes=[cqn.b])
            pv = pbank_bf(b0).rearrange("p (k t) -> p k t", k=8)
            S.op("pe", f_tr([(pv[:, k, :], cqn.t[:, k * 128:(k + 1) * 128]) for k in range(4)], ident.t[:]),
                 reads=[cqn.b, ident.b], writes=[PB[b0].b])
            S.op("act", f_copy(smT.t[:, 0:4, :], pv[:, 0:4, :]), reads=[PB[b0].b], writes=[smT.b])
            qf2 = qf.t[:].rearrange("p h d -> p (h d)")
            for g in range(3):
                bq = (b1, b0, b1)[g]
                S.op("pe", f_mm([(pbank(bq), smT.t[:, k, :], wuq.t[:, k, g * 512:(g + 1) * 512], k == 0, k == 3)
                                 for k in range(4)]), reads=[smT.b, wuq.b], writes=[PB[bq].b])
                S.op("act", f_copy(qf2[:, g * 512:(g + 1) * 512], pbank(bq)), reads=[PB[bq].b], writes=[qf.b])
            head_rstd(qf.t[:, :, :], qf.b, 8, 192, st.t[:, 16:24], st, Z["sq"])
            S.op("dve", f_tt(qn.t[:, :, 0:128], qf.t[:, :, 0:128], st.t[:, 16:24].unsqueeze(2).to_broadcast([128, 8, 128]),
                             ALU.mult), reads=[qf.b, st.b], writes=[qn.b])
            S.op("dve", f_tt(qf.t[:, :, 128:192], qf.t[:, :, 128:192],
                             st.t[:, 16:24].unsqueeze(2).to_broadcast([128, 8, 64]), ALU.mult),
                 reads=[qf.b, st.b], writes=[qf.b])
            S.op("pool", f_tt(qf.t[:, :, 128:192], qf.t[:, :, 128:192],
                              gqa_pe.t[:].unsqueeze(1).to_broadcast([128, 8, 64]), ALU.mult),
                 reads=[qf.b, gqa_pe.b], writes=[qf.b])
            rope(qn.t[:, :, 128:192], qf.t[:, :, 128:192], qf.b, qn.b, csb, 8, Z["rp"])
            pn = pbank_bf(b0).rearrange("p (h t) -> p h t", h=8)
            pp = pbank_bf(b1).rearrange("p (h t) -> p h t", h=8)
            S.op("pe", f_tr([(pn[:, h, :], qn.t[:, h, 0:128]) for h in range(8)], ident.t[:]),
                 reads=[qn.b, ident.b], writes=[PB[b0].b])
            S.op("pe", f_tr([(pp[0:64, h, :], qn.t[:, h, 128:192]) for h in range(8)], ident.t[:]),
                 reads=[qn.b, ident.b], writes=[PB[b1].b])
            S.op("act", f_act(stg_qn.t[:, :, sub * 128:(sub + 1) * 128], pn, AF.Copy, scale=gqa_n.t[:, 0:1]),
                 reads=[PB[b0].b, gqa_n.b], writes=[stg_qn.b])
            S.op("dve", f_copy(stg_qp.t[:, :, sub * 128:(sub + 1) * 128], pp[0:64, :, :]), reads=[PB[b1].b],
                 writes=[stg_qp.b])

        def mla_tile(xsrc, row0, nsub, ropesrc, rope0, kslot0, own_row0=None, from_cache=None, kdst=None, sets=None):
            KTn_d, KTp_d, V_d = kdst
            recs = []
            recs2 = []
            for sub in range(nsub):
                S.start_rec()
                nch = len(sets)
                par = cnt["s"] % nch
                Z = dict(sets[par])
                Z["csb"] = csl[cnt["s"] % 8]
                cnt["s"] += 1
                b2, b3 = Z["bk"][2], Z["bk"][3]
                r0 = row0 + sub * 128
                csb, cf, kp, hnT = Z["csb"], Z["ckvf"], Z["kpef"], Z["hnT"]
                if from_cache is not None:
                    S.dma("sp", csb.t[:], ropesrc[rope0 + sub * 128:rope0 + (sub + 1) * 128, :], csb.b, writes=[csb.b])
                if from_cache is None:
                    front(xsrc[r0:r0 + 128, :], nmix, Z, par)
                    S.dma("sp", csb.t[:], ropesrc[rope0 + sub * 128:rope0 + (sub + 1) * 128, :], csb.b, writes=[csb.b])
                    if own_row0 is not None:
                        S.op("pe", f_mm([(pbank(b2), hnT.t[:, k, :], win_m.t[:, k, 0:512], k == 0, k == 15)
                                         for k in range(16)]), reads=[hnT.b, win_m.b], writes=[PB[b2].b])
                    S.op("pe", f_mm([(pbank(b3)[:, 0:320], hnT.t[:, k, :], win_m.t[:, k, 512:832], k == 0, k == 15)
                                     for k in range(16)]), reads=[hnT.b, win_m.b], writes=[PB[b3].b])
                    if own_row0 is not None:
                        S.op("act", f_copy(Z["cqf"].t[:], pbank(b2)), reads=[PB[b2].b], writes=[Z["cqf"].b])
                    S.op("act", f_copy(cf.t[:], pbank(b3)[:, 0:256]), reads=[PB[b3].b], writes=[cf.b])
                    S.op("act", f_copy(kp.t[:], pbank(b3)[:, 256:320]), reads=[PB[b3].b], writes=[kp.b])
                    st = Z["st"]
                    S.op("pool", f_memset(st.t[:, 3:4], 0.0), writes=[st.b])
                    S.op("dve", f_stt(junk.t[:, 0:256], cf.t[:], 1.0, cf.t[:], ALU.mult, ALU.mult,
                                      accum=st.t[:, 3:4]), reads=[cf.b], writes=[junk.b, st.b])
                    rstd_from_ss(st.t[:, 3:4], st.b, 1, 256, st)
                    S.op("dve", f_stt(cf.t[:], cf.t[:], st.t[:, 3:4], gckv_b.t[:], ALU.mult, ALU.mult),
                         reads=[cf.b, st.b, gckv_b.b], writes=[cf.b])
                    if own_row0 is not None:
                        o0 = own_row0 + sub * 128
                        S.dma("pool", ckv_o[o0:o0 + 128, :], cf.t[:], cf.b, reads=[cf.b])
                        S.dma("pool", kpe_o[o0:o0 + 128, :], kp.t[:], kp.b, reads=[kp.b])
                        recs.append(S.stop_rec())
                        S.start_rec()
                        q_from_cq(Z, par, sub)
                        recs2.append(S.stop_rec())
                        S.start_rec()
                else:
                    ca, ka = from_cache
                    S.dma("sp", cf.t[:], ca[r0:r0 + 128, :], cf.b, writes=[cf.b])
                    S.dma("sp", kp.t[:], ka[r0:r0 + 128, :], kp.b, writes=[kp.b])
                kv_from_latent(Z, par, sub)
                (recs2 if (own_row0 is not None and from_cache is None) else recs).append(S.stop_rec())
            npl = 2 if (own_row0 is not None and from_cache is None) else 0
            for i0 in range(0, nsub, nch):
                S.interleave(recs[i0:i0 + nch])
                if npl:
                    S.interleave(recs2[npl * i0:npl * (i0 + nch)])
            n = nsub * 128
            S.dma("act", KTn_d[:, :, kslot0:kslot0 + n].rearrange("h d t -> d h t"), stg_kn.t[:, :, 0:n], stg_kn.b,
                  reads=[stg_kn.b])
            S.dma("act", KTp_d[:, :, kslot0:kslot0 + n].rearrange("h d t -> d h t"), stg_kp.t[:, :, 0:n], stg_kp.b,
                  reads=[stg_kp.b])
            S.dma("act", V_d[kslot0:kslot0 + n, :].rearrange("(s p) c -> p s c", p=128), stg_v.t[:, 0:nsub, :],
                  stg_v.b, reads=[stg_v.b])
            if own_row0 is not None:
                S.dma("act", QTn[:, :, own_row0:own_row0 + n].rearrange("h d t -> d h t"), stg_qn.t[:, :, 0:n],
                      stg_qn.b, reads=[stg_qn.b])
                S.dma("act", QTp[:, :, own_row0:own_row0 + n].rearrange("h d t -> d h t"), stg_qp.t[:, :, 0:n],
                      stg_qp.b, reads=[stg_qp.b])

        kd = (KTn, KTp, Vs)
        for t in range(1 if SMALL else NHIST // 512):
            mla_tile(xh, t * 512, 4, rope_h, t * 512, t * 512, kdst=kd, sets=ZM4)
        if "p4" in PHASES:
            kdc = (KTn_c, KTp_c, V_c)
            for sbi in range(1 if SMALL else 2):
                for t in range(1 if SMALL else 2):
                    mla_tile(None, sbi * PAST + t * 512, 4, rope_c, t * 512, sbi * PAST + t * 512,
                             from_cache=(c_ckv, c_kpe), kdst=kdc, sets=ZM4)
        S.barrier()
        sb.release(pm1)
        wuq = sb.alloc([128, 4, 1536], BF16, "wuq")
        load_w(wuq, w_uq, 4)
        ZM = alloc_sets(KV_SPECS + [("cqn", [128, 512], BF16), ("cqf", [128, 512], F32), ("qf", [128, 8, 192], F32),
                                    ("sq", [128, 8, 192], F32), ("qn", [128, 8, 192], BF16),
                                    ("rp", [128, 4, 8, 32], F32), ("sqk", [128, 8, 128], F32),
                                    ("rpk", [128, 4, 1, 32], F32)], nch=2)
        stg_qn = sb.alloc([128, 8, 512], BF16, "sqn")
        stg_qp = sb.alloc([64, 8, 512], BF16, "sqp")
        for t in range(1 if SMALL else 4):
            mla_tile(xo, t * 512, 4, rope_o, t * 512, NHIST + t * 512, own_row0=t * 512, kdst=kd, sets=ZM)
        mla_tile(xo, 2048, 1, rope_o, 2048, NHIST + 2048, own_row0=2048, kdst=kd, sets=ZM)
        S.barrier()
        sb.release(pm0)

        win_b = sb.alloc([128, 16, 3072], BF16, "win_b")
        load_w(win_b, w_in[:, 832:3904], 16, step=2)
        junk = sb.alloc([128, D], BF16, "junkb")
        ZB = alloc_sets([("xt", [128, D], F32), ("st", [128, 32], F32), ("xn", [128, D], BF16),
                         ("hnT", [128, 16, 128], BF16), ("bf", [128, 8, 128], F32), ("sq", [128, 8, 128], F32),
                         ("bn", [128, 8, 128], BF16), ("kbo", [128, 8, 128], F32), ("vbo", [128, 1024], F32)])
        stg_qb = sb.alloc([128, 8, 512], BF16, "sqb")
        stg_kb = sb.alloc([128, 8, 512], BF16, "skb")
        stg_vb = sb.alloc([128, 4, 1024], BF16, "svb")

        def band_tile(xsrc, row0, nsub, kslot0, own_row0=None):
            recs = []
            for sub in range(nsub):
                S.start_rec()
                par = cnt["s"] % 2
                cnt["s"] += 1
                Z = ZB[par]
                b0, b1, b2, b3 = Z["bk"]
                hnT, st, bf, bn, ko, vo = Z["hnT"], Z["st"], Z["bf"], Z["bn"], Z["kbo"], Z["vbo"]
                r0 = row0 + sub * 128
                front(xsrc[r0:r0 + 128, :], nmix, Z, par)
                if own_row0 is not None:
                    for g in range(2):
                        S.op("pe", f_mm([(pbank(b2 + g), hnT.t[:, k, :], win_b.t[:, k, g * 512:(g + 1) * 512], k == 0, k == 15)
                                         for k in range(16)]), reads=[hnT.b, win_b.b], writes=[PB[b2 + g].b])
                    for g in range(2):
                        S.op("act", f_copy(bf.t[:, g * 4:(g + 1) * 4, :], pbank(b2 + g).rearrange("p (h d) -> p h d", h=4)),
                             reads=[PB[b2 + g].b], writes=[bf.b])
                    head_rstd(bf.t[:, :, :], bf.b, 8, 128, st.t[:, 8:16], st, Z["sq"])
                    S.op("dve", f_tt(bn.t[:, :, :], bf.t[:, :, :], st.t[:, 8:16].unsqueeze(2).to_broadcast([128, 8, 128]),
                                     ALU.mult), reads=[bf.b, st.b], writes=[bn.b])
                    pn = pbank_bf(b0).rearrange("p (h t) -> p h t", h=8)
                    S.op("pe", f_tr([(pn[:, h, :], bn.t[:, h, :]) for h in range(8)], ident.t[:]),
                         reads=[bn.b, ident.b], writes=[PB[b0].b])
                    S.op("act", f_act(stg_qb.t[:, :, sub * 128:(sub + 1) * 128], pn, AF.Copy, scale=gqb_p.t[:, 0:1]),
                         reads=[PB[b0].b, gqb_p.b], writes=[stg_qb.b])
                for g in range(2):
                    S.op("pe", f_mm([(pbank(b2 + g), hnT.t[:, k, :], win_b.t[:, k, 1024 + g * 512:1024 + (g + 1) * 512],
                                      k == 0, k == 15) for k in range(16)]), reads=[hnT.b, win_b.b], writes=[PB[b2 + g].b])
                for g in range(2):
                    S.op("act", f_copy(bf.t[:, g * 4:(g + 1) * 4, :], pbank(b2 + g).rearrange("p (h d) -> p h d", h=4)),
                         reads=[PB[b2 + g].b], writes=[bf.b])
                for g in range(2):
                    S.op("pe", f_mm([(pbank(b0 + g), hnT.t[:, k, :], win_b.t[:, k, 2048 + g * 512:2048 + (g + 1) * 512],
                                      k == 0, k == 15) for k in range(16)]), reads=[hnT.b, win_b.b], writes=[PB[b0 + g].b])
                head_rstd(bf.t[:, :, :], bf.b, 8, 128, st.t[:, 16:24], st, Z["sq"])
                S.op("dve", f_tt(bf.t[:, :, :], bf.t[:, :, :], st.t[:, 16:24].unsqueeze(2).to_broadcast([128, 8, 128]),
                                 ALU.mult), reads=[bf.b, st.b], writes=[bf.b])
                S.op("pool", f_tt(ko.t[:, :, :], bf.t[:, :, :], gkb_b.t[:].unsqueeze(1).to_broadcast([128, 8, 128]),
                                  ALU.mult), reads=[bf.b, gkb_b.b], writes=[ko.b])
                S.op("dve", f_copy(bn.t[:, :, :], ko.t[:, :, :]), reads=[ko.b], writes=[bn.b])
                for g in range(2):
                    S.op("act", f_copy(vo.t[:, g * 512:(g + 1) * 512], pbank(b0 + g)), reads=[PB[b0 + g].b], writes=[vo.b])
                pn = pbank_bf(b2).rearrange("p (h t) -> p h t", h=8)
                S.op("pe", f_tr([(pn[:, h, :], bn.t[:, h, :]) for h in range(8)], ident.t[:]),
                     reads=[bn.b, ident.b], writes=[PB[b2].b])
                S.op("act", f_copy(stg_kb.t[:, :, sub * 128:(sub + 1) * 128], pn), reads=[PB[b2].b],
                     writes=[stg_kb.b])
                S.op("pool", f_copy(stg_vb.t[:, sub, :], vo.t[:]), reads=[vo.b], writes=[stg_vb.b])
                if own_row0 is not None:
                    o0 = own_row0 + sub * 128
                    S.dma("pool", bk_o[o0:o0 + 128, :], ko.t[:].rearrange("p h d -> p (h d)"), ko.b, reads=[ko.b])
                    S.dma("pool", bv_o[o0:o0 + 128, :], vo.t[:], vo.b, reads=[vo.b])
                recs.append(S.stop_rec())
            for i0 in range(0, nsub, 2):
                S.interleave(recs[i0:i0 + 2])
            n = nsub * 128
            S.dma("act", KBT[:, :, kslot0:kslot0 + n].rearrange("h d t -> d h t"), stg_kb.t[:, :, 0:n], stg_kb.b,
                  reads=[stg_kb.b])
            S.dma("act", VB[kslot0:kslot0 + n, :].rearrange("(s p) c -> p s c", p=128), stg_vb.t[:, 0:nsub, :],
                  stg_vb.b, reads=[stg_vb.b])
            if own_row0 is not None:
                S.dma("act", QBT[:, :, own_row0:own_row0 + n].rearrange("h d t -> d h t"), stg_qb.t[:, :, 0:n],
                      stg_qb.b, reads=[stg_qb.b])

        band_tile(xl, 0, 4, KB_HA)
        if not SMALL:
            band_tile(xl, 512, 4, KB_HB)
        for t in range(1 if SMALL else 2):
            band_tile(xo, t * 512, 4, KB_A + t * 512, own_row0=t * 512)
        for t in range(0 if SMALL else 2):
            band_tile(xo, 1024 + t * 512, 4, KB_B + t * 512, own_row0=1024 + t * 512)
        band_tile(xo, 2048, 1, KB_S, own_row0=2048)
        S.barrier()
        sb.release(pm0)

    att = {"i": 0, "u": 0, "pend": deque()}
    SKEW = 3

    def att_flush():
        while att["pend"]:
            att["pend"].popleft()()

    def attn_unit(qparts, ktiles, ncols, ot_dst, pT, obufs, rec):
        nt = len(ktiles)
        u = att["u"]
        att["u"] += 1
        bo, bl = (4, 5) if u % 2 == 0 else (6, 7)
        obuf = obufs[u % len(obufs)]
        rc = rec["rc"][u % len(rec["rc"])]
        for ti, kt in enumerate(ktiles):
            nk, c0 = kt["nk"], kt["c0"]
            w = ncols - c0
            bi = att["i"] % 4
            p = pT[att["i"] % len(pT)]
            tmp = rec["tmp"][att["i"] % len(rec["tmp"])] if "tmp" in rec else None
            att["i"] += 1
            sbk = PB[bi]
            groups = []
            np_ = len(qparts)
            multi = kt.get("multi")
            if multi is None:
                for pi, ((qa, qb_), (ka, kb_)) in enumerate(zip(qparts, kt["kparts"])):
                    groups.append((pbank(bi)[0:nk, 0:w], ka, qa[:, c0:ncols], pi == 0, pi == np_ - 1))
                S.op("pe", f_mm(groups), reads=[q[1] for q in qparts] + [k[1] for k in kt["kparts"]], writes=[sbk.b])
            else:
                kbufs = []
                for m_, (kps, _) in enumerate(multi):
                    for pi, ((qa, qb_), (ka, kb_)) in enumerate(zip(qparts, kps)):
                        groups.append((pbank(bi)[0:nk, m_ * ncols:(m_ + 1) * ncols], ka, qa[:, 0:ncols], pi == 0, pi == np_ - 1))
                        kbufs.append(kb_)
                S.op("pe", f_mm(groups), reads=[q[1] for q in qparts] + kbufs, writes=[sbk.b])
                w = len(multi) * ncols
            src = pbank(bi)[0:nk, 0:w]
            if kt.get("btab") is not None:
                ba, bb = kt["btab"]
                S.op("dve", f_stt(tmp.t[0:nk, 0:w], src, kt["scale"], ba, ALU.mult, ALU.add),
                     reads=[sbk.b, bb], writes=[tmp.b])
                S.op("act", f_act(p.t[0:nk, 0:w], tmp.t[0:nk, 0:w], AF.Exp, bias=kt["bias"]),
                     reads=[tmp.b, flags.b], writes=[p.b])
            else:
                S.op("act", f_act(p.t[0:nk, 0:w], src, AF.Exp, bias=kt["bias"], scale=kt["scale"]),
                     reads=[sbk.b, flags.b], writes=[p.b])
            if kt.get("zero") is not None:
                r0, r1, cc0, cc1 = kt["zero"]
                S.op("pool", f_memset(p.t[r0:r1, cc0:cc1], 0.0), writes=[p.b])

            def stage_c(kt=kt, p=p, ti=ti, nk=nk, c0=c0, w=w, multi=multi):
                if multi is None:
                    va, vb_ = kt["v"]
                    S.op("pe", f_mm([(pbank(bo)[:, c0:ncols], va, p.t[0:nk, 0:w], ti == 0, ti == nt - 1),
                                     (pbank(bl)[:, c0:ncols], ones.t[0:nk, :], p.t[0:nk, 0:w], ti == 0, ti == nt - 1)]),
                         reads=[p.b, vb_, ones.b], writes=[PB[bo].b, PB[bl].b])
                else:
                    mms, vbufs = [], []
                    nm_ = len(multi)
                    for m_, (_, (va, vb_)) in enumerate(multi):
                        pm = p.t[0:nk, m_ * ncols:(m_ + 1) * ncols]
                        first = (ti == 0 and m_ == 0)
                        last = (ti == nt - 1 and m_ == nm_ - 1)
                        mms.append((pbank(bo)[:, 0:ncols], va, pm, first, last))
                        mms.append((pbank(bl)[:, 0:ncols], ones.t[0:nk, :], pm, first, last))
                        vbufs.append(vb_)
                    S.op("pe", f_mm(mms), reads=[p.b, ones.b] + vbufs, writes=[PB[bo].b, PB[bl].b])
                if ti == nt - 1:
                    S.op("dve", f_recip(rc.t[:, 0:ncols], pbank(bl)[:, 0:ncols]), reads=[PB[bl].b], writes=[rc.b])
                    S.op("dve", f_tt(obuf.t[:, 0:ncols], pbank(bo)[:, 0:ncols], rc.t[:, 0:ncols], ALU.mult),
                         reads=[PB[bo].b, rc.b], writes=[obuf.b])
                    S.dma("act", ot_dst, obuf.t[:, 0:ncols], obuf.b, reads=[obuf.b])

            att["pend"].append(stage_c)
            while len(att["pend"]) > SKEW:
                att["pend"].popleft()()

    if "p2" in PHASES:
        NK2 = NHIST + 2 * SEG
        ktn = [sb.alloc([128, NK2], BF16, "ktn%d" % i) for i in range(2)]
        ktp = [sb.alloc([128, NK2], BF16, "ktp%d" % i) for i in range(2)]
        vv = [sb.alloc([128, NK2 // 128, 128], BF16, "vv%d" % i) for i in range(2)]
        qtn = [sb.alloc([128, 2 * SEG], BF16, "qtn%d" % i) for i in range(2)]
        qtp = [sb.alloc([128, 2 * SEG], BF16, "qtp%d" % i) for i in range(2)]
        for i_ in range(2):
            S.op("pool", f_memset(ktp[i_].t[64:128, :], 0.0), writes=[ktp[i_].b])
            S.op("pool", f_memset(qtp[i_].t[64:128, :], 0.0), writes=[qtp[i_].b])
        pT = [sb.alloc([128, 512], BF16, "pT%d" % i) for i in range(6)]
        ob = [sb.alloc([128, 512], BF16, "ob%d" % i) for i in range(2)]
        rec = {"rc": [sb.alloc([128, 512], F32, "rc%d" % i) for i in range(2)]}

        def load_head(h):
            i = h % 2
            for c in range(0, NK2, 2304):
                pass
            S.dma("sp", ktn[i].t[:], KTn[h, :, 0:NK2], ktn[i].b, writes=[ktn[i].b])
            S.dma("sp", ktp[i].t[0:64, :], KTp[h, :, 0:NK2], ktp[i].b, writes=[ktp[i].b])
            vsrc = Vs[0:NK2, h * 128:(h + 1) * 128].rearrange("(t p) d -> p t d", p=128)
            dma_split("sp", [(vv[i].t[:, t0:t0 + 8, :], vsrc[:, t0:t0 + 8, :]) for t0 in range(0, NK2 // 128, 8)], vv[i].b)
            S.dma("sp", qtn[i].t[:], QTn[h, :, 0:2 * SEG], qtn[i].b, writes=[qtn[i].b])
            S.dma("sp", qtp[i].t[0:64, :], QTp[h, :, 0:2 * SEG], qtp[i].b, writes=[qtp[i].b])

        DO3 = "p3" in PHASES
        NKB2 = KB_S
        if DO3:
            kbt = [sb.alloc([128, NKB2], BF16, "kbt%d" % i) for i in range(2)]
            vbt = [sb.alloc([128, NKB2 // 128, 128], BF16, "vbt%d" % i) for i in range(2)]
            qbt = [sb.alloc([128, 2 * SEG], BF16, "qbt%d" % i) for i in range(2)]
            btab = [sb.alloc([128, 5, 128], F32, "btab%d" % i) for i in range(2)]
            bmk = sb.alloc([128, 5, 128], F32, "bmk")
            pTb = [sb.alloc([128, 512], BF16, "pTb%d" % i) for i in range(6)]
            obb = [sb.alloc([128, 128], BF16, "obb%d" % i) for i in range(2)]
            recb = {"rc": [sb.alloc([128, 128], F32, "rcb%d" % i) for i in range(2)],
                    "tmp": [sb.alloc([128, 512], F32, "tmpb%d" % i) for i in range(4)]}
            S.dma("sp", bmk.t[:], bmask.rearrange("(t p) q -> p t q", p=128), bmk.b, writes=[bmk.b])

        def load_bhead(h):
            i = h % 2
            S.dma("sp", kbt[i].t[:], KBT[h, :, 0:NKB2], kbt[i].b, writes=[kbt[i].b])
            vsrc = VB[0:NKB2, h * 128:(h + 1) * 128].rearrange("(t p) d -> p t d", p=128)
            dma_split("sp", [(vbt[i].t[:, t0:t0 + 8, :], vsrc[:, t0:t0 + 8, :]) for t0 in range(0, NKB2 // 128, 8)], vbt[i].b)
            S.dma("sp", qbt[i].t[:], QBT[h, :, 0:2 * SEG], qbt[i].b, writes=[qbt[i].b])
            S.dma("sp", btab[i].t[:], bbias[h].rearrange("(t p) q -> p t q", p=128), btab[i].b, writes=[btab[i].b])
            S.op("pool", f_tt(btab[i].t[:], btab[i].t[:], bmk.t[:], ALU.add), reads=[btab[i].b, bmk.b], writes=[btab[i].b])

        def band_units(h, lo, hi):
            i = h % 2
            for un in range(lo, hi):
                seg, pr = un // 8, un % 8
                kbase = KB_HA if seg == 0 else KB_HB
                q0 = seg * SEG + pr * 128
                qparts = [(qbt[i].t[:, q0:q0 + 128], qbt[i].b)]
                tiles = []

                def one(t):
                    k0 = kbase + pr * 128 + t * 128
                    is_halo = (pr * 128 + t * 128) < 512
                    fcol = 10 if (seg == 0 and is_halo) else 11
                    return dict(kparts=[(kbt[i].t[:, k0:k0 + 128], kbt[i].b)],
                                v=(vbt[i].t[:, k0 // 128, :], vbt[i].b), nk=128, c0=0,
                                bias=flags.t[:, fcol:fcol + 1], scale=SCALE_B,
                                btab=(btab[i].t[:, t, :], btab[i].b)), fcol

                singles = [one(t) for t in range(5)]
                if len(set(fc for (_, fc) in singles[0:4])) == 1:
                    fc = singles[0][1]
                    tiles.append(dict(multi=[(d_["kparts"], d_["v"]) for (d_, _) in singles[0:4]], nk=128, c0=0,
                                      bias=flags.t[:, fc:fc + 1], scale=SCALE_B,
                                      btab=(btab[i].t[:, 0:4, :].rearrange("p t q -> p (t q)"), btab[i].b)))
                    tiles.append(singles[4][0])
                else:
                    tiles = [d_ for (d_, _) in singles]
                attn_unit(qparts, tiles, 128, OT[H + h, :, q0:q0 + 128], pTb, obb, recb)

        load_head(0)
        if DO3:
            load_bhead(0)
        ui = 0
        H2 = 1 if SMALL else H
        for h in range(H2):
            att_flush()
            if h + 1 < H2:
                load_head(h + 1)
            i = h % 2
            for qb_ in range(4):
                seg = qb_ // 2
                half = qb_ % 2
                q0 = qb_ * 512
                qparts = [(qtn[i].t[:, q0:q0 + 512], qtn[i].b), (qtp[i].t[:, q0:q0 + 512], qtp[i].b)]
                tiles = []
                nslots = 3 if seg == 0 else 7
                for s_ in range(nslots):
                    fcol = s_ if seg == 0 else 3 + s_
                    for t in range(8):
                        k0 = s_ * SEG + t * 128
                        tiles.append(dict(kparts=[(ktn[i].t[:, k0:k0 + 128], ktn[i].b), (ktp[i].t[:, k0:k0 + 128], ktp[i].b)],
                                          v=(vv[i].t[:, k0 // 128, :], vv[i].b), nk=128, c0=0,
                                          bias=flags.t[:, fcol:fcol + 1], scale=SCALE_A))
                own0 = NHIST + seg * SEG
                for t in range(4 * half + 4):
                    k0 = own0 + t * 128
                    rel = t - 4 * half
                    c0 = max(0, rel) * 128
                    z = (64, 128, 0, 64) if rel >= 0 else None
                    tiles.append(dict(kparts=[(ktn[i].t[:, k0:k0 + 128], ktn[i].b), (ktp[i].t[:, k0:k0 + 128], ktp[i].b)],
                                      v=(vv[i].t[:, k0 // 128, :], vv[i].b), nk=128, c0=c0,
                                      bias=flags.t[:, 11:12], scale=SCALE_A, zero=z))
                attn_unit(qparts, tiles, 512, OT[h, :, q0:q0 + 512], pT, ob, rec)
                bg_step(2)
                ui += 1
        if DO3:
            for h in range(H2):
                att_flush()
                if h + 1 < H2:
                    load_bhead(h + 1)
                band_units(h, 0, 16)
        att_flush()
        S.barrier()
        sb.release(pm0)

    if "p4" in PHASES:
        cb = [sb.alloc([128, 1024], BF16, "cb%d" % i) for i in range(2)]
        stc = [sb.alloc([128, 8, 128], BF16, "stc%d" % i) for i in range(2)]
        for r in range(8):
            c_ = cb[r % 2]
            S.dma("pool", c_.t[:], c_bk[r * 128:(r + 1) * 128, :], c_.b, writes=[c_.b])
            pn = pbank_bf(r % 2).rearrange("p (h t) -> p h t", h=8)
            S.op("pe", f_tr([(pn[:, h, :], c_.t[:, h * 128:(h + 1) * 128]) for h in range(8)], ident.t[:]),
                 reads=[c_.b, ident.b], writes=[PB[r % 2].b])
            S.op("dve", f_copy(stc[r % 2].t[:], pn), reads=[PB[r % 2].b], writes=[stc[r % 2].b])
            S.dma("sp", KBT_c[:, :, r * 128:(r + 1) * 128].rearrange("h d t -> d h t"), stc[r % 2].t[:], stc[r % 2].b,
                  reads=[stc[r % 2].b])
        S.barrier()
        sb.release(pm0)
        wo_pref = sb.alloc([128, 16, D], BF16, "wo")
        load_w(wo_pref, w_o, 16, step=2)
        pm_p4 = sb.mark()
        NKS = PAST + 32
        ktn = [sb.alloc([128, NKS], BF16, "sktn%d" % i) for i in range(2)]
        ktp = [sb.alloc([64, NKS], BF16, "sktp%d" % i) for i in range(2)]
        vv = [sb.alloc([128, 9, 128], BF16, "svv%d" % i) for i in range(2)]
        qtn = [sb.alloc([128, 32], BF16, "sqtn%d" % i) for i in range(2)]
        qtp = [sb.alloc([64, 32], BF16, "sqtp%d" % i) for i in range(2)]
        kbt = [sb.alloc([128, LB + 32], BF16, "skbt%d" % i) for i in range(2)]
        vbt = [sb.alloc([128, 4, 128], BF16, "svbt%d" % i) for i in range(2)]
        vbn = [sb.alloc([32, 128], BF16, "svbn%d" % i) for i in range(2)]
        qbt = [sb.alloc([128, 32], BF16, "sqbt%d" % i) for i in range(2)]
        btab = [sb.alloc([128, 5, 32], F32, "sbtab%d" % i) for i in range(2)]
        pT = [sb.alloc([128, 32], BF16, "spT%d" % i) for i in range(6)]
        ob = [sb.alloc([128, 32], BF16, "sob%d" % i) for i in range(2)]
        rec = {"rc": [sb.alloc([128, 32], F32, "src%d" % i) for i in range(2)],
               "tmp": [sb.alloc([128, 32], F32, "stmp%d" % i) for i in range(4)]}
        units = [(sbi, h) for sbi in range(1 if SMALL else 2) for h in range(1 if SMALL else H)]

        def p4_load(n):
            sbi, h = units[n]
            i = n % 2
            qrow = 2048 + sbi * 32
            knew = NHIST + 2048 + sbi * 32
            kbnew = KB_S + sbi * 32
            dma_split("sp", [(ktn[i].t[:, 0:PAST], KTn_c[h, :, sbi * PAST:(sbi + 1) * PAST]),
                             (ktn[i].t[:, PAST:NKS], KTn[h, :, knew:knew + 32])], ktn[i].b)
            dma_split("sp", [(ktp[i].t[:, 0:PAST], KTp_c[h, :, sbi * PAST:(sbi + 1) * PAST]),
                             (ktp[i].t[:, PAST:NKS], KTp[h, :, knew:knew + 32])], ktp[i].b)
            dma_split("sp", [(vv[i].t[:, 0:8, :], V_c[sbi * PAST:(sbi + 1) * PAST, h * 128:(h + 1) * 128].rearrange(
                "(t p) d -> p t d", p=128)), (vv[i].t[0:32, 8, :], Vs[knew:knew + 32, h * 128:(h + 1) * 128])], vv[i].b)
            S.dma("sp", qtn[i].t[:], QTn[h, :, qrow:qrow + 32], qtn[i].b, writes=[qtn[i].b])
            S.dma("sp", qtp[i].t[:], QTp[h, :, qrow:qrow + 32], qtp[i].b, writes=[qtp[i].b])
            dma_split("sp", [(kbt[i].t[:, 0:LB], KBT_c[h, :, sbi * LB:(sbi + 1) * LB]),
                             (kbt[i].t[:, LB:LB + 32], KBT[h, :, kbnew:kbnew + 32])], kbt[i].b)
            S.dma("pool", vbt[i].t[:], c_bv[sbi * LB:(sbi + 1) * LB, h * 128:(h + 1) * 128].rearrange(
                "(t p) d -> p t d", p=128), vbt[i].b, writes=[vbt[i].b])
            S.dma("sp", vbn[i].t[:], VB[kbnew:kbnew + 32, h * 128:(h + 1) * 128], vbn[i].b, writes=[vbn[i].b])
            S.dma("sp", qbt[i].t[:], QBT[h, :, qrow:qrow + 32], qbt[i].b, writes=[qbt[i].b])
            S.dma("sp", btab[i].t[:], sbias[h].rearrange("(t p) q -> p t q", p=128), btab[i].b, writes=[btab[i].b])

        def p4_compute(n):
            sbi, h = units[n]
            i = n % 2
            qrow = 2048 + sbi * 32
            qparts = [(qtn[i].t[:, :], qtn[i].b), (qtp[i].t[:, :], qtp[i].b)]
            tiles = []
            for t in range(9):
                nk = 128 if t < 8 else 32
                k0 = t * 128
                tiles.append(dict(kparts=[(ktn[i].t[:, k0:k0 + nk], ktn[i].b), (ktp[i].t[:, k0:k0 + nk], ktp[i].b)],
                                  v=(vv[i].t[0:nk, t, :], vv[i].b), nk=nk, c0=0, bias=flags.t[0:nk, 11:12],
                                  scale=SCALE_A))
            attn_unit(qparts, tiles, 32, OT[h, :, qrow:qrow + 32], pT, ob, rec)
            qparts = [(qbt[i].t[:, :], qbt[i].b)]
            tiles = []
            for t in range(5):
                nk = 128 if t < 4 else 32
                k0 = t * 128
                vsrc = (vbt[i].t[:, t, :], vbt[i].b) if t < 4 else (vbn[i].t[:, :], vbn[i].b)
                tiles.append(dict(kparts=[(kbt[i].t[:, k0:k0 + nk], kbt[i].b)], v=vsrc,
                                  nk=nk, c0=0, bias=flags.t[0:nk, 11:12], scale=SCALE_B,
                                  btab=(btab[i].t[0:nk, t, :], btab[i].b)))
            attn_unit(qparts, tiles, 32, OT[H + h, :, qrow:qrow + 32], pT, ob, rec)

        p4_load(0)
        for n in range(len(units)):
            att_flush()
            if n + 1 < len(units):
                p4_load(n + 1)
            p4_compute(n)
        att_flush()
        S.barrier()
        sb.release(pm_p4)

    if "p5" in PHASES:
        bg_step(1000)
        if "p4" in PHASES:
            wo = wo_pref
        else:
            wo = sb.alloc([128, 16, D], BF16, "wo")
            load_w(wo, w_o, 16, step=2)
        otb = [sb.alloc([128, 16, 512], BF16, "otb%d" % i) for i in range(2)]
        xr = [sb.alloc([128, D], F32, "xr%d" % i) for i in range(2)]
        hb = [sb.alloc([128, D], F32, "hb%d" % i) for i in range(2)]
        t5a = [(1536, 4), (2048, 1)] if SMALL else [(0, 4), (512, 4), (1024, 4), (1536, 4), (2048, 1)]
        sc = 0
        for ti_, (tok0, nsub) in enumerate(t5a):
            ob_ = otb[ti_ % 2]
            n = nsub * 128
            osrc = OT[:, :, tok0:tok0 + n].rearrange("h d t -> d h t")
            dma_split("sp", [(ob_.t[:, h0:h0 + 8, 0:n], osrc[:, h0:h0 + 8, :]) for h0 in (0, 8)], ob_.b)
            for s_ in range(nsub):
                i = sc % 2
                sc += 1
                r0 = tok0 + s_ * 128
                S.dma("sp", xr[i].t[:], xo[r0:r0 + 128, :], xr[i].b, writes=[xr[i].b])
                for g in range(4):
                    bk = (sc * 4 + g) % 8
                    S.op("pe", f_mm([(pbank(bk), ob_.t[:, k, s_ * 128:(s_ + 1) * 128], wo.t[:, k, g * 512:(g + 1) * 512],
                                      k == 0, k == 15) for k in range(16)]), reads=[ob_.b, wo.b], writes=[PB[bk].b])
                    S.op("dve", f_tt(hb[i].t[:, g * 512:(g + 1) * 512], pbank(bk), xr[i].t[:, g * 512:(g + 1) * 512], ALU.add),
                         reads=[PB[bk].b, xr[i].b], writes=[hb[i].b])
                S.dma("act", Hs[r0:r0 + 128, :], hb[i].t[:], hb[i].b, reads=[hb[i].b])
        S.barrier()
        sb.release(pm0)

        TMAX = 640
        uT = sb.alloc([128, 64, TMAX], BF16, "uT")
        hn2T = sb.alloc([128, 16, TMAX], BF16, "hn2T")
        hx = [sb.alloc([128, D], F32, "hx%d" % i) for i in range(2)]
        junk = sb.alloc([128, D], BF16, "junk5")
        st = sb.alloc([128, 8], F32, "st5")
        xn = sb.alloc([128, D], BF16, "xn5")
        wup = [sb.alloc([128, 16, 256], BF16, "wup%d" % i) for i in range(3)]
        wdn = [sb.alloc([128, 16, 256], BF16, "wdn%d" % i) for i in range(3)]
        rl = [sb.alloc([128, 512], F32, "rl%d" % i) for i in range(2)]
        hres = [sb.alloc([128, 256], F32, "hres%d" % i) for i in range(2)]
        yo = [sb.alloc([128, 256], F32, "yo%d" % i) for i in range(2)]
        tiles5 = [(1536, 5)] if SMALL else [(0, 4), (512, 4), (1024, 4), (1536, 5)]
        cx = 0
        wi = 0
        di = 0
        ri = 0
        for (tok0, nsub) in tiles5:
            T = nsub * 128
            for s_ in range(nsub):
                xb = hx[cx % 2]
                cx += 1
                r0 = tok0 + s_ * 128
                S.dma("sp", xb.t[:], Hs[r0:r0 + 128, :], xb.b, writes=[xb.b])
                S.op("pool", f_memset(st.t[:, 0:1], 0.0), writes=[st.b])
                S.op("dve", f_stt(junk.t[:], xb.t[:], 1.0, xb.t[:], ALU.mult, ALU.mult, accum=st.t[:, 0:1]),
                     reads=[xb.b], writes=[junk.b, st.b])
                rstd_from_ss(st.t[:, 0:1], st.b, 1, D, st)
                S.op("act", f_act(xn.t[:], xb.t[:], AF.Copy, scale=st.t[:, 0:1]), reads=[xb.b, st.b], writes=[xn.b])
                for half in range(2):
                    pv = pbank_bf(half).rearrange("p (k t) -> p k t", k=8)
                    S.op("pe", f_tr([(pv[:, k, :], xn.t[:, (half * 8 + k) * 128:(half * 8 + k + 1) * 128])
                                     for k in range(8)], ident.t[:]), reads=[xn.b, ident.b], writes=[PB[half].b])
                    S.op("dve", f_tt(hn2T.t[:, half * 8:half * 8 + 8, s_ * 128:(s_ + 1) * 128], pv,
                                     nffn.t[:, half * 8:half * 8 + 8].unsqueeze(2).to_broadcast([128, 8, 128]), ALU.mult),
                         reads=[PB[half].b, nffn.b], writes=[hn2T.b])
            for gcol in range(DFF // 256):
                wb = wup[wi % 3]
                wi += 1
                S.dma("sp", wb.t[:], WU[gcol], wb.b, writes=[wb.b])
                for m2 in range(2):
                    m = gcol * 2 + m2
                    pieces = [(0, min(T, 512))] + ([(512, T)] if T > 512 else [])
                    for (a, b_) in pieces:
                        bk = 2 + (ri % 4)
                        r_ = rl[ri % 2]
                        ri += 1
                        S.op("pe", f_mm([(pbank(bk)[:, 0:b_ - a], wb.t[:, k, m2 * 128:(m2 + 1) * 128], hn2T.t[:, k, a:b_],
                                          k == 0, k == 15) for k in range(16)]), reads=[wb.b, hn2T.b], writes=[PB[bk].b])
                        S.op("act", f_act(r_.t[:, 0:b_ - a], pbank(bk)[:, 0:b_ - a], AF.Relu), reads=[PB[bk].b],
                             writes=[r_.b])
                        S.op("pool" if (ri % 2) else "dve", f_tt(uT.t[:, m, a:b_], r_.t[:, 0:b_ - a], r_.t[:, 0:b_ - a],
                                                                  ALU.mult), reads=[r_.b], writes=[uT.b])
            for gcol in range(D // 256):
                for pc in range(4):
                    wb = wdn[di % 3]
                    di += 1
                    S.dma("sp", wb.t[:], WD[gcol][:, pc * 16:(pc + 1) * 16, :], wb.b, writes=[wb.b])
                    for s_ in range(nsub):
                        bk = 2 + s_ if s_ < 4 else 0
                        S.op("pe", f_mm([(pbank(bk)[:, 0:256], uT.t[:, pc * 16 + k, s_ * 128:(s_ + 1) * 128], wb.t[:, k, :],
                                          pc == 0 and k == 0, pc == 3 and k == 15) for k in range(16)]),
                             reads=[wb.b, uT.b], writes=[PB[bk].b])
                for s_ in range(nsub):
                    bk = 2 + s_ if s_ < 4 else 0
                    r0 = tok0 + s_ * 128
                    hr = hres[(gcol * 5 + s_) % 2]
                    y_ = yo[(gcol * 5 + s_) % 2]
                    S.dma("act", hr.t[:], Hs[r0:r0 + 128, gcol * 256:(gcol + 1) * 256], hr.b, writes=[hr.b])
                    S.op("dve", f_tt(y_.t[:], pbank(bk)[:, 0:256], hr.t[:], ALU.add), reads=[PB[bk].b, hr.b], writes=[y_.b])
                    S.dma("act", y_o[r0:r0 + 128, gcol * 256:(gcol + 1) * 256], y_.t[:], y_.b, reads=[y_.b])
        S.barrier()
        sb.release(pm0)

    S.barrier()
    with nc.Block() as block:
        @block.tensor
        def _(e):
            S.emit("pe", e)

        @block.scalar
        def _(e):
            S.emit("act", e)

        @block.vector
        def _(e):
            S.emit("dve", e)

        @block.gpsimd
        def _(e):
            S.emit("pool", e)

        @block.sync
        def _(e):
            S.emit("sp", e)
    return nc


def _rope_table(pos):
    inv = 1.0 / (10000.0 ** (np.arange(0, 64, 2, dtype=np.float32) / 64.0))
    ang = pos.astype(np.float32)[:, None] * inv[None, :].astype(np.float32)
    return np.concatenate([np.cos(ang), np.sin(ang)], axis=1).astype(np.float32)


def _host_inputs(inp):
    f32 = np.float32
    xp = np.asarray(inp["x_prompt"], f32)
    xs = np.asarray(inp["x_sample"], f32)
    rb = np.asarray(inp["rel_bias"], f32)[0]
    kl = np.arange(640)[:, None]
    ql = np.arange(128)[None, :]
    idx = np.clip(ql - kl + 512, -128, 128) + 128
    bbias = np.ascontiguousarray(rb[:, idx])
    qc = ql // 64
    kc = kl // 64
    allowed = (kc <= qc + 8) & (kc >= qc)
    bmask = np.where(allowed, 0.0, NEGM).astype(f32)
    ks = np.arange(640)[:, None]
    ts = np.arange(32)[None, :]
    rel_s = np.where(ks < 512, 512 + ts - ks, ts - (ks - 512))
    sidx = np.clip(rel_s, -128, 128) + 128
    sbias = np.ascontiguousarray(rb[:, sidx])
    shared = {
        "bbias": bbias, "bmask": bmask, "sbias": sbias, "ident": np.eye(128, dtype=f32),
        "rope_h": _rope_table(np.arange(NHIST)), "rope_c": _rope_table(np.arange(PAST)),
    }
    for k in ("norm_mix", "norm_ffn", "w_in", "g_cq", "w_uq", "g_ckv", "w_uk", "w_uv", "g_qa", "g_ka", "g_qb", "g_kb",
              "w_o", "w_up", "w_down"):
        shared[k] = np.ascontiguousarray(np.asarray(inp[k], f32)[0])
    cck = np.asarray(inp["cache_mla_ckv"], f32)[0]
    ckp = np.asarray(inp["cache_mla_kpe"], f32)[0]
    cbk = np.asarray(inp["cache_band_k"], f32)[0].reshape(16, LB, 1024)
    cbv = np.asarray(inp["cache_band_v"], f32)[0].reshape(16, LB, 1024)
    maps = []
    for c in range(8):
        b, j = c // 4, c % 4
        a0, b0 = j * SEG, (7 - j) * SEG
        m = dict(shared)
        m["xh"] = np.ascontiguousarray(xp[b, 0:NHIST])
        xo = np.zeros((NOWN, D), f32)
        xo[0:SEG] = xp[b, a0:a0 + SEG]
        xo[SEG:2 * SEG] = xp[b, b0:b0 + SEG]
        xo[2048:2080] = xs[2 * c]
        xo[2080:2112] = xs[2 * c + 1]
        m["xo"] = xo
        xl = np.zeros((NHALO, D), f32)
        if a0 >= 512:
            xl[0:512] = xp[b, a0 - 512:a0]
        xl[512:1024] = xp[b, b0 - 512:b0]
        m["xl"] = xl
        m["c_ckv"] = np.ascontiguousarray(cck[2 * c:2 * c + 2].reshape(2 * PAST, 256))
        m["c_kpe"] = np.ascontiguousarray(ckp[2 * c:2 * c + 2].reshape(2 * PAST, 64))
        m["c_bk"] = np.ascontiguousarray(cbk[2 * c:2 * c + 2].reshape(2 * LB, 1024))
        m["c_bv"] = np.ascontiguousarray(cbv[2 * c:2 * c + 2].reshape(2 * LB, 1024))
        pos_o = np.concatenate([np.arange(a0, a0 + SEG), np.arange(b0, b0 + SEG), PAST + np.arange(32),
                                PAST + np.arange(32), np.zeros(64, np.int64)])
        m["rope_o"] = _rope_table(pos_o)
        fl = np.zeros((128, 16), f32)
        for s_ in range(3):
            fl[:, s_] = 0.0 if s_ < j else NEGM
        for s_ in range(7):
            fl[:, 3 + s_] = 0.0 if s_ < 7 - j else NEGM
        fl[:, 10] = NEGM if j == 0 else 0.0
        m["flg"] = fl
        maps.append(m)
    return maps


_NC_CACHE = {}


def kernel(**inputs):
    maps = _host_inputs(inputs)
    if "nc" not in _NC_CACHE:
        _NC_CACHE["nc"] = build_nc()
    nc = _NC_CACHE["nc"]
    res = run_bass_kernel_spmd(nc, maps, core_ids=list(range(8)))
    R = res.results
    f32 = np.float32
    S_ = 8192
    y_p = np.zeros((2, S_, D), f32)
    ckv_p = np.zeros((1, 2, S_, 256), f32)
    kpe_p = np.zeros((1, 2, S_, 64), f32)
    bk_p = np.zeros((1, 2, 512, 8, 128), f32)
    bv_p = np.zeros((1, 2, 512, 8, 128), f32)
    y_s = np.zeros((16, 32, D), f32)
    ckv_s = np.zeros((1, 16, 32, 256), f32)
    kpe_s = np.zeros((1, 16, 32, 64), f32)
    bk_s = np.zeros((1, 16, 32, 8, 128), f32)
    bv_s = np.zeros((1, 16, 32, 8, 128), f32)
    for c in range(8):
        b, j = c // 4, c % 4
        a0, b0 = j * SEG, (7 - j) * SEG
        r = R[c]
        for (dst, src, w) in ((y_p, "y_o", None), (ckv_p[0], "ckv_o", None), (kpe_p[0], "kpe_o", None)):
            dst[b, a0:a0 + SEG] = r[src][0:SEG]
            dst[b, b0:b0 + SEG] = r[src][SEG:2 * SEG]
        if j == 0:
            bk_p[0, b] = r["bk_o"][1536:2048].reshape(512, 8, 128)
            bv_p[0, b] = r["bv_o"][1536:2048].reshape(512, 8, 128)
        for sbi in range(2):
            rows = slice(2048 + 32 * sbi, 2048 + 32 * sbi + 32)
            y_s[2 * c + sbi] = r["y_o"][rows]
            ckv_s[0, 2 * c + sbi] = r["ckv_o"][rows]
            kpe_s[0, 2 * c + sbi] = r["kpe_o"][rows]
            bk_s[0, 2 * c + sbi] = r["bk_o"][rows].reshape(32, 8, 128)
            bv_s[0, 2 * c + sbi] = r["bv_o"][rows].reshape(32, 8, 128)
    return (y_p, y_s, ckv_p, kpe_p, bk_p, bv_p, ckv_s, kpe_s, bk_s, bv_s)
```

```python
import numpy as np
from collections import deque
from contextlib import ExitStack
import concourse.bass as bass
import concourse.mybir as mybir
from concourse.bass_utils import run_bass_kernel_spmd

F32 = mybir.dt.float32
BF16 = mybir.dt.bfloat16
AF = mybir.ActivationFunctionType
ALU = mybir.AluOpType
AX = mybir.AxisListType

D = 2048
H = 8
SEG = 1024
NHIST = 7 * SEG
NOWN = 2 * SEG + 128
NHALO = 1024
NKM = NHIST + NOWN
NKB = 512 + SEG + 512 + SEG + 128
KB_HA, KB_A, KB_HB, KB_B, KB_S = 0, 512, 1536, 2048, 3072
PAST = 1024
LB = 512
EPS = 1e-6
SCALE_A = 192 ** -0.5
SCALE_B = 128 ** -0.5
NEGM = -30000.0
DFF = 8192
IN_COLS = 3904

PHASES = {"p1", "p2", "p3", "p4", "p5"}
SMALL = False


class Buf:
    __slots__ = ("name", "w", "r", "sem", "semval", "semname", "kind", "alt")

    def __init__(self, name):
        self.name = name
        self.w = None
        self.r = {}
        self.sem = None
        self.semval = 0
        self.semname = None
        self.kind = None
        self.alt = None


class Tn:
    def __init__(self, t, name):
        self.t = t
        self.b = Buf(name)


ENGS = ("pe", "act", "dve", "pool", "sp")


class Sched:
    def __init__(self, nc, es):
        self.nc = nc
        self.es = es
        self.q = {e: [] for e in ENGS}
        self.esem = {e: es.enter_context(nc.semaphore("s_" + e)) for e in ENGS}
        self.ecnt = {e: 0 for e in ENGS}
        self.seen = {e: {} for e in ENGS}
        self.dbufs = []
        self.nbuf = 0
        self.rec = None
        self.semfinal = {}
        self.freesems = []
        self.nsem = 0

    def _waits(self, eng, reads, writes):
        need = {}

        def add(tok):
            if tok is None:
                return
            k, h, v = tok
            if k not in need or need[k][1] < v:
                need[k] = (h, v)

        for b in reads:
            add(b.w)
        for b in writes:
            add(b.w)
            for tok in b.r.values():
                add(tok)
        for k, (h, v) in need.items():
            if self.seen[eng].get(k, 0) >= v:
                continue
            self.seen[eng][k] = v
            self.q[eng].append(("w", h, v))

    def _mark(self, tok, reads, writes):
        for b in reads:
            b.r[tok[0]] = tok
        for b in writes:
            b.w = tok
            b.r = {}

    def start_rec(self):
        self.rec = []

    def stop_rec(self):
        r = self.rec
        self.rec = None
        return r

    def interleave(self, lists):
        idx = [0] * len(lists)
        live = True
        while live:
            live = False
            for li, l in enumerate(lists):
                if idx[li] < len(l):
                    it = l[idx[li]]
                    idx[li] += 1
                    live = True
                    if it[0] == "op":
                        self.op(*it[1:])
                    else:
                        self.dma(*it[1:])

    def op(self, eng, fn, reads=(), writes=()):
        if self.rec is not None:
            self.rec.append(("op", eng, fn, tuple(reads), tuple(writes)))
            return None
        self._waits(eng, reads, writes)
        self.ecnt[eng] += 1
        tok = ("e_" + eng, self.esem[eng], self.ecnt[eng])
        self.q[eng].append(("o", fn, self.esem[eng], 1))
        if eng == "pe":
            self.seen[eng]["e_pe"] = self.ecnt[eng]
        self._mark(tok, reads, writes)
        return tok

    def dma(self, queue, out_ap, in_ap, semb, reads=(), writes=(), slow=False):
        if self.rec is not None:
            self.rec.append(("dma", queue, out_ap, in_ap, semb, tuple(reads), tuple(writes), slow))
            return None
        self._waits(queue, reads, writes)
        kind = "sw" if queue == "pool" else "hw"
        if semb.sem is not None and semb.kind != kind:
            if semb.alt is None:
                semb.alt = Buf(semb.name + "_alt")
            semb = semb.alt
        if semb.sem is None:
            fl = [x for x in self.freesems if x[0] == kind]
            semb.kind = kind
            if fl:
                self.freesems.remove(fl[-1])
                _, semb.semname, semb.sem, semb.semval = fl[-1]
            else:
                semb.semname = "d%d" % self.nsem
                self.nsem += 1
                semb.sem = self.es.enter_context(self.nc.semaphore(semb.semname))
            self.dbufs.append(semb)
        semb.semval += 16
        self.semfinal[semb.semname] = (semb.sem, semb.semval)
        tok = (semb.semname, semb.sem, semb.semval)

        def fn(e, o=out_ap, i=in_ap):
            if slow:
                return e.dma_start(out=o, in_=i, allow_slow_non_contiguous=True)
            return e.dma_start(out=o, in_=i)

        self.q[queue].append(("o", fn, semb.sem, 16))
        self._mark(tok, reads, writes)
        return tok

    def barrier(self):
        toks = []
        for e in ENGS:
            if self.ecnt[e] > 0:
                toks.append(("e_" + e, self.esem[e], self.ecnt[e]))
        for k, (h, v) in self.semfinal.items():
            toks.append((k, h, v))
        for e in ENGS:
            for (k, h, v) in toks:
                if self.seen[e].get(k, 0) >= v:
                    continue
                self.seen[e][k] = v
                self.q[e].append(("w", h, v))
        for b in self.dbufs:
            self.freesems.append((b.kind, b.semname, b.sem, b.semval))
            b.sem = None
        self.dbufs = []

    def emit(self, eng, e):
        for it in self.q[eng]:
            if it[0] == "w":
                e.wait_ge(it[1], it[2])
            elif it[0] == "o":
                ins = it[1](e)
                ins.then_inc(it[2], it[3])
            else:
                e.nop().then_inc(it[1], 1)


class SBAlloc:
    def __init__(self, nc):
        self.nc = nc
        self.base = (nc.sbuf_base + 63) // 64 * 64
        self.top = nc.sbuf_top
        self.off = self.base
        self.n = 0

    def alloc(self, shape, dt, name=None):
        per = 1
        for x in shape[1:]:
            per *= x
        per *= 2 if dt == BF16 else 4
        per = (per + 63) // 64 * 64
        assert self.off + per <= self.top, ("SBUF overflow", name, self.off, per, self.top)
        self.n += 1
        nm = "%s_%d" % (name or "t", self.n)
        t = self.nc.alloc_sbuf_tensor_at(nm, list(shape), dt, offset=self.off)
        self.off += per
        return Tn(t, nm)

    def mark(self):
        return self.off

    def release(self, m):
        self.off = m


def f_mm(groups):
    def fn(e):
        ins = None
        for (o, l, r, st, sp) in groups:
            ins = e.matmul(o, l, r, start=st, stop=sp)
        return ins
    return fn


def f_tr(items, ident):
    def fn(e):
        ins = None
        for (o, i) in items:
            ins = e.transpose(o, i, ident)
        return ins
    return fn


def f_tt(out, in0, in1, op):
    return lambda e: e.tensor_tensor(out=out, in0=in0, in1=in1, op=op)


def f_ts(out, in0, s1, s2, op0, op1=None):
    if op1 is None:
        return lambda e: e.tensor_scalar(out=out, in0=in0, scalar1=s1, scalar2=None, op0=op0)
    return lambda e: e.tensor_scalar(out=out, in0=in0, scalar1=s1, scalar2=s2, op0=op0, op1=op1)


def f_stt(out, in0, scalar, in1, op0, op1, accum=None):
    if accum is None:
        return lambda e: e.scalar_tensor_tensor(out=out, in0=in0, scalar=scalar, in1=in1, op0=op0, op1=op1)
    return lambda e: e.scalar_tensor_tensor(out=out, in0=in0, scalar=scalar, in1=in1, op0=op0, op1=op1,
                                            accum_out=accum)


def f_act(out, in_, func, bias=None, scale=None):
    kw = {}
    if bias is not None:
        kw["bias"] = bias
    if scale is not None:
        kw["scale"] = scale
    return lambda e: e.activation(out=out, in_=in_, func=func, **kw)


def f_red(out, in_):
    return lambda e: e.tensor_reduce(out=out, in_=in_, axis=AX.X, op=ALU.add)


def f_copy(out, in_):
    def fn(e):
        if hasattr(e, "tensor_copy"):
            return e.tensor_copy(out, in_)
        return e.activation(out=out, in_=in_, func=AF.Copy)
    return fn


def f_memset(ap, v):
    return lambda e: e.memset(ap, v)


def f_recip(out, in_):
    return lambda e: e.reciprocal(out, in_)


def build_nc(debug=False):
    nc = bass.Bass("TRN2", target_bir_lowering=False)
    es = ExitStack()

    def din(name, shape, dt=F32):
        return nc.dram_tensor(name, list(shape), dt, kind="ExternalInput").ap()

    def dout(name, shape, dt=F32):
        return nc.dram_tensor(name, list(shape), dt, kind="ExternalOutput").ap()

    def dscr(name, shape, dt=BF16):
        kind = "ExternalOutput" if debug else "Internal"
        return nc.dram_tensor(name, list(shape), dt, kind=kind).ap()

    xh = din("xh", [NHIST, D])
    xo = din("xo", [NOWN, D])
    xl = din("xl", [NHALO, D])
    c_ckv = din("c_ckv", [2 * PAST, 256])
    c_kpe = din("c_kpe", [2 * PAST, 64])
    c_bk = din("c_bk", [2 * LB, 1024])
    c_bv = din("c_bv", [2 * LB, 1024])
    rope_h = din("rope_h", [NHIST, 64])
    rope_o = din("rope_o", [NOWN, 64])
    rope_c = din("rope_c", [PAST, 64])
    flg = din("flg", [128, 16])
    bbias = din("bbias", [H, 640, 128])
    bmask = din("bmask", [640, 128])
    sbias = din("sbias", [H, 640, 32])
    ident_d = din("ident", [128, 128])
    norm_mix = din("norm_mix", [D])
    norm_ffn = din("norm_ffn", [D])
    w_in = din("w_in", [D, IN_COLS])
    g_cq = din("g_cq", [512])
    w_uq = din("w_uq", [512, 1536])
    g_ckv = din("g_ckv", [256])
    w_uk = din("w_uk", [256, 1024])
    w_uv = din("w_uv", [256, 1024])
    g_qa = din("g_qa", [192])
    g_ka = din("g_ka", [192])
    g_qb = din("g_qb", [128])
    g_kb = din("g_kb", [128])
    w_o = din("w_o", [D, D])
    w_up = din("w_up", [D, DFF])
    w_down = din("w_down", [DFF, D])

    y_o = dout("y_o", [NOWN, D])
    ckv_o = dout("ckv_o", [NOWN, 256])
    kpe_o = dout("kpe_o", [NOWN, 64])
    bk_o = dout("bk_o", [NOWN, 1024])
    bv_o = dout("bv_o", [NOWN, 1024])

    KTn = dscr("KTn", [H, 128, NKM])
    KTp = dscr("KTp", [H, 64, NKM])
    Vs = dscr("Vs", [NKM, 1024])
    KTn_c = dscr("KTn_c", [H, 128, 2 * PAST])
    KTp_c = dscr("KTp_c", [H, 64, 2 * PAST])
    V_c = dscr("V_c", [2 * PAST, 1024])
    QTn = dscr("QTn", [H, 128, NOWN])
    QTp = dscr("QTp", [H, 64, NOWN])
    QBT = dscr("QBT", [H, 128, NOWN])
    KBT = dscr("KBT", [H, 128, NKB])
    VB = dscr("VB", [NKB, 1024])
    KBT_c = dscr("KBT_c", [H, 128, 2 * LB])
    OT = dscr("OT", [2 * H, 128, NOWN])
    Hs = dscr("Hs", [NOWN, D], F32)

    WU = dscr("WU", [DFF // 256, 128, 16, 256])
    WD = dscr("WD", [D // 512, 128, 64, 512])
    S = Sched(nc, es)
    sb = SBAlloc(nc)
    psum = nc.alloc_psum_tensor("ps", [128, 8, 512], F32)
    PB = [Tn(None, "pb%d" % i) for i in range(8)]

    def pbank(i):
        return psum[:, i, :]

    def pbank_bf(i):
        return psum[:, i, :].bitcast(BF16)

    ident = sb.alloc([128, 128], BF16, "ident")
    ones = sb.alloc([128, 128], BF16, "ones")
    epsc = sb.alloc([128, 1], F32, "eps")
    zero_c = sb.alloc([128, 1], F32, "zero")
    flags = sb.alloc([128, 16], F32, "flags")
    nmix = sb.alloc([128, 16], F32, "nmix")
    nffn = sb.alloc([128, 16], F32, "nffn")
    gqa_n = sb.alloc([128, 1], F32, "gqa_n")
    gka_n = sb.alloc([128, 1], F32, "gka_n")
    gqb_p = sb.alloc([128, 1], F32, "gqb_p")
    gqa_pe = sb.alloc([128, 64], F32, "gqa_pe")
    gka_pe = sb.alloc([128, 64], F32, "gka_pe")
    gkb_b = sb.alloc([128, 128], F32, "gkb_b")
    gcq_b = sb.alloc([128, 512], F32, "gcq_b")
    gckv_b = sb.alloc([128, 256], F32, "gckv_b")

    S.dma("pool", ident.t[:], ident_d, ident.b, writes=[ident.b])
    S.op("pool", f_memset(ones.t[:], 1.0), writes=[ones.b])
    S.op("pool", f_memset(epsc.t[:], EPS), writes=[epsc.b])
    S.op("pool", f_memset(zero_c.t[:], 0.0), writes=[zero_c.b])
    S.dma("sp", flags.t[:], flg, flags.b, writes=[flags.b])
    S.dma("sp", nmix.t[:], norm_mix.rearrange("(k p) -> p k", p=128), nmix.b, writes=[nmix.b], slow=True)
    S.dma("sp", nffn.t[:], norm_ffn.rearrange("(k p) -> p k", p=128), nffn.b, writes=[nffn.b], slow=True)
    S.dma("sp", gqa_n.t[:], g_qa[0:128].rearrange("(p o) -> p o", o=1), gqa_n.b, writes=[gqa_n.b])
    S.dma("sp", gka_n.t[:], g_ka[0:128].rearrange("(p o) -> p o", o=1), gka_n.b, writes=[gka_n.b])
    S.dma("sp", gqb_p.t[:], g_qb.rearrange("(p o) -> p o", o=1), gqb_p.b, writes=[gqb_p.b])
    S.dma("sp", gqa_pe.t[:], g_qa[128:192].partition_broadcast(128), gqa_pe.b, writes=[gqa_pe.b])
    S.dma("sp", gka_pe.t[:], g_ka[128:192].partition_broadcast(128), gka_pe.b, writes=[gka_pe.b])
    S.dma("sp", gkb_b.t[:], g_kb.partition_broadcast(128), gkb_b.b, writes=[gkb_b.b])
    S.dma("sp", gcq_b.t[:], g_cq.partition_broadcast(128), gcq_b.b, writes=[gcq_b.b])
    S.dma("sp", gckv_b.t[:], g_ckv.partition_broadcast(128), gckv_b.b, writes=[gckv_b.b])
    bg = deque()
    cvb = Buf("cv")
    w_up_v = w_up.rearrange("(k p) c -> p k c", p=128)
    w_dn_v = w_down.rearrange("(k p) c -> p k c", p=128)
    if "p5" in PHASES:
        for g_ in range(DFF // 256):
            bg.append(lambda g_=g_: S.dma("pool", WU[g_], w_up_v[:, :, g_ * 256:(g_ + 1) * 256], cvb))
        for g_ in range(D // 512):
            for pc_ in range(8):
                bg.append(lambda g_=g_, pc_=pc_: S.dma("pool", WD[g_][:, pc_ * 8:(pc_ + 1) * 8, :],
                                                       w_dn_v[:, pc_ * 8:(pc_ + 1) * 8, g_ * 512:(g_ + 1) * 512], cvb))

    def bg_step(n=1):
        for _ in range(n):
            if bg:
                bg.popleft()()

    consts = [ident.b, ones.b, epsc.b, zero_c.b, flags.b, nmix.b, nffn.b, gqa_n.b, gka_n.b, gqb_p.b,
              gqa_pe.b, gka_pe.b, gkb_b.b, gcq_b.b, gckv_b.b]
    S.barrier()
    pm0 = sb.mark()

    def load_w(dst, src_ap, kchunks, step=4):
        v = src_ap.rearrange("(k p) c -> p k c", p=128)
        for k0 in range(0, kchunks, step):
            k1 = min(kchunks, k0 + step)
            S.dma("pool", dst.t[:, k0:k1, :], v[:, k0:k1, :], dst.b, writes=[])
        dst.b.w = (dst.b.semname, dst.b.sem, dst.b.semval)
        dst.b.r = {}

    def dma_split(queue, pieces, buf, reads=()):
        tok = None
        for n_, (o_, i_) in enumerate(pieces):
            tok = S.dma(queue, o_, i_, buf, reads=reads, writes=[buf] if n_ == 0 else [])
        buf.w = tok
        buf.r = {}

    def rstd_from_ss(ss_ap, ss_buf, n, dim, stat):
        S.op("act", f_act(ss_ap, ss_ap, AF.Ln, bias=epsc.t[:, 0:1], scale=1.0 / dim), reads=[ss_buf], writes=[ss_buf])
        S.op("act", f_act(ss_ap, ss_ap, AF.Exp, scale=-0.5), reads=[ss_buf], writes=[ss_buf])

    if "p1" in PHASES:
        cnt = {"s": 0}

        def alloc_sets(specs, nch=2):
            sets = [{nm: sb.alloc(shape, dt, "%s_%d" % (nm, p)) for (nm, shape, dt) in specs} for p in range(nch)]
            for p, z in enumerate(sets):
                z["bk"] = [4 * p + n for n in range(4)] if nch == 2 else [2 * p, 2 * p + 1, 2 * p, 2 * p + 1]
            return sets

        def front(x_rows, nm, Z, par):
            xb, st, xn, hnT = Z["xt"], Z["st"], Z["xn"], Z["hnT"]
            S.dma("sp", xb.t[:], x_rows, xb.b, writes=[xb.b])
            S.op("pool", f_memset(st.t[:, 0:1], 0.0), writes=[st.b])
            S.op("dve", f_stt(junk.t[:], xb.t[:], 1.0, xb.t[:], ALU.mult, ALU.mult, accum=st.t[:, 0:1]),
                 reads=[xb.b], writes=[junk.b, st.b])
            rstd_from_ss(st.t[:, 0:1], st.b, 1, D, st)
            S.op("act", f_act(xn.t[:], xb.t[:], AF.Copy, scale=st.t[:, 0:1]), reads=[xb.b, st.b], writes=[xn.b])
            for half in range(2):
                bk = Z["bk"][half]
                pv = pbank_bf(bk).rearrange("p (k t) -> p k t", k=8)
                S.op("pe", f_tr([(pv[:, k, :], xn.t[:, (half * 8 + k) * 128:(half * 8 + k + 1) * 128])
                                 for k in range(8)], ident.t[:]), reads=[xn.b, ident.b], writes=[PB[bk].b])
                S.op("dve", f_tt(hnT.t[:, half * 8:half * 8 + 8, :], pv,
                                 nm.t[:, half * 8:half * 8 + 8].unsqueeze(2).to_broadcast([128, 8, 128]), ALU.mult),
                     reads=[PB[bk].b, nm.b], writes=[hnT.b])

        def head_rstd(src3, srcbuf, nh, dh, ss_ap, stb, sqb, extra=None, div=None):
            S.op("pool", f_tt(sqb.t[:, 0:nh, 0:dh], src3, src3, ALU.mult), reads=[srcbuf], writes=[sqb.b])
            S.op("dve", f_red(ss_ap, sqb.t[:, 0:nh, 0:dh]), reads=[sqb.b], writes=[stb.b])
            if extra is not None:
                S.op("dve", f_ts(ss_ap, ss_ap, extra, None, ALU.add), reads=[stb.b], writes=[stb.b])
            rstd_from_ss(ss_ap, stb.b, nh, div or dh, stb)

        def rope(dst3, src3, srcbuf, dstbuf, csb, nh, rp):
            c = csb.t[:, 0:32].unsqueeze(1).to_broadcast([128, nh, 32])
            sn = csb.t[:, 32:64].unsqueeze(1).to_broadcast([128, nh, 32])
            x1 = src3[:, :, 0:32]
            x2 = src3[:, :, 32:64]
            t = [rp.t[:, i, 0:nh, :] for i in range(4)]
            S.op("pool", f_tt(t[0], x1, c, ALU.mult), reads=[srcbuf, csb.b], writes=[rp.b])
            S.op("pool", f_tt(t[1], x2, sn, ALU.mult), reads=[srcbuf, csb.b], writes=[rp.b])
            S.op("pool", f_tt(t[2], x1, sn, ALU.mult), reads=[srcbuf, csb.b], writes=[rp.b])
            S.op("pool", f_tt(t[3], x2, c, ALU.mult), reads=[srcbuf, csb.b], writes=[rp.b])
            S.op("pool", f_tt(dst3[:, :, 0:32], t[0], t[1], ALU.subtract), reads=[rp.b], writes=[dstbuf])
            S.op("pool", f_tt(dst3[:, :, 32:64], t[2], t[3], ALU.add), reads=[rp.b], writes=[dstbuf])

        win_m = sb.alloc([128, 16, 832], BF16, "win_m")
        wuk = sb.alloc([128, 2, 1024], BF16, "wuk")
        wuv = sb.alloc([128, 2, 1024], BF16, "wuv")
        load_w(win_m, w_in[:, 0:832], 16)
        load_w(wuk, w_uk, 2)
        load_w(wuv, w_uv, 2)
        junk = sb.alloc([128, D], BF16, "junk")
        stg_kn = sb.alloc([128, 8, 512], BF16, "skn")
        stg_kp = sb.alloc([64, 8, 512], BF16, "skp")
        stg_v = sb.alloc([128, 4, 1024], BF16, "sv")
        csl = [sb.alloc([128, 64], F32, "cs%d" % i) for i in range(8)]
        pm1 = sb.mark()
        KV_SPECS = [("xt", [128, D], F32), ("st", [128, 32], F32), ("xn", [128, D], BF16),
                    ("hnT", [128, 16, 128], BF16), ("ckvf", [128, 256], F32), ("kpef", [128, 64], F32),
                    ("ckvb", [128, 256], BF16), ("smT", [128, 6, 128], BF16), ("kf", [128, 8, 128], F32),
                    ("kn", [128, 8, 192], BF16), ("kpg", [128, 64], F32), ("kpr", [128, 64], F32)]
        ZM4 = alloc_sets(KV_SPECS + [("sqk", [128, 8, 128], F32), ("rpk", [128, 4, 1, 32], F32)], nch=4)

        def kv_from_latent(Z, par, sub):
            ckv_f, kpe_f, csb, st = Z["ckvf"], Z["kpef"], Z["csb"], Z["st"]
            ckvb, smT, kf, kn, kpg, kpr = Z["ckvb"], Z["smT"], Z["kf"], Z["kn"], Z["kpg"], Z["kpr"]
            S.op("dve", f_copy(ckvb.t[:], ckv_f.t[:]), reads=[ckv_f.b], writes=[ckvb.b])
            b0, b1, b2, b3 = Z["bk"]
            b0, b1 = b2, b3
            pv = pbank_bf(b2).rearrange("p (k t) -> p k t", k=8)
            S.op("pe", f_tr([(pv[:, 4 + k, :], ckvb.t[:, k * 128:(k + 1) * 128]) for k in range(2)], ident.t[:]),
                 reads=[ckvb.b, ident.b], writes=[PB[b2].b])
            S.op("act", f_copy(smT.t[:, 4:6, :], pv[:, 4:6, :]), reads=[PB[b2].b], writes=[smT.b])
            for g in range(2):
                S.op("pe", f_mm([(pbank(b2 + g), smT.t[:, 4 + k, :], wuk.t[:, k, g * 512:(g + 1) * 512], k == 0, k == 1)
                                 for k in range(2)]), reads=[smT.b, wuk.b], writes=[PB[b2 + g].b])
            for g in range(2):
                S.op("act", f_copy(kf.t[:, g * 4:(g + 1) * 4, :], pbank(b2 + g).rearrange("p (h d) -> p h d", h=4)),
                     reads=[PB[b2 + g].b], writes=[kf.b])
            for g in range(2):
                S.op("pe", f_mm([(pbank(b0 + g), smT.t[:, 4 + k, :], wuv.t[:, k, g * 512:(g + 1) * 512], k == 0, k == 1)
                                 for k in range(2)]), reads=[smT.b, wuv.b], writes=[PB[b0 + g].b])
            for g in range(2):
                S.op("act", f_copy(stg_v.t[:, sub, g * 512:(g + 1) * 512], pbank(b0 + g)), reads=[PB[b0 + g].b],
                     writes=[stg_v.b])
            S.op("pool", f_memset(st.t[:, 1:2], 0.0), writes=[st.b])
            S.op("dve", f_stt(kpg.t[:], kpe_f.t[:], 1.0, kpe_f.t[:], ALU.mult, ALU.mult, accum=st.t[:, 1:2]),
                 reads=[kpe_f.b], writes=[kpg.b, st.b])
            head_rstd(kf.t[:, :, :], kf.b, 8, 128, st.t[:, 8:16], st, Z["sqk"], extra=st.t[:, 1:2], div=192)
            S.op("dve", f_tt(kn.t[:, :, 0:128], kf.t[:, :, :], st.t[:, 8:16].unsqueeze(2).to_broadcast([128, 8, 128]),
                             ALU.mult), reads=[kf.b, st.b], writes=[kn.b])
            S.op("pool", f_tt(kpg.t[:], kpe_f.t[:], gka_pe.t[:], ALU.mult), reads=[kpe_f.b, gka_pe.b], writes=[kpg.b])
            rope(kpr.t[:].rearrange("p (h d) -> p h d", h=1), kpg.t[:].rearrange("p (h d) -> p h d", h=1),
                 kpg.b, kpr.b, csb, 1, Z["rpk"])
            S.op("dve", f_tt(kn.t[:, :, 128:192], kpr.t[:].unsqueeze(1).to_broadcast([128, 8, 64]),
                             st.t[:, 8:16].unsqueeze(2).to_broadcast([128, 8, 64]), ALU.mult),
                 reads=[kpr.b, st.b], writes=[kn.b])
            pn = pbank_bf(b2).rearrange("p (h t) -> p h t", h=8)
            pp = pbank_bf(b3).rearrange("p (h t) -> p h t", h=8)
            S.op("pe", f_tr([(pn[:, h, :], kn.t[:, h, 0:128]) for h in range(8)], ident.t[:]),
                 reads=[kn.b, ident.b], writes=[PB[b2].b])
            S.op("pe", f_tr([(pp[0:64, h, :], kn.t[:, h, 128:192]) for h in range(8)], ident.t[:]),
                 reads=[kn.b, ident.b], writes=[PB[b3].b])
            S.op("act", f_act(stg_kn.t[:, :, sub * 128:(sub + 1) * 128], pn, AF.Copy, scale=gka_n.t[:, 0:1]),
                 reads=[PB[b2].b, gka_n.b], writes=[stg_kn.b])
            S.op("dve", f_copy(stg_kp.t[:, :, sub * 128:(sub + 1) * 128], pp[0:64, :, :]), reads=[PB[b3].b],
                 writes=[stg_kp.b])

        def q_from_cq(Z, par, sub):
            st, cqf, cqn, smT, qf, qn, csb = Z["st"], Z["cqf"], Z["cqn"], Z["smT"], Z["qf"], Z["qn"], Z["csb"]
            b0, b1, b2, b3 = Z["bk"]
            S.op("pool", f_memset(st.t[:, 2:3], 0.0), writes=[st.b])
            S.op("dve", f_stt(junk.t[:, 0:512], cqf.t[:], 1.0, cqf.t[:], ALU.mult, ALU.mult, accum=st.t[:, 2:3]),
                 reads=[cqf.b], writes=[junk.b, st.b])
            rstd_from_ss(st.t[:, 2:3], st.b, 1, 512, st)
            S.op("dve", f_stt(cqn.t[:], cqf.t[:], st.t[:, 2:3], gcq_b.t[:], ALU.mult, ALU.mult),
                 reads=[cqf.b, st.b, gcq_b.b], writes=[cqn.b])
            pv = pbank_bf(b0).rearrange("p (k t) -> p k t", k=8)
            S.op("pe", f_tr([(pv[:, k, :], cqn.t[:, k * 128:(k + 1) * 128]) for k in range(4)], ident.t[:]),
                 reads=[cqn.b, ident.b], writes=[PB[b0].b])
            S.op("act", f_copy(smT.t[:, 0:4, :], pv[:, 0:4, :]), reads=[PB[b0].b], writes=[smT.b])
            qf2 = qf.t[:].rearrange("p h d -> p (h d)")
            for g in range(3):
                bq = (b1, b0, b1)[g]
                S.op("pe", f_mm([(pbank(bq), smT.t[:, k, :], wuq.t[:, k, g * 512:(g + 1) * 512], k == 0, k == 3)
                                 for k in range(4)]), reads=[smT.b, wuq.b], writes=[PB[bq].b])
                S.op("act", f_copy(qf2[:, g * 512:(g + 1) * 512], pbank(bq)), reads=[PB[bq].b], writes=[qf.b])
            head_rstd(qf.t[:, :, :], qf.b, 8, 192, st.t[:, 16:24], st, Z["sq"])
            S.op("dve", f_tt(qn.t[:, :, 0:128], qf.t[:, :, 0:128], st.t[:, 16:24].unsqueeze(2).to_broadcast([128, 8, 128]),
                             ALU.mult), reads=[qf.b, st.b], writes=[qn.b])
            S.op("dve", f_tt(qf.t[:, :, 128:192], qf.t[:, :, 128:192],
                             st.t[:, 16:24].unsqueeze(2).to_broadcast([128, 8, 64]), ALU.mult),
                 reads=[qf.b, st.b], writes=[qf.b])
            S.op("pool", f_tt(qf.t[:, :, 128:192], qf.t[:, :, 128:192],
                              gqa_pe.t[:].unsqueeze(1).to_broadcast([128, 8, 64]), ALU.mult),
                 reads=[qf.b, gqa_pe.b], writes=[qf.b])
            rope(qn.t[:, :, 128:192], qf.t[:, :, 128:192], qf.b, qn.b, csb, 8, Z["rp"])
            pn = pbank_bf(b0).rearrange("p (h t) -> p h t", h=8)
            pp = pbank_bf(b1).rearrange("p (h t) -> p h t", h=8)
            S.op("pe", f_tr([(pn[:, h, :], qn.t[:, h, 0:128]) for h in range(8)], ident.t[:]),
                 reads=[qn.b, ident.b], writes=[PB[b0].b])
            S.op("pe", f_tr([(pp[0:64, h, :], qn.t[:, h, 128:192]) for h in range(8)], ident.t[:]),
                 reads=[qn.b, ident.b], writes=[PB[b1].b])
            S.op("act", f_act(stg_qn.t[:, :, sub * 128:(sub + 1) * 128], pn, AF.Copy, scale=gqa_n.t[:, 0:1]),
                 reads=[PB[b0].b, gqa_n.b], writes=[stg_qn.b])
            S.op("dve", f_copy(stg_qp.t[:, :, sub * 128:(sub + 1) * 128], pp[0:64, :, :]), reads=[PB[b1].b],
                 writes=[stg_qp.b])

        def mla_tile(xsrc, row0, nsub, ropesrc, rope0, kslot0, own_row0=None, from_cache=None, kdst=None, sets=None):
            KTn_d, KTp_d, V_d = kdst
            recs = []
            recs2 = []
            for sub in range(nsub):
                S.start_rec()
                nch = len(sets)
                par = cnt["s"] % nch
                Z = dict(sets[par])
                Z["csb"] = csl[cnt["s"] % 8]
                cnt["s"] += 1
                b2, b3 = Z["bk"][2], Z["bk"][3]
                r0 = row0 + sub * 128
                csb, cf, kp, hnT = Z["csb"], Z["ckvf"], Z["kpef"], Z["hnT"]
                if from_cache is not None:
                    S.dma("sp", csb.t[:], ropesrc[rope0 + sub * 128:rope0 + (sub + 1) * 128, :], csb.b, writes=[csb.b])
                if from_cache is None:
                    front(xsrc[r0:r0 + 128, :], nmix, Z, par)
                    S.dma("sp", csb.t[:], ropesrc[rope0 + sub * 128:rope0 + (sub + 1) * 128, :], csb.b, writes=[csb.b])
                    if own_row0 is not None:
                        S.op("pe", f_mm([(pbank(b2), hnT.t[:, k, :], win_m.t[:, k, 0:512], k == 0, k == 15)
                                         for k in range(16)]), reads=[hnT.b, win_m.b], writes=[PB[b2].b])
                    S.op("pe", f_mm([(pbank(b3)[:, 0:320], hnT.t[:, k, :], win_m.t[:, k, 512:832], k == 0, k == 15)
                                     for k in range(16)]), reads=[hnT.b, win_m.b], writes=[PB[b3].b])
                    if own_row0 is not None:
                        S.op("act", f_copy(Z["cqf"].t[:], pbank(b2)), reads=[PB[b2].b], writes=[Z["cqf"].b])
                    S.op("act", f_copy(cf.t[:], pbank(b3)[:, 0:256]), reads=[PB[b3].b], writes=[cf.b])
                    S.op("act", f_copy(kp.t[:], pbank(b3)[:, 256:320]), reads=[PB[b3].b], writes=[kp.b])
                    st = Z["st"]
                    S.op("pool", f_memset(st.t[:, 3:4], 0.0), writes=[st.b])
                    S.op("dve", f_stt(junk.t[:, 0:256], cf.t[:], 1.0, cf.t[:], ALU.mult, ALU.mult,
                                      accum=st.t[:, 3:4]), reads=[cf.b], writes=[junk.b, st.b])
                    rstd_from_ss(st.t[:, 3:4], st.b, 1, 256, st)
                    S.op("dve", f_stt(cf.t[:], cf.t[:], st.t[:, 3:4], gckv_b.t[:], ALU.mult, ALU.mult),
                         reads=[cf.b, st.b, gckv_b.b], writes=[cf.b])
                    if own_row0 is not None:
                        o0 = own_row0 + sub * 128
                        S.dma("pool", ckv_o[o0:o0 + 128, :], cf.t[:], cf.b, reads=[cf.b])
                        S.dma("pool", kpe_o[o0:o0 + 128, :], kp.t[:], kp.b, reads=[kp.b])
                        recs.append(S.stop_rec())
                        S.start_rec()
                        q_from_cq(Z, par, sub)
                        recs2.append(S.stop_rec())
                        S.start_rec()
                else:
                    ca, ka = from_cache
                    S.dma("sp", cf.t[:], ca[r0:r0 + 128, :], cf.b, writes=[cf.b])
                    S.dma("sp", kp.t[:], ka[r0:r0 + 128, :], kp.b, writes=[kp.b])
                kv_from_latent(Z, par, sub)
                (recs2 if (own_row0 is not None and from_cache is None) else recs).append(S.stop_rec())
            npl = 2 if (own_row0 is not None and from_cache is None) else 0
            for i0 in range(0, nsub, nch):
                S.interleave(recs[i0:i0 + nch])
                if npl:
                    S.interleave(recs2[npl * i0:npl * (i0 + nch)])
            n = nsub * 128
            S.dma("act", KTn_d[:, :, kslot0:kslot0 + n].rearrange("h d t -> d h t"), stg_kn.t[:, :, 0:n], stg_kn.b,
                  reads=[stg_kn.b])
            S.dma("act", KTp_d[:, :, kslot0:kslot0 + n].rearrange("h d t -> d h t"), stg_kp.t[:, :, 0:n], stg_kp.b,
                  reads=[stg_kp.b])
            S.dma("act", V_d[kslot0:kslot0 + n, :].rearrange("(s p) c -> p s c", p=128), stg_v.t[:, 0:nsub, :],
                  stg_v.b, reads=[stg_v.b])
            if own_row0 is not None:
                S.dma("act", QTn[:, :, own_row0:own_row0 + n].rearrange("h d t -> d h t"), stg_qn.t[:, :, 0:n],
                      stg_qn.b, reads=[stg_qn.b])
                S.dma("act", QTp[:, :, own_row0:own_row0 + n].rearrange("h d t -> d h t"), stg_qp.t[:, :, 0:n],
                      stg_qp.b, reads=[stg_qp.b])

        kd = (KTn, KTp, Vs)
        for t in range(1 if SMALL else NHIST // 512):
            mla_tile(xh, t * 512, 4, rope_h, t * 512, t * 512, kdst=kd, sets=ZM4)
        if "p4" in PHASES:
            kdc = (KTn_c, KTp_c, V_c)
            for sbi in range(1 if SMALL else 2):
                for t in range(1 if SMALL else 2):
                    mla_tile(None, sbi * PAST + t * 512, 4, rope_c, t * 512, sbi * PAST + t * 512,
                             from_cache=(c_ckv, c_kpe), kdst=kdc, sets=ZM4)
        S.barrier()
        sb.release(pm1)
        wuq = sb.alloc([128, 4, 1536], BF16, "wuq")
        load_w(wuq, w_uq, 4)
        ZM = alloc_sets(KV_SPECS + [("cqn", [128, 512], BF16), ("cqf", [128, 512], F32), ("qf", [128, 8, 192], F32),
                                    ("sq", [128, 8, 192], F32), ("qn", [128, 8, 192], BF16),
                                    ("rp", [128, 4, 8, 32], F32), ("sqk", [128, 8, 128], F32),
                                    ("rpk", [128, 4, 1, 32], F32)], nch=2)
        stg_qn = sb.alloc([128, 8, 512], BF16, "sqn")
        stg_qp = sb.alloc([64, 8, 512], BF16, "sqp")
        for t in range(1 if SMALL else 4):
            mla_tile(xo, t * 512, 4, rope_o, t * 512, NHIST + t * 512, own_row0=t * 512, kdst=kd, sets=ZM)
        mla_tile(xo, 2048, 1, rope_o, 2048, NHIST + 2048, own_row0=2048, kdst=kd, sets=ZM)
        S.barrier()
        sb.release(pm0)

        win_b = sb.alloc([128, 16, 3072], BF16, "win_b")
        load_w(win_b, w_in[:, 832:3904], 16, step=2)
        junk = sb.alloc([128, D], BF16, "junkb")
        ZB = alloc_sets([("xt", [128, D], F32), ("st", [128, 32], F32), ("xn", [128, D], BF16),
                         ("hnT", [128, 16, 128], BF16), ("bf", [128, 8, 128], F32), ("sq", [128, 8, 128], F32),
                         ("bn", [128, 8, 128], BF16), ("kbo", [128, 8, 128], F32), ("vbo", [128, 1024], F32)])
        stg_qb = sb.alloc([128, 8, 512], BF16, "sqb")
        stg_kb = sb.alloc([128, 8, 512], BF16, "skb")
        stg_vb = sb.alloc([128, 4, 1024], BF16, "svb")

        def band_tile(xsrc, row0, nsub, kslot0, own_row0=None):
            recs = []
            for sub in range(nsub):
                S.start_rec()
                par = cnt["s"] % 2
                cnt["s"] += 1
                Z = ZB[par]
                b0, b1, b2, b3 = Z["bk"]
                hnT, st, bf, bn, ko, vo = Z["hnT"], Z["st"], Z["bf"], Z["bn"], Z["kbo"], Z["vbo"]
                r0 = row0 + sub * 128
                front(xsrc[r0:r0 + 128, :], nmix, Z, par)
                if own_row0 is not None:
                    for g in range(2):
                        S.op("pe", f_mm([(pbank(b2 + g), hnT.t[:, k, :], win_b.t[:, k, g * 512:(g + 1) * 512], k == 0, k == 15)
                                         for k in range(16)]), reads=[hnT.b, win_b.b], writes=[PB[b2 + g].b])
                    for g in range(2):
                        S.op("act", f_copy(bf.t[:, g * 4:(g + 1) * 4, :], pbank(b2 + g).rearrange("p (h d) -> p h d", h=4)),
                             reads=[PB[b2 + g].b], writes=[bf.b])
                    head_rstd(bf.t[:, :, :], bf.b, 8, 128, st.t[:, 8:16], st, Z["sq"])
                    S.op("dve", f_tt(bn.t[:, :, :], bf.t[:, :, :], st.t[:, 8:16].unsqueeze(2).to_broadcast([128, 8, 128]),
                                     ALU.mult), reads=[bf.b, st.b], writes=[bn.b])
                    pn = pbank_bf(b0).rearrange("p (h t) -> p h t", h=8)
                    S.op("pe", f_tr([(pn[:, h, :], bn.t[:, h, :]) for h in range(8)], ident.t[:]),
                         reads=[bn.b, ident.b], writes=[PB[b0].b])
                    S.op("act", f_act(stg_qb.t[:, :, sub * 128:(sub + 1) * 128], pn, AF.Copy, scale=gqb_p.t[:, 0:1]),
                         reads=[PB[b0].b, gqb_p.b], writes=[stg_qb.b])
                for g in range(2):
                    S.op("pe", f_mm([(pbank(b2 + g), hnT.t[:, k, :], win_b.t[:, k, 1024 + g * 512:1024 + (g + 1) * 512],
                                      k == 0, k == 15) for k in range(16)]), reads=[hnT.b, win_b.b], writes=[PB[b2 + g].b])
                for g in range(2):
                    S.op("act", f_copy(bf.t[:, g * 4:(g + 1) * 4, :], pbank(b2 + g).rearrange("p (h d) -> p h d", h=4)),
                         reads=[PB[b2 + g].b], writes=[bf.b])
                for g in range(2):
                    S.op("pe", f_mm([(pbank(b0 + g), hnT.t[:, k, :], win_b.t[:, k, 2048 + g * 512:2048 + (g + 1) * 512],
                                      k == 0, k == 15) for k in range(16)]), reads=[hnT.b, win_b.b], writes=[PB[b0 + g].b])
                head_rstd(bf.t[:, :, :], bf.b, 8, 128, st.t[:, 16:24], st, Z["sq"])
                S.op("dve", f_tt(bf.t[:, :, :], bf.t[:, :, :], st.t[:, 16:24].unsqueeze(2).to_broadcast([128, 8, 128]),
                                 ALU.mult), reads=[bf.b, st.b], writes=[bf.b])
                S.op("pool", f_tt(ko.t[:, :, :], bf.t[:, :, :], gkb_b.t[:].unsqueeze(1).to_broadcast([128, 8, 128]),
                                  ALU.mult), reads=[bf.b, gkb_b.b], writes=[ko.b])
                S.op("dve", f_copy(bn.t[:, :, :], ko.t[:, :, :]), reads=[ko.b], writes=[bn.b])
                for g in range(2):
                    S.op("act", f_copy(vo.t[:, g * 512:(g + 1) * 512], pbank(b0 + g)), reads=[PB[b0 + g].b], writes=[vo.b])
                pn = pbank_bf(b2).rearrange("p (h t) -> p h t", h=8)
                S.op("pe", f_tr([(pn[:, h, :], bn.t[:, h, :]) for h in range(8)], ident.t[:]),
                     reads=[bn.b, ident.b], writes=[PB[b2].b])
                S.op("act", f_copy(stg_kb.t[:, :, sub * 128:(sub + 1) * 128], pn), reads=[PB[b2].b],
                     writes=[stg_kb.b])
                S.op("pool", f_copy(stg_vb.t[:, sub, :], vo.t[:]), reads=[vo.b], writes=[stg_vb.b])
                if own_row0 is not None:
                    o0 = own_row0 + sub * 128
                    S.dma("pool", bk_o[o0:o0 + 128, :], ko.t[:].rearrange("p h d -> p (h d)"), ko.b, reads=[ko.b])
                    S.dma("pool", bv_o[o0:o0 + 128, :], vo.t[:], vo.b, reads=[vo.b])
                recs.append(S.stop_rec())
            for i0 in range(0, nsub, 2):
                S.interleave(recs[i0:i0 + 2])
            n = nsub * 128
            S.dma("act", KBT[:, :, kslot0:kslot0 + n].rearrange("h d t -> d h t"), stg_kb.t[:, :, 0:n], stg_kb.b,
                  reads=[stg_kb.b])
            S.dma("act", VB[kslot0:kslot0 + n, :].rearrange("(s p) c -> p s c", p=128), stg_vb.t[:, 0:nsub, :],
                  stg_vb.b, reads=[stg_vb.b])
            if own_row0 is not None:
                S.dma("act", QBT[:, :, own_row0:own_row0 + n].rearrange("h d t -> d h t"), stg_qb.t[:, :, 0:n],
                      stg_qb.b, reads=[stg_qb.b])

        band_tile(xl, 0, 4, KB_HA)
        if not SMALL:
            band_tile(xl, 512, 4, KB_HB)
        for t in range(1 if SMALL else 2):
            band_tile(xo, t * 512, 4, KB_A + t * 512, own_row0=t * 512)
        for t in range(0 if SMALL else 2):
            band_tile(xo, 1024 + t * 512, 4, KB_B + t * 512, own_row0=1024 + t * 512)
        band_tile(xo, 2048, 1, KB_S, own_row0=2048)
        S.barrier()
        sb.release(pm0)

    att = {"i": 0, "u": 0, "pend": deque()}
    SKEW = 3

    def att_flush():
        while att["pend"]:
            att["pend"].popleft()()

    def attn_unit(qparts, ktiles, ncols, ot_dst, pT, obufs, rec):
        nt = len(ktiles)
        u = att["u"]
        att["u"] += 1
        bo, bl = (4, 5) if u % 2 == 0 else (6, 7)
        obuf = obufs[u % len(obufs)]
        rc = rec["rc"][u % len(rec["rc"])]
        for ti, kt in enumerate(ktiles):
            nk, c0 = kt["nk"], kt["c0"]
            w = ncols - c0
            bi = att["i"] % 4
            p = pT[att["i"] % len(pT)]
            tmp = rec["tmp"][att["i"] % len(rec["tmp"])] if "tmp" in rec else None
            att["i"] += 1
            sbk = PB[bi]
            groups = []
            np_ = len(qparts)
            multi = kt.get("multi")
            if multi is None:
                for pi, ((qa, qb_), (ka, kb_)) in enumerate(zip(qparts, kt["kparts"])):
                    groups.append((pbank(bi)[0:nk, 0:w], ka, qa[:, c0:ncols], pi == 0, pi == np_ - 1))
                S.op("pe", f_mm(groups), reads=[q[1] for q in qparts] + [k[1] for k in kt["kparts"]], writes=[sbk.b])
            else:
                kbufs = []
                for m_, (kps, _) in enumerate(multi):
                    for pi, ((qa, qb_), (ka, kb_)) in enumerate(zip(qparts, kps)):
                        groups.append((pbank(bi)[0:nk, m_ * ncols:(m_ + 1) * ncols], ka, qa[:, 0:ncols], pi == 0, pi == np_ - 1))
                        kbufs.append(kb_)
                S.op("pe", f_mm(groups), reads=[q[1] for q in qparts] + kbufs, writes=[sbk.b])
                w = len(multi) * ncols
            src = pbank(bi)[0:nk, 0:w]
            if kt.get("btab") is not None:
                ba, bb = kt["btab"]
                S.op("dve", f_stt(tmp.t[0:nk, 0:w], src, kt["scale"], ba, ALU.mult, ALU.add),
                     reads=[sbk.b, bb], writes=[tmp.b])
                S.op("act", f_act(p.t[0:nk, 0:w], tmp.t[0:nk, 0:w], AF.Exp, bias=kt["bias"]),
                     reads=[tmp.b, flags.b], writes=[p.b])
            else:
                S.op("act", f_act(p.t[0:nk, 0:w], src, AF.Exp, bias=kt["bias"], scale=kt["scale"]),
                     reads=[sbk.b, flags.b], writes=[p.b])
            if kt.get("zero") is not None:
                r0, r1, cc0, cc1 = kt["zero"]
                S.op("pool", f_memset(p.t[r0:r1, cc0:cc1], 0.0), writes=[p.b])

            def stage_c(kt=kt, p=p, ti=ti, nk=nk, c0=c0, w=w, multi=multi):
                if multi is None:
                    va, vb_ = kt["v"]
                    S.op("pe", f_mm([(pbank(bo)[:, c0:ncols], va, p.t[0:nk, 0:w], ti == 0, ti == nt - 1),
                                     (pbank(bl)[:, c0:ncols], ones.t[0:nk, :], p.t[0:nk, 0:w], ti == 0, ti == nt - 1)]),
                         reads=[p.b, vb_, ones.b], writes=[PB[bo].b, PB[bl].b])
                else:
                    mms, vbufs = [], []
                    nm_ = len(multi)
                    for m_, (_, (va, vb_)) in enumerate(multi):
                        pm = p.t[0:nk, m_ * ncols:(m_ + 1) * ncols]
                        first = (ti == 0 and m_ == 0)
                        last = (ti == nt - 1 and m_ == nm_ - 1)
                        mms.append((pbank(bo)[:, 0:ncols], va, pm, first, last))
                        mms.append((pbank(bl)[:, 0:ncols], ones.t[0:nk, :], pm, first, last))
                        vbufs.append(vb_)
                    S.op("pe", f_mm(mms), reads=[p.b, ones.b] + vbufs, writes=[PB[bo].b, PB[bl].b])
                if ti == nt - 1:
                    S.op("dve", f_recip(rc.t[:, 0:ncols], pbank(bl)[:, 0:ncols]), reads=[PB[bl].b], writes=[rc.b])
                    S.op("dve", f_tt(obuf.t[:, 0:ncols], pbank(bo)[:, 0:ncols], rc.t[:, 0:ncols], ALU.mult),
                         reads=[PB[bo].b, rc.b], writes=[obuf.b])
                    S.dma("act", ot_dst, obuf.t[:, 0:ncols], obuf.b, reads=[obuf.b])

            att["pend"].append(stage_c)
            while len(att["pend"]) > SKEW:
                att["pend"].popleft()()

    if "p2" in PHASES:
        NK2 = NHIST + 2 * SEG
        ktn = [sb.alloc([128, NK2], BF16, "ktn%d" % i) for i in range(2)]
        ktp = [sb.alloc([128, NK2], BF16, "ktp%d" % i) for i in range(2)]
        vv = [sb.alloc([128, NK2 // 128, 128], BF16, "vv%d" % i) for i in range(2)]
        qtn = [sb.alloc([128, 2 * SEG], BF16, "qtn%d" % i) for i in range(2)]
        qtp = [sb.alloc([128, 2 * SEG], BF16, "qtp%d" % i) for i in range(2)]
        for i_ in range(2):
            S.op("pool", f_memset(ktp[i_].t[64:128, :], 0.0), writes=[ktp[i_].b])
            S.op("pool", f_memset(qtp[i_].t[64:128, :], 0.0), writes=[qtp[i_].b])
        pT = [sb.alloc([128, 512], BF16, "pT%d" % i) for i in range(6)]
        ob = [sb.alloc([128, 512], BF16, "ob%d" % i) for i in range(2)]
        rec = {"rc": [sb.alloc([128, 512], F32, "rc%d" % i) for i in range(2)]}

        def load_head(h):
            i = h % 2
            for c in range(0, NK2, 2304):
                pass
            S.dma("sp", ktn[i].t[:], KTn[h, :, 0:NK2], ktn[i].b, writes=[ktn[i].b])
            S.dma("sp", ktp[i].t[0:64, :], KTp[h, :, 0:NK2], ktp[i].b, writes=[ktp[i].b])
            vsrc = Vs[0:NK2, h * 128:(h + 1) * 128].rearrange("(t p) d -> p t d", p=128)
            dma_split("sp", [(vv[i].t[:, t0:t0 + 8, :], vsrc[:, t0:t0 + 8, :]) for t0 in range(0, NK2 // 128, 8)], vv[i].b)
            S.dma("sp", qtn[i].t[:], QTn[h, :, 0:2 * SEG], qtn[i].b, writes=[qtn[i].b])
            S.dma("sp", qtp[i].t[0:64, :], QTp[h, :, 0:2 * SEG], qtp[i].b, writes=[qtp[i].b])

        DO3 = "p3" in PHASES
        NKB2 = KB_S
        if DO3:
            kbt = [sb.alloc([128, NKB2], BF16, "kbt%d" % i) for i in range(2)]
            vbt = [sb.alloc([128, NKB2 // 128, 128], BF16, "vbt%d" % i) for i in range(2)]
            qbt = [sb.alloc([128, 2 * SEG], BF16, "qbt%d" % i) for i in range(2)]
            btab = [sb.alloc([128, 5, 128], F32, "btab%d" % i) for i in range(2)]
            bmk = sb.alloc([128, 5, 128], F32, "bmk")
            pTb = [sb.alloc([128, 512], BF16, "pTb%d" % i) for i in range(6)]
            obb = [sb.alloc([128, 128], BF16, "obb%d" % i) for i in range(2)]
            recb = {"rc": [sb.alloc([128, 128], F32, "rcb%d" % i) for i in range(2)],
                    "tmp": [sb.alloc([128, 512], F32, "tmpb%d" % i) for i in range(4)]}
            S.dma("sp", bmk.t[:], bmask.rearrange("(t p) q -> p t q", p=128), bmk.b, writes=[bmk.b])

        def load_bhead(h):
            i = h % 2
            S.dma("sp", kbt[i].t[:], KBT[h, :, 0:NKB2], kbt[i].b, writes=[kbt[i].b])
            vsrc = VB[0:NKB2, h * 128:(h + 1) * 128].rearrange("(t p) d -> p t d", p=128)
            dma_split("sp", [(vbt[i].t[:, t0:t0 + 8, :], vsrc[:, t0:t0 + 8, :]) for t0 in range(0, NKB2 // 128, 8)], vbt[i].b)
            S.dma("sp", qbt[i].t[:], QBT[h, :, 0:2 * SEG], qbt[i].b, writes=[qbt[i].b])
            S.dma("sp", btab[i].t[:], bbias[h].rearrange("(t p) q -> p t q", p=128), btab[i].b, writes=[btab[i].b])
            S.op("pool", f_tt(btab[i].t[:], btab[i].t[:], bmk.t[:], ALU.add), reads=[btab[i].b, bmk.b], writes=[btab[i].b])

        def band_units(h, lo, hi):
            i = h % 2
            for un in range(lo, hi):
                seg, pr = un // 8, un % 8
                kbase = KB_HA if seg == 0 else KB_HB
                q0 = seg * SEG + pr * 128
                qparts = [(qbt[i].t[:, q0:q0 + 128], qbt[i].b)]
                tiles = []

                def one(t):
                    k0 = kbase + pr * 128 + t * 128
                    is_halo = (pr * 128 + t * 128) < 512
                    fcol = 10 if (seg == 0 and is_halo) else 11
                    return dict(kparts=[(kbt[i].t[:, k0:k0 + 128], kbt[i].b)],
                                v=(vbt[i].t[:, k0 // 128, :], vbt[i].b), nk=128, c0=0,
                                bias=flags.t[:, fcol:fcol + 1], scale=SCALE_B,
                                btab=(btab[i].t[:, t, :], btab[i].b)), fcol

                singles = [one(t) for t in range(5)]
                if len(set(fc for (_, fc) in singles[0:4])) == 1:
                    fc = singles[0][1]
                    tiles.append(dict(multi=[(d_["kparts"], d_["v"]) for (d_, _) in singles[0:4]], nk=128, c0=0,
                                      bias=flags.t[:, fc:fc + 1], scale=SCALE_B,
                                      btab=(btab[i].t[:, 0:4, :].rearrange("p t q -> p (t q)"), btab[i].b)))
                    tiles.append(singles[4][0])
                else:
                    tiles = [d_ for (d_, _) in singles]
                attn_unit(qparts, tiles, 128, OT[H + h, :, q0:q0 + 128], pTb, obb, recb)

        load_head(0)
        if DO3:
            load_bhead(0)
        ui = 0
        H2 = 1 if SMALL else H
        for h in range(H2):
            att_flush()
            if h + 1 < H2:
                load_head(h + 1)
            i = h % 2
            for qb_ in range(4):
                seg = qb_ // 2
                half = qb_ % 2
                q0 = qb_ * 512
                qparts = [(qtn[i].t[:, q0:q0 + 512], qtn[i].b), (qtp[i].t[:, q0:q0 + 512], qtp[i].b)]
                tiles = []
                nslots = 3 if seg == 0 else 7
                for s_ in range(nslots):
                    fcol = s_ if seg == 0 else 3 + s_
                    for t in range(8):
                        k0 = s_ * SEG + t * 128
                        tiles.append(dict(kparts=[(ktn[i].t[:, k0:k0 + 128], ktn[i].b), (ktp[i].t[:, k0:k0 + 128], ktp[i].b)],
                                          v=(vv[i].t[:, k0 // 128, :], vv[i].b), nk=128, c0=0,
                                          bias=flags.t[:, fcol:fcol + 1], scale=SCALE_A))
                own0 = NHIST + seg * SEG
                for t in range(4 * half + 4):
                    k0 = own0 + t * 128
                    rel = t - 4 * half
                    c0 = max(0, rel) * 128
                    z = (64, 128, 0, 64) if rel >= 0 else None
                    tiles.append(dict(kparts=[(ktn[i].t[:, k0:k0 + 128], ktn[i].b), (ktp[i].t[:, k0:k0 + 128], ktp[i].b)],
                                      v=(vv[i].t[:, k0 // 128, :], vv[i].b), nk=128, c0=c0,
                                      bias=flags.t[:, 11:12], scale=SCALE_A, zero=z))
                attn_unit(qparts, tiles, 512, OT[h, :, q0:q0 + 512], pT, ob, rec)
                bg_step(2)
                ui += 1
        if DO3:
            for h in range(H2):
                att_flush()
                if h + 1 < H2:
                    load_bhead(h + 1)
                band_units(h, 0, 16)
        att_flush()
        S.barrier()
        sb.release(pm0)

    if "p4" in PHASES:
        cb = [sb.alloc([128, 1024], BF16, "cb%d" % i) for i in range(2)]
        stc = [sb.alloc([128, 8, 128], BF16, "stc%d" % i) for i in range(2)]
        for r in range(8):
            c_ = cb[r % 2]
            S.dma("pool", c_.t[:], c_bk[r * 128:(r + 1) * 128, :], c_.b, writes=[c_.b])
            pn = pbank_bf(r % 2).rearrange("p (h t) -> p h t", h=8)
            S.op("pe", f_tr([(pn[:, h, :], c_.t[:, h * 128:(h + 1) * 128]) for h in range(8)], ident.t[:]),
                 reads=[c_.b, ident.b], writes=[PB[r % 2].b])
            S.op("dve", f_copy(stc[r % 2].t[:], pn), reads=[PB[r % 2].b], writes=[stc[r % 2].b])
            S.dma("sp", KBT_c[:, :, r * 128:(r + 1) * 128].rearrange("h d t -> d h t"), stc[r % 2].t[:], stc[r % 2].b,
                  reads=[stc[r % 2].b])
        S.barrier()
        sb.release(pm0)
        wo_pref = sb.alloc([128, 16, D], BF16, "wo")
        load_w(wo_pref, w_o, 16, step=2)
        pm_p4 = sb.mark()
        NKS = PAST + 32
        ktn = [sb.alloc([128, NKS], BF16, "sktn%d" % i) for i in range(2)]
        ktp = [sb.alloc([64, NKS], BF16, "sktp%d" % i) for i in range(2)]
        vv = [sb.alloc([128, 9, 128], BF16, "svv%d" % i) for i in range(2)]
        qtn = [sb.alloc([128, 32], BF16, "sqtn%d" % i) for i in range(2)]
        qtp = [sb.alloc([64, 32], BF16, "sqtp%d" % i) for i in range(2)]
        kbt = [sb.alloc([128, LB + 32], BF16, "skbt%d" % i) for i in range(2)]
        vbt = [sb.alloc([128, 4, 128], BF16, "svbt%d" % i) for i in range(2)]
        vbn = [sb.alloc([32, 128], BF16, "svbn%d" % i) for i in range(2)]
        qbt = [sb.alloc([128, 32], BF16, "sqbt%d" % i) for i in range(2)]
        btab = [sb.alloc([128, 5, 32], F32, "sbtab%d" % i) for i in range(2)]
        pT = [sb.alloc([128, 32], BF16, "spT%d" % i) for i in range(6)]
        ob = [sb.alloc([128, 32], BF16, "sob%d" % i) for i in range(2)]
        rec = {"rc": [sb.alloc([128, 32], F32, "src%d" % i) for i in range(2)],
               "tmp": [sb.alloc([128, 32], F32, "stmp%d" % i) for i in range(4)]}
        units = [(sbi, h) for sbi in range(1 if SMALL else 2) for h in range(1 if SMALL else H)]

        def p4_load(n):
            sbi, h = units[n]
            i = n % 2
            qrow = 2048 + sbi * 32
            knew = NHIST + 2048 + sbi * 32
            kbnew = KB_S + sbi * 32
            dma_split("sp", [(ktn[i].t[:, 0:PAST], KTn_c[h, :, sbi * PAST:(sbi + 1) * PAST]),
                             (ktn[i].t[:, PAST:NKS], KTn[h, :, knew:knew + 32])], ktn[i].b)
            dma_split("sp", [(ktp[i].t[:, 0:PAST], KTp_c[h, :, sbi * PAST:(sbi + 1) * PAST]),
                             (ktp[i].t[:, PAST:NKS], KTp[h, :, knew:knew + 32])], ktp[i].b)
            dma_split("sp", [(vv[i].t[:, 0:8, :], V_c[sbi * PAST:(sbi + 1) * PAST, h * 128:(h + 1) * 128].rearrange(
                "(t p) d -> p t d", p=128)), (vv[i].t[0:32, 8, :], Vs[knew:knew + 32, h * 128:(h + 1) * 128])], vv[i].b)
            S.dma("sp", qtn[i].t[:], QTn[h, :, qrow:qrow + 32], qtn[i].b, writes=[qtn[i].b])
            S.dma("sp", qtp[i].t[:], QTp[h, :, qrow:qrow + 32], qtp[i].b, writes=[qtp[i].b])
            dma_split("sp", [(kbt[i].t[:, 0:LB], KBT_c[h, :, sbi * LB:(sbi + 1) * LB]),
                             (kbt[i].t[:, LB:LB + 32], KBT[h, :, kbnew:kbnew + 32])], kbt[i].b)
            S.dma("pool", vbt[i].t[:], c_bv[sbi * LB:(sbi + 1) * LB, h * 128:(h + 1) * 128].rearrange(
                "(t p) d -> p t d", p=128), vbt[i].b, writes=[vbt[i].b])
            S.dma("sp", vbn[i].t[:], VB[kbnew:kbnew + 32, h * 128:(h + 1) * 128], vbn[i].b, writes=[vbn[i].b])
            S.dma("sp", qbt[i].t[:], QBT[h, :, qrow:qrow + 32], qbt[i].b, writes=[qbt[i].b])
            S.dma("sp", btab[i].t[:], sbias[h].rearrange("(t p) q -> p t q", p=128), btab[i].b, writes=[btab[i].b])

        def p4_compute(n):
            sbi, h = units[n]
            i = n % 2
            qrow = 2048 + sbi * 32
            qparts = [(qtn[i].t[:, :], qtn[i].b), (qtp[i].t[:, :], qtp[i].b)]
            tiles = []
            for t in range(9):
                nk = 128 if t < 8 else 32
                k0 = t * 128
                tiles.append(dict(kparts=[(ktn[i].t[:, k0:k0 + nk], ktn[i].b), (ktp[i].t[:, k0:k0 + nk], ktp[i].b)],
                                  v=(vv[i].t[0:nk, t, :], vv[i].b), nk=nk, c0=0, bias=flags.t[0:nk, 11:12],
                                  scale=SCALE_A))
            attn_unit(qparts, tiles, 32, OT[h, :, qrow:qrow + 32], pT, ob, rec)
            qparts = [(qbt[i].t[:, :], qbt[i].b)]
            tiles = []
            for t in range(5):
                nk = 128 if t < 4 else 32
                k0 = t * 128
                vsrc = (vbt[i].t[:, t, :], vbt[i].b) if t < 4 else (vbn[i].t[:, :], vbn[i].b)
                tiles.append(dict(kparts=[(kbt[i].t[:, k0:k0 + nk], kbt[i].b)], v=vsrc,
                                  nk=nk, c0=0, bias=flags.t[0:nk, 11:12], scale=SCALE_B,
                                  btab=(btab[i].t[0:nk, t, :], btab[i].b)))
            attn_unit(qparts, tiles, 32, OT[H + h, :, qrow:qrow + 32], pT, ob, rec)

        p4_load(0)
        for n in range(len(units)):
            att_flush()
            if n + 1 < len(units):
                p4_load(n + 1)
            p4_compute(n)
        att_flush()
        S.barrier()
        sb.release(pm_p4)

    if "p5" in PHASES:
        bg_step(1000)
        if "p4" in PHASES:
            wo = wo_pref
        else:
            wo = sb.alloc([128, 16, D], BF16, "wo")
            load_w(wo, w_o, 16, step=2)
        otb = [sb.alloc([128, 16, 512], BF16, "otb%d" % i) for i in range(2)]
        xr = [sb.alloc([128, D], F32, "xr%d" % i) for i in range(2)]
        hb = [sb.alloc([128, D], F32, "hb%d" % i) for i in range(2)]
        t5a = [(1024, 4), (1536, 4), (2048, 1)] if SMALL else [(0, 4), (512, 4), (1024, 4), (1536, 4), (2048, 1)]
        sc = 0
        for ti_, (tok0, nsub) in enumerate(t5a):
            ob_ = otb[ti_ % 2]
            n = nsub * 128
            osrc = OT[:, :, tok0:tok0 + n].rearrange("h d t -> d h t")
            dma_split("sp", [(ob_.t[:, h0:h0 + 8, 0:n], osrc[:, h0:h0 + 8, :]) for h0 in (0, 8)], ob_.b)
            for s_ in range(nsub):
                i = sc % 2
                sc += 1
                r0 = tok0 + s_ * 128
                S.dma("sp", xr[i].t[:], xo[r0:r0 + 128, :], xr[i].b, writes=[xr[i].b])
                for g in range(4):
                    bk = (sc * 4 + g) % 8
                    S.op("pe", f_mm([(pbank(bk), ob_.t[:, k, s_ * 128:(s_ + 1) * 128], wo.t[:, k, g * 512:(g + 1) * 512],
                                      k == 0, k == 15) for k in range(16)]), reads=[ob_.b, wo.b], writes=[PB[bk].b])
                    S.op("dve", f_tt(hb[i].t[:, g * 512:(g + 1) * 512], pbank(bk), xr[i].t[:, g * 512:(g + 1) * 512], ALU.add),
                         reads=[PB[bk].b, xr[i].b], writes=[hb[i].b])
                S.dma("act", Hs[r0:r0 + 128, :], hb[i].t[:], hb[i].b, reads=[hb[i].b])
        S.barrier()
        sb.release(pm0)

        TMAX = 640
        uT = sb.alloc([128, 64, TMAX], BF16, "uT")
        hn2T = sb.alloc([128, 16, TMAX], BF16, "hn2T")
        hx = [sb.alloc([128, D], F32, "hx%d" % i) for i in range(2)]
        junk = sb.alloc([128, D], BF16, "junk5")
        st = sb.alloc([128, 8], F32, "st5")
        xn = sb.alloc([128, D], BF16, "xn5")
        wup = [sb.alloc([128, 16, 256], BF16, "wup%d" % i) for i in range(3)]
        wdn = [sb.alloc([128, 8, 512], BF16, "wdn%d" % i) for i in range(3)]
        rl = [sb.alloc([128, 512], F32, "rl%d" % i) for i in range(2)]
        hres = [sb.alloc([128, 512], F32, "hres%d" % i) for i in range(2)]
        yo = [sb.alloc([128, 512], F32, "yo%d" % i) for i in range(2)]
        tiles5 = [(1024, 4), (1536, 5)] if SMALL else [(0, 4), (512, 4), (1024, 4), (1536, 5)]
        cx = 0
        wi = 0
        di = 0
        ri = 0
        for (tok0, nsub) in tiles5:
            T = nsub * 128
            for s_ in range(nsub):
                xb = hx[cx % 2]
                cx += 1
                r0 = tok0 + s_ * 128
                S.dma("sp", xb.t[:], Hs[r0:r0 + 128, :], xb.b, writes=[xb.b])
                S.op("pool", f_memset(st.t[:, 0:1], 0.0), writes=[st.b])
                S.op("dve", f_stt(junk.t[:], xb.t[:], 1.0, xb.t[:], ALU.mult, ALU.mult, accum=st.t[:, 0:1]),
                     reads=[xb.b], writes=[junk.b, st.b])
                rstd_from_ss(st.t[:, 0:1], st.b, 1, D, st)
                S.op("act", f_act(xn.t[:], xb.t[:], AF.Copy, scale=st.t[:, 0:1]), reads=[xb.b, st.b], writes=[xn.b])
                for half in range(2):
                    pv = pbank_bf(half).rearrange("p (k t) -> p k t", k=8)
                    S.op("pe", f_tr([(pv[:, k, :], xn.t[:, (half * 8 + k) * 128:(half * 8 + k + 1) * 128])
                                     for k in range(8)], ident.t[:]), reads=[xn.b, ident.b], writes=[PB[half].b])
                    S.op("dve", f_tt(hn2T.t[:, half * 8:half * 8 + 8, s_ * 128:(s_ + 1) * 128], pv,
                                     nffn.t[:, half * 8:half * 8 + 8].unsqueeze(2).to_broadcast([128, 8, 128]), ALU.mult),
                         reads=[PB[half].b, nffn.b], writes=[hn2T.b])
            for gcol in range(DFF // 256):
                wb = wup[wi % 3]
                wi += 1
                S.dma("sp", wb.t[:], WU[gcol], wb.b, writes=[wb.b])
                for m2 in range(2):
                    m = gcol * 2 + m2
                    pieces = [(0, min(T, 512))] + ([(512, T)] if T > 512 else [])
                    for (a, b_) in pieces:
                        bk = 2 + (ri % 4)
                        r_ = rl[ri % 2]
                        ri += 1
                        S.op("pe", f_mm([(pbank(bk)[:, 0:b_ - a], wb.t[:, k, m2 * 128:(m2 + 1) * 128], hn2T.t[:, k, a:b_],
                                          k == 0, k == 15) for k in range(16)]), reads=[wb.b, hn2T.b], writes=[PB[bk].b])
                        S.op("act", f_act(r_.t[:, 0:b_ - a], pbank(bk)[:, 0:b_ - a], AF.Relu), reads=[PB[bk].b],
                             writes=[r_.b])
                        S.op("pool" if (ri % 2) else "dve", f_tt(uT.t[:, m, a:b_], r_.t[:, 0:b_ - a], r_.t[:, 0:b_ - a],
                                                                  ALU.mult), reads=[r_.b], writes=[uT.b])
            for gcol in range(D // 512):
                if nsub <= 4:
                    banks = [2, 3, 4, 5] if gcol % 2 == 0 else [6, 7, 0, 1]
                else:
                    banks = [2, 3, 4, 5, 6]
                for pc in range(8):
                    wb = wdn[di % 3]
                    di += 1
                    S.dma("sp", wb.t[:], WD[gcol][:, pc * 8:(pc + 1) * 8, :], wb.b, writes=[wb.b])
                    for s_ in range(nsub):
                        bk = banks[s_]
                        S.op("pe", f_mm([(pbank(bk), uT.t[:, pc * 8 + k, s_ * 128:(s_ + 1) * 128], wb.t[:, k, :],
                                          pc == 0 and k == 0, pc == 7 and k == 7) for k in range(8)]),
                             reads=[wb.b, uT.b], writes=[PB[bk].b])
                for s_ in range(nsub):
                    bk = banks[s_]
                    r0 = tok0 + s_ * 128
                    hr = hres[(gcol * 5 + s_) % 2]
                    y_ = yo[(gcol * 5 + s_) % 2]
                    S.dma("act", hr.t[:], Hs[r0:r0 + 128, gcol * 512:(gcol + 1) * 512], hr.b, writes=[hr.b])
                    S.op("dve", f_tt(y_.t[:], pbank(bk), hr.t[:], ALU.add), reads=[PB[bk].b, hr.b], writes=[y_.b])
                    S.dma("act", y_o[r0:r0 + 128, gcol * 512:(gcol + 1) * 512], y_.t[:], y_.b, reads=[y_.b])
        S.barrier()
        sb.release(pm0)

    S.barrier()
    with nc.Block() as block:
        @block.tensor
        def _(e):
            S.emit("pe", e)

        @block.scalar
        def _(e):
            S.emit("act", e)

        @block.vector
        def _(e):
            S.emit("dve", e)

        @block.gpsimd
        def _(e):
            S.emit("pool", e)

        @block.sync
        def _(e):
            S.emit("sp", e)
    return nc


def _rope_table(pos):
    inv = 1.0 / (10000.0 ** (np.arange(0, 64, 2, dtype=np.float32) / 64.0))
    ang = pos.astype(np.float32)[:, None] * inv[None, :].astype(np.float32)
    return np.concatenate([np.cos(ang), np.sin(ang)], axis=1).astype(np.float32)


def _host_inputs(inp):
    f32 = np.float32
    xp = np.asarray(inp["x_prompt"], f32)
    xs = np.asarray(inp["x_sample"], f32)
    rb = np.asarray(inp["rel_bias"], f32)[0]
    kl = np.arange(640)[:, None]
    ql = np.arange(128)[None, :]
    idx = np.clip(ql - kl + 512, -128, 128) + 128
    bbias = np.ascontiguousarray(rb[:, idx])
    qc = ql // 64
    kc = kl // 64
    allowed = (kc <= qc + 8) & (kc >= qc)
    bmask = np.where(allowed, 0.0, NEGM).astype(f32)
    ks = np.arange(640)[:, None]
    ts = np.arange(32)[None, :]
    rel_s = np.where(ks < 512, 512 + ts - ks, ts - (ks - 512))
    sidx = np.clip(rel_s, -128, 128) + 128
    sbias = np.ascontiguousarray(rb[:, sidx])
    shared = {
        "bbias": bbias, "bmask": bmask, "sbias": sbias, "ident": np.eye(128, dtype=f32),
        "rope_h": _rope_table(np.arange(NHIST)), "rope_c": _rope_table(np.arange(PAST)),
    }
    for k in ("norm_mix", "norm_ffn", "w_in", "g_cq", "w_uq", "g_ckv", "w_uk", "w_uv", "g_qa", "g_ka", "g_qb", "g_kb",
              "w_o", "w_up", "w_down"):
        shared[k] = np.ascontiguousarray(np.asarray(inp[k], f32)[0])
    cck = np.asarray(inp["cache_mla_ckv"], f32)[0]
    ckp = np.asarray(inp["cache_mla_kpe"], f32)[0]
    cbk = np.asarray(inp["cache_band_k"], f32)[0].reshape(16, LB, 1024)
    cbv = np.asarray(inp["cache_band_v"], f32)[0].reshape(16, LB, 1024)
    maps = []
    for c in range(8):
        b, j = c // 4, c % 4
        a0, b0 = j * SEG, (7 - j) * SEG
        m = dict(shared)
        m["xh"] = np.ascontiguousarray(xp[b, 0:NHIST])
        xo = np.zeros((NOWN, D), f32)
        xo[0:SEG] = xp[b, a0:a0 + SEG]
        xo[SEG:2 * SEG] = xp[b, b0:b0 + SEG]
        xo[2048:2080] = xs[2 * c]
        xo[2080:2112] = xs[2 * c + 1]
        m["xo"] = xo
        xl = np.zeros((NHALO, D), f32)
        if a0 >= 512:
            xl[0:512] = xp[b, a0 - 512:a0]
        xl[512:1024] = xp[b, b0 - 512:b0]
        m["xl"] = xl
        m["c_ckv"] = np.ascontiguousarray(cck[2 * c:2 * c + 2].reshape(2 * PAST, 256))
        m["c_kpe"] = np.ascontiguousarray(ckp[2 * c:2 * c + 2].reshape(2 * PAST, 64))
        m["c_bk"] = np.ascontiguousarray(cbk[2 * c:2 * c + 2].reshape(2 * LB, 1024))
        m["c_bv"] = np.ascontiguousarray(cbv[2 * c:2 * c + 2].reshape(2 * LB, 1024))
        pos_o = np.concatenate([np.arange(a0, a0 + SEG), np.arange(b0, b0 + SEG), PAST + np.arange(32),
                                PAST + np.arange(32), np.zeros(64, np.int64)])
        m["rope_o"] = _rope_table(pos_o)
        fl = np.zeros((128, 16), f32)
        for s_ in range(3):
            fl[:, s_] = 0.0 if s_ < j else NEGM
        for s_ in range(7):
            fl[:, 3 + s_] = 0.0 if s_ < 7 - j else NEGM
        fl[:, 10] = NEGM if j == 0 else 0.0
        m["flg"] = fl
        maps.append(m)
    return maps


_NC_CACHE = {}


def kernel(**inputs):
    maps = _host_inputs(inputs)
    if "nc" not in _NC_CACHE:
        _NC_CACHE["nc"] = build_nc()
    nc = _NC_CACHE["nc"]
    res = run_bass_kernel_spmd(nc, maps, core_ids=list(range(8)))
    R = res.results
    f32 = np.float32
    S_ = 8192
    y_p = np.zeros((2, S_, D), f32)
    ckv_p = np.zeros((1, 2, S_, 256), f32)
    kpe_p = np.zeros((1, 2, S_, 64), f32)
    bk_p = np.zeros((1, 2, 512, 8, 128), f32)
    bv_p = np.zeros((1, 2, 512, 8, 128), f32)
    y_s = np.zeros((16, 32, D), f32)
    ckv_s = np.zeros((1, 16, 32, 256), f32)
    kpe_s = np.zeros((1, 16, 32, 64), f32)
    bk_s = np.zeros((1, 16, 32, 8, 128), f32)
    bv_s = np.zeros((1, 16, 32, 8, 128), f32)
    for c in range(8):
        b, j = c // 4, c % 4
        a0, b0 = j * SEG, (7 - j) * SEG
        r = R[c]
        for (dst, src, w) in ((y_p, "y_o", None), (ckv_p[0], "ckv_o", None), (kpe_p[0], "kpe_o", None)):
            dst[b, a0:a0 + SEG] = r[src][0:SEG]
            dst[b, b0:b0 + SEG] = r[src][SEG:2 * SEG]
        if j == 0:
            bk_p[0, b] = r["bk_o"][1536:2048].reshape(512, 8, 128)
            bv_p[0, b] = r["bv_o"][1536:2048].reshape(512, 8, 128)
        for sbi in range(2):
            rows = slice(2048 + 32 * sbi, 2048 + 32 * sbi + 32)
            y_s[2 * c + sbi] = r["y_o"][rows]
            ckv_s[0, 2 * c + sbi] = r["ckv_o"][rows]
            kpe_s[0, 2 * c + sbi] = r["kpe_o"][rows]
            bk_s[0, 2 * c + sbi] = r["bk_o"][rows].reshape(32, 8, 128)
            bv_s[0, 2 * c + sbi] = r["bv_o"][rows].reshape(32, 8, 128)
    return (y_p, y_s, ckv_p, kpe_p, bk_p, bv_p, ckv_s, kpe_s, bk_s, bv_s)
```
